# Optimizing a Trainium2 kernel written in Bass

```python
import math
import jax, jax.numpy as jnp
from jax import lax
import numpy as np

D_MODEL = 1024
BATCH = 2
SEQ = 8192
DEPTH = 2

HEAD_DIM = 64
GRID_W = 64
NQ_A = 4
NKV_A = 2
NQ_B = 4
NKV_B = 2
NH_C = 4
NQ_D = 4
NKV_D = 2
DIFF_DIM = HEAD_DIM // 2
BRANCH_W = 4 * HEAD_DIM
D_MIX = 4 * BRANCH_W
D_IN = (2 * NQ_A + 2 * NKV_A) * HEAD_DIM + (2 * NQ_B + 2 * NKV_B) * HEAD_DIM + 4 * NH_C * HEAD_DIM + (2 * NQ_D + 2 * NKV_D) * 2 * DIFF_DIM
Q_BLOCK = 128
WINDOW = 128
NA_KH = 8
NA_KW = 16
ROPE_THETA = 10000.0
EPS = 1e-6
NEG_INF = -1e30

kernel_name = "hybrid_parallel_heads_encoder"


def _in_sizes():
    hd, dd2 = HEAD_DIM, 2 * DIFF_DIM
    return [NQ_A * hd, NKV_A * hd, NKV_A * hd, NQ_A * hd,
            NQ_B * hd, NKV_B * hd, NKV_B * hd, NQ_B * hd,
            NH_C * hd, NH_C * hd, NH_C * hd, NH_C * hd,
            NQ_D * dd2, NKV_D * dd2, NKV_D * dd2, NQ_D * dd2]


def _rmsnorm(x, g):
    x32 = x.astype(jnp.float32)
    y = x32 * lax.rsqrt(jnp.mean(x32 * x32, axis=-1, keepdims=True) + EPS)
    return (y * g.astype(jnp.float32)).astype(x.dtype)


def _rope(x, pos):
    d = x.shape[-1]
    half = d // 2
    inv = jnp.power(ROPE_THETA, -jnp.arange(0, d, 2, dtype=jnp.float32) / d)
    ang = pos.astype(jnp.float32)[:, None] * inv[None, :]
    shp = (1, x.shape[1]) + (1,) * (x.ndim - 3) + (half,)
    cos = jnp.cos(ang).reshape(shp)
    sin = jnp.sin(ang).reshape(shp)
    x32 = x.astype(jnp.float32)
    x1, x2 = x32[..., :half], x32[..., half:]
    return jnp.concatenate([x1 * cos - x2 * sin, x2 * cos + x1 * sin], axis=-1).astype(x.dtype)


def _axial_rope(x, row, col):
    half = x.shape[-1] // 2
    return jnp.concatenate([_rope(x[..., :half], row), _rope(x[..., half:], col)], axis=-1)


def _dense_blocked(q, k, v, scale):
    b, s, hk, g, d = q.shape
    nb = s // Q_BLOCK
    qb = q.reshape(b, nb, Q_BLOCK, hk, g, d).transpose(1, 0, 2, 3, 4, 5)

    def body(qi):
        sc = jnp.einsum('bqkgd,bskd->bkgqs', qi, k).astype(jnp.float32) * scale
        p = jax.nn.softmax(sc, axis=-1).astype(v.dtype)
        return jnp.einsum('bkgqs,bskd->bqkgd', p, v)

    o = lax.map(body, qb)
    return o.transpose(1, 0, 2, 3, 4, 5).reshape(b, s, hk * g * v.shape[-1])


def _window_sink(q, k, v, sink, scale):
    b, s, hk, g, d = q.shape
    nb = s // Q_BLOCK
    pad = ((0, 0), (Q_BLOCK, Q_BLOCK), (0, 0), (0, 0))
    kp = jnp.pad(k, pad).reshape(b, nb + 2, Q_BLOCK, hk, d)
    vp = jnp.pad(v, pad).reshape(b, nb + 2, Q_BLOCK, hk, v.shape[-1])
    kw = jnp.concatenate([kp[:, :-2], kp[:, 1:-1], kp[:, 2:]], axis=2)
    vw = jnp.concatenate([vp[:, :-2], vp[:, 1:-1], vp[:, 2:]], axis=2)
    qb = q.reshape(b, nb, Q_BLOCK, hk, g, d)
    sc = jnp.einsum('bnqkgd,bnskd->bnkgqs', qb, kw).astype(jnp.float32) * scale
    blk = jnp.arange(nb)[:, None] * Q_BLOCK
    qpos = blk + jnp.arange(Q_BLOCK)[None, :]
    kpos = blk - Q_BLOCK + jnp.arange(3 * Q_BLOCK)[None, :]
    valid = ((jnp.abs(qpos[:, :, None] - kpos[:, None, :]) <= WINDOW)
             & (kpos[:, None, :] >= 0) & (kpos[:, None, :] < s))
    sc = jnp.where(valid[None, :, None, None], sc, NEG_INF)
    sk = jnp.broadcast_to(sink.astype(jnp.float32).reshape(1, 1, hk, g, 1, 1), sc.shape[:-1] + (1,))
    p = jax.nn.softmax(jnp.concatenate([sc, sk], axis=-1), axis=-1)[..., :-1].astype(v.dtype)
    o = jnp.einsum('bnkgqs,bnskd->bnqkgd', p, vw)
    return o.reshape(b, s, hk * g * v.shape[-1])


def _neighbourhood(q, k, v, rpb, scale):
    b, s, h, d = q.shape
    rows = s // GRID_W
    kh = min(NA_KH, rows)
    kw = NA_KW
    ncb = GRID_W // kw
    kc = 2 * kw
    nl = kh * kc
    r = jnp.arange(rows)
    rs = jnp.clip(r - kh // 2, 0, rows - kh)
    key_rows = rs[:, None] + jnp.arange(kh)[None, :]
    cb = jnp.arange(ncb)
    cbase = jnp.clip(cb * kw - kw // 2, 0, GRID_W - kc)
    key_cols = cbase[:, None] + jnp.arange(kc)[None, :]
    idx = (key_rows[:, None, :, None] * GRID_W + key_cols[None, :, None, :]).reshape(-1)
    kg = jnp.take(k, idx, axis=1).reshape(b, rows, ncb, nl, h, d)
    vg = jnp.take(v, idx, axis=1).reshape(b, rows, ncb, nl, h, v.shape[-1])
    qg = q.reshape(b, rows, ncb, kw, h, d)
    sc = jnp.einsum('brcqhd,brclhd->brchql', qg, kg).astype(jnp.float32) * scale
    qcol = cb[:, None] * kw + jnp.arange(kw)[None, :]
    cs = jnp.clip(qcol - kw // 2, 0, GRID_W - kw)
    kcol = key_cols[:, None, :]
    col_ok = (kcol >= cs[:, :, None]) & (kcol < cs[:, :, None] + kw)
    mask = jnp.broadcast_to(col_ok[:, :, None, :], (ncb, kw, kh, kc)).reshape(ncb, kw, nl)
    dr = key_rows - r[:, None] + NA_KH - 1
    dc = jnp.clip(kcol - qcol[:, :, None], -(kw - 1), kw - 1) + kw - 1
    bias = rpb.astype(jnp.float32)[:, dr[:, None, None, :, None], dc[None, :, :, None, :]]
    bias = bias.reshape(h, rows, ncb, kw, nl).transpose(1, 2, 0, 3, 4)
    sc = jnp.where(mask[None, None, :, None], sc + bias[None], NEG_INF)
    p = jax.nn.softmax(sc, axis=-1).astype(v.dtype)
    o = jnp.einsum('brchql,brclhd->brcqhd', p, vg)
    return o.reshape(b, s, h * v.shape[-1])


def _diff_blocked(q, k, v, lam, scale):
    b, s, hk, g, _, dd = q.shape
    nb = s // Q_BLOCK
    qb = q.reshape(b, nb, Q_BLOCK, hk, g, 2, dd).transpose(1, 0, 2, 3, 4, 5, 6)

    def body(qi):
        sc = jnp.einsum('bqkgcd,bskcd->cbkgqs', qi, k).astype(jnp.float32) * scale
        p = jax.nn.softmax(sc, axis=-1)
        w = (p[0] - lam * p[1]).astype(v.dtype)
        return jnp.einsum('bkgqs,bskd->bqkgd', w, v)

    o = lax.map(body, qb)
    return o.transpose(1, 0, 2, 3, 4, 5).reshape(b, s, hk, g, v.shape[-1])


def setup_inputs(seed: int = 0) -> dict:
    key = jax.random.key(seed)
    ks = jax.random.split(key, 14)
    f32 = jnp.float32
    nrm = lambda k, shp: jax.random.normal(k, shp, dtype=f32)
    return {
        "x": nrm(ks[0], (BATCH, SEQ, D_MODEL)),
        "norm_g": 1.0 + 0.02 * nrm(ks[1], (DEPTH, D_MODEL)),
        "w_in": nrm(ks[2], (DEPTH, D_MODEL, D_IN)) * D_MODEL ** -0.5,
        "w_out": nrm(ks[3], (DEPTH, D_MIX, D_MODEL)) * D_MIX ** -0.5,
        "qn_a": 1.0 + 0.02 * nrm(ks[4], (DEPTH, HEAD_DIM)),
        "kn_a": 1.0 + 0.02 * nrm(ks[5], (DEPTH, HEAD_DIM)),
        "sink_b": 0.5 * nrm(ks[6], (DEPTH, NQ_B)),
        "rpb_c": 0.1 * nrm(ks[7], (DEPTH, NH_C, 2 * NA_KH - 1, 2 * NA_KW - 1)),
        "lam_q1": 0.1 * nrm(ks[8], (DEPTH, DIFF_DIM)),
        "lam_k1": 0.1 * nrm(ks[9], (DEPTH, DIFF_DIM)),
        "lam_q2": 0.1 * nrm(ks[10], (DEPTH, DIFF_DIM)),
        "lam_k2": 0.1 * nrm(ks[11], (DEPTH, DIFF_DIM)),
        "subln_d": 1.0 + 0.02 * nrm(ks[12], (DEPTH, 2 * DIFF_DIM)),
        "final_g": 1.0 + 0.02 * nrm(ks[13], (D_MODEL,)),
    }


def reference(x, norm_g, w_in, w_out, qn_a, kn_a, sink_b, rpb_c, lam_q1, lam_k1, lam_q2, lam_k2, subln_d, final_g):
    b, s, _ = x.shape
    t = jnp.arange(s)
    row = t // GRID_W
    col = t % GRID_W
    offs = [int(o) for o in np.cumsum(_in_sizes())[:-1]]
    sc_hd = HEAD_DIM ** -0.5
    for l in range(DEPTH):
        h = _rmsnorm(x, norm_g[l])
        z = jnp.einsum('bsd,de->bse', h, w_in[l])
        (q_a, k_a, v_a, g_a, q_b, k_b, v_b, g_b,
         q_c, k_c, v_c, g_c, q_d, k_d, v_d, g_d) = jnp.split(z, offs, axis=-1)

        q_a = _rmsnorm(q_a.reshape(b, s, NKV_A, NQ_A // NKV_A, HEAD_DIM), qn_a[l])
        k_a = _rmsnorm(k_a.reshape(b, s, NKV_A, HEAD_DIM), kn_a[l])
        q_a = _axial_rope(q_a, row, col)
        k_a = _axial_rope(k_a, row, col)
        o_a = _dense_blocked(q_a, k_a, v_a.reshape(b, s, NKV_A, HEAD_DIM), sc_hd)

        q_b = _rope(q_b.reshape(b, s, NKV_B, NQ_B // NKV_B, HEAD_DIM), t)
        k_b = _rope(k_b.reshape(b, s, NKV_B, HEAD_DIM), t)
        o_b = _window_sink(q_b, k_b, v_b.reshape(b, s, NKV_B, HEAD_DIM), sink_b[l], sc_hd)

        o_c = _neighbourhood(q_c.reshape(b, s, NH_C, HEAD_DIM), k_c.reshape(b, s, NH_C, HEAD_DIM),
                             v_c.reshape(b, s, NH_C, HEAD_DIM), rpb_c[l], sc_hd)

        lam_init = 0.8 - 0.6 * math.exp(-0.3 * l)
        lam = (jnp.exp(jnp.sum(lam_q1[l].astype(jnp.float32) * lam_k1[l].astype(jnp.float32)))
               - jnp.exp(jnp.sum(lam_q2[l].astype(jnp.float32) * lam_k2[l].astype(jnp.float32)))
               + lam_init)
        q_d = _rope(q_d.reshape(b, s, NKV_D, NQ_D // NKV_D, 2, DIFF_DIM), t)
        k_d = _rope(k_d.reshape(b, s, NKV_D, 2, DIFF_DIM), t)
        o_d = _diff_blocked(q_d, k_d, v_d.reshape(b, s, NKV_D, 2 * DIFF_DIM), lam, DIFF_DIM ** -0.5)
        o_d = (_rmsnorm(o_d, subln_d[l]) * (1.0 - lam_init)).reshape(b, s, NQ_D * 2 * DIFF_DIM)

        mix = jnp.concatenate([o_a * jax.nn.silu(g_a), o_b * jax.nn.silu(g_b),
                               o_c * jax.nn.silu(g_c), o_d * jax.nn.silu(g_d)], axis=-1)
        x = x + jnp.einsum('bsm,md->bsd', mix, w_out[l])
    return _rmsnorm(x, final_g)
```

```python
import os
import numpy as np
from contextlib import ExitStack
import concourse.bass as bass
import concourse.mybir as mybir
from concourse.bass_utils import run_bass_kernel_spmd

F32 = mybir.dt.float32
GDT = mybir.dt.bfloat16
BF16 = mybir.dt.bfloat16
AF = mybir.ActivationFunctionType
ALU = mybir.AluOpType
AX = mybir.AxisListType

D_MODEL = 1024
SEQ = 8192
BATCH = 2
DEPTH = 2
D_IN = 3328
NCORES = 8
TOWN = 2048
NT_OWN = 16
NT_ALL = 64
NT_LOC = 20
EPS = 1e-6
MASKV = -30000.0

V_QN, V_KN, V_SUB, V_SINK, V_LQ1, V_LK1, V_LQ2, V_LK2, V_LC = 0, 64, 128, 192, 196, 228, 260, 292, 324
V_N = 328


class Buf:
    __slots__ = ("w", "r", "dsem", "dcnt", "name")
    ALL = []

    def __init__(self, name=""):
        self.w = None
        self.r = {}
        self.dsem = None
        self.dcnt = 0
        self.name = name
        Buf.ALL.append(self)


class Q:
    def __init__(self, name, eng, sem):
        self.name, self.eng, self.sem = name, eng, sem
        self.cnt = 0
        self.seen = {}


class Ctx:
    def __init__(self, nc, st):
        self.nc, self.st = nc, st
        self.q = {}
        for name, eng in (("pe", nc.tensor), ("act", nc.scalar), ("dve", nc.vector),
                          ("pool", nc.gpsimd), ("sp", nc.sync)):
            self.q[name] = Q(name, eng, st.enter_context(nc.semaphore("q_" + name)))
        self.dbufs = []
        self.nsem = 5

    def _wait(self, q, deps):
        for key, (s, v) in deps.items():
            if q.seen.get(key, 0) >= v:
                continue
            q.eng.wait_ge(s, v)
            q.seen[key] = v

    def _deps(self, q, reads, writes):
        deps = {}

        def add(tok, raw):
            if tok is None:
                return
            s, v = tok
            if s is q.sem and (q.name == "pe" or not raw):
                return
            k = id(s)
            if k not in deps or deps[k][1] < v:
                deps[k] = (s, v)
        for b in reads:
            add(b.w, True)
        for b in writes:
            add(b.w, True)
            for tok in b.r.values():
                add(tok, True)
        return deps

    def op(self, qn, fn, reads=(), writes=()):
        q = self.q[qn]
        self._wait(q, self._deps(q, reads, writes))
        ins = fn(q.eng)
        q.cnt += 1
        ins.then_inc(q.sem, 1)
        tok = (q.sem, q.cnt)
        for b in reads:
            b.r[id(q.sem)] = tok
        for b in writes:
            b.w = tok
            b.r = {}
        return ins

    def dma(self, out, in_, sb, reads=(), writes=(), qn="sp"):
        q = self.q[qn]
        if sb.dsem is None:
            sb.dsem = self.st.enter_context(self.nc.semaphore("d_%d" % self.nsem))
            self.nsem += 1
            self.dbufs.append(sb)
        self._wait(q, self._deps(q, reads, writes))
        ins = q.eng.dma_start(out=out, in_=in_)
        sb.dcnt += 16
        ins.then_inc(sb.dsem, 16)
        tok = (sb.dsem, sb.dcnt)
        for b in reads:
            b.r[id(sb.dsem)] = tok
        for b in writes:
            b.w = tok
            b.r = {}

    def collective(self, kind, op, groups, in_ap, out_ap, in_B, out_B):
        q = self.q["pool"]
        self._wait(q, self._deps(q, [in_B], [out_B]))
        sem = self.st.enter_context(self.nc.semaphore("cc_%d" % self.nsem))
        self.nsem += 1
        ins = q.eng.collective_compute(kind, op, replica_groups=groups, ins=[in_ap], outs=[out_ap])
        ins.then_inc(sem)
        tok = (sem, 1)
        in_B.r[id(sem)] = tok
        out_B.w = tok
        out_B.r = {}
        self.ccs = getattr(self, "ccs", []) + [tok]

    def renew(self):
        self.barrier()
        ccsems = set(id(cs) for (cs, cv) in getattr(self, "ccs", []))
        for b in Buf.ALL:
            if b.w is not None and id(b.w[0]) in ccsems:
                b.r = {}
                continue
            b.w = None
            b.r = {}
        for name, qq in self.q.items():
            qq.sem = self.st.enter_context(self.nc.semaphore("q%d_%s" % (self.nsem, name)))
            self.nsem += 1
            qq.cnt = 0
            qq.seen = {}
        for b in self.dbufs:
            pass

    def barrier(self, final=False):
        sp = self.q["sp"]
        deps = {}
        for b in self.dbufs:
            if b.dcnt:
                deps[id(b.dsem)] = (b.dsem, b.dcnt)
        if final:
            for (cs, cv) in getattr(self, "ccs", []):
                deps[id(cs)] = (cs, cv)
        for qq in self.q.values():
            if qq is not sp and qq.cnt:
                deps[id(qq.sem)] = (qq.sem, qq.cnt)
        self._wait(sp, deps)
        self.op("sp", lambda e: e.nop())
        for qq in self.q.values():
            if qq is sp:
                continue
            d = {id(sp.sem): (sp.sem, sp.cnt)}
            self._wait(qq, d)


def build_program(n_layers, fused):
    PH = os.environ.get('KPH', '123abBC')
    nc = bass.Bass("TRN2", target_bir_lowering=False)
    L = n_layers

    def din(name, shape, dt=F32):
        return nc.dram_tensor(name, shape, dt, kind="ExternalInput").ap()

    io = dict(
        xfull=din("xfull", [SEQ, D_MODEL]),
        xown=din("xown", [TOWN, D_MODEL]),
        w_in=din("w_in", [L, D_MODEL, D_IN]),
        w_out=din("w_out", [L, D_MODEL, D_MODEL]),
        gcol=din("gcol", [L, 128, 8]),
        vecs=din("vecs", [L, 128, V_N]),
        fg=din("fg", [128, D_MODEL]),
        tabKV=din("tabKV", [SEQ, 256]),
        tabQ=din("tabQ", [TOWN, 256]),
        tabB=din("tabB", [NT_LOC * 128, 256]),
        cb_int=din("cb_int", [L, 128, 5 * 512]),
        cb_sp=din("cb_sp", [L, 4, 128, 6 * 512]),
        bmask=din("bmask", [128, 512]),
        halosel=din("halosel", [128, 6]),
        y=nc.dram_tensor("y", [TOWN, D_MODEL], F32, kind="ExternalOutput").ap(),
    )
    if fused:
        io["rsel"] = din("rsel", [128, 4])
        io["x1own"] = nc.dram_tensor("x1own", [TOWN, D_MODEL], F32, kind="Internal").ap()
        io["xg_src"] = nc.dram_tensor("xg_src", [SEQ, D_MODEL], BF16, kind="Internal").ap()
        io["xg_dst"] = nc.dram_tensor("xg_dst", [SEQ, D_MODEL], BF16, kind="Internal").ap()
    else:
        io["xnext"] = nc.dram_tensor("xnext", [TOWN, D_MODEL], F32, kind="ExternalOutput").ap()
    x1ownB = Buf("x1own")
    xgsB = [Buf("xg_src%d" % i) for i in range(8)]
    xgdB = [Buf("xg_dst%d" % i) for i in range(8)]

    with ExitStack() as st:
        K = Ctx(nc, st)
        E = st.enter_context

        uniq = [0]

        def sb(name, shape, dt, stack=None):
            uniq[0] += 1
            t = (stack or st).enter_context(nc.sbuf_tensor("s%d_%s" % (uniq[0], name), shape, dt))
            return t, Buf(name)

        ident, identB = sb("ident", [128, 128], BF16)
        identf, identfB = sb("identf", [128, 128], F32)
        pT = [(E(nc.psum_tensor("pT%d" % i, [128, 1024], BF16)), Buf("pT%d" % i)) for i in range(2)]
        pzd = [E(nc.psum_tensor("pzd%d" % i, [128, 1024], F32)) for i in range(2)]
        pz = [(pzd[i // 2][:, (i % 2) * 512:(i % 2 + 1) * 512], Buf("pz%d" % i)) for i in range(4)]
        pz += [(E(nc.psum_tensor("pz%d" % i, [128, 512], F32)), Buf("pz%d" % i)) for i in (4, 5)]
        pzs = pz[:4]
        xs = [sb("xs%d" % i, [128, 1024], F32) for i in range(3)]
        xb = [sb("xb%d" % i, [128, 1024], BF16) for i in range(2)]
        xT = [sb("xT%d" % i, [128, 8, 128], BF16) for i in range(3)]
        junk, junkB = sb("junk", [128, 1024], BF16)
        st1 = [sb("st1_%d" % i, [128, 8], F32) for i in range(4)]
        hn = [sb("hn%d" % i, [128, 8], F32) for i in range(3)]
        t1 = [sb("t1_%d" % i, [128, 256], F32) for i in range(2)]
        t2 = [sb("t2_%d" % i, [128, 256], F32) for i in range(2)]
        t3 = [sb("t3_%d" % i, [128, 256], F32) for i in range(2)]
        tabt = [sb("tabt%d" % i, [128, 256], F32) for i in range(5)]
        PT2 = [sb("PT%d" % i, [128, 1024], BF16)[0] for i in range(3)]
        PT = [(PT2[i // 2][:, (i % 2) * 512:(i % 2 + 1) * 512], Buf("PT%d" % i)) for i in range(6)]
        mixT, mixTB = sb("mixT", [128, 8, TOWN], BF16)
        mixtm = [sb("mixtm%d" % i, [128, 4, 256], BF16) for i in range(2)]
        vecs, vecsB = sb("vecs", [128, V_N], F32)
        gcol, gcolB = sb("gcol", [128, 8], F32)
        lamt, lamtB = sb("lamt", [128, 8], F32)
        esink, esinkB = sb("esink", [128, 4], F32)
        subg, subgB = sb("subg", [128, 64], F32)
        hsel, hselB = sb("hsel", [128, 6], F32)
        bmask, bmaskB = sb("bmask", [128, 4, 128], F32)
        fin = [sb("fin%d" % i, [128, 4, 64], F32) for i in range(2)]
        fin2 = [sb("fin2_%d" % i, [128, 4, 64], F32) for i in range(2)]
        rd = [sb("rd%d" % i, [128, 8], F32) for i in range(2)]

        rr = {}

        def rot(lst, key=None):
            k = key or id(lst)
            i = rr.get(k, 0)
            rr[k] = i + 1
            return lst[i % len(lst)]

        mhalf, mhalfB = sb("mhalf", [128, 8], F32)
        K.op("pool", lambda e: e.memset(mhalf[:], -0.5), writes=[mhalfB])
        K.op("pool", lambda e: e.memset(identf[:], 1.0), writes=[identfB])
        K.op("pool", lambda e: e.affine_select(out=identf[:], in_=identf[:], pattern=[[-1, 128]],
                                               compare_op=ALU.is_equal, fill=0.0, base=0,
                                               channel_multiplier=1), reads=[identfB], writes=[identfB])
        K.op("dve", lambda e: e.tensor_copy(out=ident[:], in_=identf[:]), reads=[identfB], writes=[identB])
        K.dma(bmask[:].rearrange("p a b -> p (a b)"), io["bmask"][:, :], bmaskB, writes=[bmaskB])
        K.dma(hsel[:], io["halosel"][:, :], hselB, writes=[hselB])
        if fused:
            rsel, rselB = sb("rsel", [128, 4], F32)
            K.dma(rsel[:], io["rsel"][:, :], rselB, writes=[rselB])

        def stats_rstd(x_t, x_B, width, s_t, s_B):
            K.op("dve", lambda e: e.scalar_tensor_tensor(out=junk[:, 0:width], in0=x_t, scalar=1.0, in1=x_t,
                                                         op0=ALU.mult, op1=ALU.mult, accum_out=s_t[:, 0:1]),
                 reads=[x_B], writes=[junkB, s_B])
            K.op("pool", lambda e: e.tensor_scalar(out=s_t[:, 1:2], in0=s_t[:, 0:1], scalar1=1.0 / width, scalar2=EPS,
                                                   op0=ALU.mult, op1=ALU.add), reads=[s_B], writes=[s_B])
            K.op("pool", lambda e: e.tensor_tensor(out=s_t[:, 2:3], in0=s_t[:, 1:2], in1=mhalf[:, 0:1], op=ALU.pow),
                 reads=[s_B, mhalfB], writes=[s_B])

        def prep_a(x_t, x_B, eng="act"):
            s_t, s_B = rot(st1)
            stats_rstd(x_t[:], x_B, 1024, s_t, s_B)
            b_t, b_B = rot(xb)
            if eng == "act":
                K.op("act", lambda e: e.activation(out=b_t[:], in_=x_t[:], func=AF.Copy), reads=[x_B], writes=[b_B])
            else:
                K.op("dve", lambda e: e.tensor_copy(out=b_t[:], in_=x_t[:]), reads=[x_B], writes=[b_B])
            return b_t, b_B, s_t, s_B

        def prep_b(b_t, b_B, eng="act"):
            p_t, p_B = rot(pT)
            for k in range(8):
                K.op("pe", lambda e, k=k: e.transpose(out=p_t[:, k * 128:(k + 1) * 128],
                                                      in_=b_t[:, k * 128:(k + 1) * 128], identity=ident[:]),
                     reads=[b_B, identB], writes=[p_B])
            T_t, T_B = rot(xT)
            if eng == "act":
                K.op("act", lambda e: e.activation(out=T_t[:].rearrange("p c t -> p (c t)"), in_=p_t[:], func=AF.Copy),
                     reads=[p_B], writes=[T_B])
            else:
                K.op("dve", lambda e: e.tensor_copy(out=T_t[:].rearrange("p c t -> p (c t)"), in_=p_t[:]),
                     reads=[p_B], writes=[T_B])
            return T_t, T_B

        def prep_cast(x_t, x_B):
            b_t, b_B = rot(xb)
            K.op("act", lambda e: e.activation(out=b_t[:], in_=x_t[:], func=AF.Copy), reads=[x_B], writes=[b_B])
            return b_t, b_B

        def prep_stats(x_t, x_B):
            s_t, s_B = rot(st1)
            stats_rstd(x_t[:], x_B, 1024, s_t, s_B)
            return s_t, s_B

        def run_pipe(n, order, lo=0):
            md = max(d for d, _ in order)
            for step in range(n + md):
                for d, f in order:
                    idx = step - d
                    if 0 <= idx < n:
                        f(lo + idx)

        def prep_tile(x_t, x_B):
            b_t, b_B, s_t, s_B = prep_a(x_t, x_B)
            T_t, T_B = prep_b(b_t, b_B)
            return T_t, T_B, s_t, s_B

        def proj(T_t, T_B, w_t, w_B, groups, z_t, z_B, s_t, s_B, evac="dve"):
            for (c0, n) in groups:
                p_t, p_B = rot(pzs, "pzproj")
                for k in range(8):
                    K.op("pe", lambda e, k=k: e.matmul(p_t[:, 0:n], lhsT=T_t[:, k, :], rhs=w_t[:, k, c0:c0 + n],
                                                       start=(k == 0), stop=(k == 7)),
                         reads=[T_B, w_B], writes=[p_B])
                if evac == "act":
                    K.op("act", lambda e: e.activation(out=z_t[:, c0:c0 + n], in_=p_t[:, 0:n], func=AF.Copy, scale=s_t[:, 2:3]),
                         reads=[p_B, s_B], writes=[z_B])
                else:
                    K.op("dve", lambda e: e.tensor_scalar(out=z_t[:, c0:c0 + n], in0=p_t[:, 0:n], scalar1=s_t[:, 2:3],
                                                          scalar2=None, op0=ALU.mult),
                         reads=[p_B, s_B], writes=[z_B])

        def headnorm(z_t, z_B, c0, H, gain_ap):
            a_t, a_B = rot(t1)
            h_t, h_B = rot(hn)
            src = z_t[:, c0:c0 + 64 * H]
            src3 = src.rearrange("p (h d) -> p h d", h=H)
            for hh_ in range(H):
                K.op("dve", lambda e: e.scalar_tensor_tensor(
                    out=a_t[:, hh_ * 64:(hh_ + 1) * 64], in0=src[:, hh_ * 64:(hh_ + 1) * 64], scalar=1.0,
                    in1=src[:, hh_ * 64:(hh_ + 1) * 64], op0=ALU.mult, op1=ALU.mult, accum_out=h_t[:, hh_:hh_ + 1]),
                    reads=[z_B], writes=([a_B, h_B] if hh_ in (0, H - 1) else []))
            K.op("pool", lambda e: e.tensor_scalar(out=h_t[:, 0:H], in0=h_t[:, 0:H], scalar1=1.0 / 64, scalar2=EPS,
                                                   op0=ALU.mult, op1=ALU.add), reads=[h_B], writes=[h_B])
            K.op("pool", lambda e: e.tensor_tensor(out=h_t[:, 0:H], in0=h_t[:, 0:H], in1=mhalf[:, 0:H], op=ALU.pow),
                 reads=[h_B, mhalfB], writes=[h_B])
            K.op("dve", lambda e: e.tensor_tensor(out=src3, in0=src3,
                                                  in1=h_t[:, 0:H].unsqueeze(2).to_broadcast([128, H, 64]), op=ALU.mult),
                 reads=[z_B, h_B], writes=[z_B])
            K.op("dve", lambda e: e.tensor_tensor(out=src3, in0=src3,
                                                  in1=gain_ap.unsqueeze(1).to_broadcast([128, H, 64]), op=ALU.mult),
                 reads=[z_B, vecsB], writes=[z_B])

        def rope(z_t, z_B, c0, H, hs, tab_t, tab_B, tc0, dsts, dst_B):
            nseg = 32 // hs
            a_t, a_B = rot(t2)
            b_t, b_B = rot(t3)
            src = z_t[:, c0:c0 + 64 * H]
            src3 = src.rearrange("p (h d) -> p h d", h=H)
            cc = tab_t[:, tc0:tc0 + 64]
            ss = tab_t[:, tc0 + 64:tc0 + 128]
            K.op("dve", lambda e: e.tensor_tensor(out=a_t[:, 0:64 * H].rearrange("p (h d) -> p h d", h=H), in0=src3,
                                                  in1=cc.unsqueeze(1).to_broadcast([128, H, 64]), op=ALU.mult),
                 reads=[z_B, tab_B], writes=[a_B])
            s5 = src.rearrange("p (h s two k) -> p h s two k", h=H, s=nseg, two=2)
            b5 = b_t[:, 0:64 * H].rearrange("p (h s two k) -> p h s two k", h=H, s=nseg, two=2)
            ss4 = ss.rearrange("p (s two k) -> p s two k", s=nseg, two=2)
            for (o_, i_) in ((0, 1), (1, 0)):
                K.op("pool", lambda e, o_=o_, i_=i_: e.tensor_tensor(
                    out=b5[:, :, :, o_, :], in0=s5[:, :, :, i_, :],
                    in1=ss4[:, :, o_, :].unsqueeze(1).to_broadcast([128, H, nseg, hs]), op=ALU.mult),
                    reads=[z_B, tab_B], writes=[b_B])
            for (dst_ap, sel) in dsts:
                K.op("dve", lambda e, dst_ap=dst_ap, sel=sel: e.tensor_tensor(
                    out=dst_ap, in0=sel(a_t[:, 0:64 * H]), in1=sel(b_t[:, 0:64 * H]), op=ALU.add),
                    reads=[a_B, b_B], writes=[dst_B])

        def silu_gate(z_t, z_B, c0, n, dst_ap, dst_B):
            a_t, a_B = rot(t1)
            K.op("act", lambda e: e.activation(out=a_t[:, 0:n], in_=z_t[:, c0:c0 + n], func=AF.Exp, scale=-1.0),
                 reads=[z_B], writes=[a_B])
            K.op("act", lambda e: e.activation(out=a_t[:, 0:n], in_=a_t[:, 0:n], func=AF.Ln, scale=1.0, bias=1.0),
                 reads=[a_B], writes=[a_B])
            K.op("act", lambda e: e.activation(out=a_t[:, 0:n], in_=a_t[:, 0:n], func=AF.Exp, scale=-1.0),
                 reads=[a_B], writes=[a_B])
            K.op("dve", lambda e: e.tensor_tensor(out=dst_ap, in0=z_t[:, c0:c0 + n], in1=a_t[:, 0:n], op=ALU.mult),
                 reads=[a_B, z_B], writes=[dst_B])

        def transpose_multi(srcs, dsts):
            p_t, p_B = rot(pT)
            for i, (s_ap, s_B) in enumerate(srcs):
                K.op("pe", lambda e: e.transpose(out=p_t[:, i * 128:(i + 1) * 128], in_=s_ap, identity=ident[:]),
                     reads=[s_B, identB], writes=[p_B])
            for (dst_ap, dst_B, b0, nb_, eng) in dsts:
                src = p_t[:, b0 * 128:(b0 + nb_) * 128]
                if len(dst_ap.shape) == 3:
                    src = src.rearrange("p (a b) -> p a b", a=dst_ap.shape[1])
                if eng == "act":
                    K.op("act", lambda e: e.activation(out=dst_ap, in_=src, func=AF.Copy), reads=[p_B], writes=[dst_B])
                else:
                    K.op("dve", lambda e: e.tensor_copy(out=dst_ap, in_=src), reads=[p_B], writes=[dst_B])

        def transpose_to(src_aps, src_B, dst_ap, dst_B, eng="dve"):
            p_t, p_B = rot(pT)
            n = len(src_aps)
            for i, s_ap in enumerate(src_aps):
                K.op("pe", lambda e, i=i, s_ap=s_ap: e.transpose(out=p_t[:, i * 128:(i + 1) * 128], in_=s_ap,
                                                                 identity=ident[:]),
                     reads=[src_B, identB], writes=[p_B])
            src = p_t[:, 0:n * 128]
            if len(dst_ap.shape) == 3:
                src = src.rearrange("p (a b) -> p a b", a=dst_ap.shape[1])
            if eng == "act":
                K.op("act", lambda e: e.activation(out=dst_ap, in_=src, func=AF.Copy), reads=[p_B], writes=[dst_B])
            else:
                K.op("dve", lambda e: e.tensor_copy(out=dst_ap, in_=src), reads=[p_B], writes=[dst_B])

        def load_weights_chunk(l, c, pieces):
            groups, cur, used = [], [], 0
            for pc in pieces:
                if used + pc[1] > 1024:
                    groups.append(cur)
                    cur, used = [], 0
                cur.append(pc)
                used += pc[1]
            groups.append(cur)
            i = 0
            for grp_ in groups:
                w_t, w_B = rot(xs)
                off = 0
                offs = []
                for (c0, n, dst_ap, dst_B) in grp_:
                    K.dma(w_t[:, off:off + n], io["w_in"][l, c * 128:(c + 1) * 128, c0:c0 + n], w_B, writes=[w_B])
                    offs.append(off)
                    off += n
                for k_, (c0, n, dst_ap, dst_B) in enumerate(grp_):
                    o_ = offs[k_]
                    if (i + c) % 2 == 0:
                        K.op("act", lambda e: e.activation(out=dst_ap, in_=w_t[:, o_:o_ + n], func=AF.Copy, scale=gcol[:, c:c + 1]),
                             reads=[w_B, gcolB], writes=[dst_B])
                    else:
                        K.op("dve", lambda e: e.tensor_scalar(
                            out=dst_ap, in0=w_t[:, o_:o_ + n], scalar1=gcol[:, c:c + 1], scalar2=None, op0=ALU.mult),
                            reads=[w_B, gcolB], writes=[dst_B])
                    i += 1

        def load_x(src_ap, extra=()):
            x_t, x_B = rot(xs)
            K.dma(x_t[:], src_ap, x_B, reads=list(extra), writes=[x_B])
            return x_t, x_B

        def attn_pipe(n, qk, ex, pv, la):
            for idx in range(n + la):
                if idx < n:
                    qk(idx)
                    ex(idx)
                if idx >= la:
                    pv(idx - la)

        for l in range(L):
            if l > 0:
                K.renew()
            x_src_full = io["xfull"] if l == 0 else io["xg_dst"]
            x_src_own = io["xown"] if l == 0 else io["x1own"]
            xfB = (lambda row: []) if l == 0 else (lambda row: [xgdB[row // 1024]])
            xoB = [] if l == 0 else [x1ownB]
            K.dma(vecs[:], io["vecs"][l, :, :], vecsB, writes=[vecsB])
            K.dma(gcol[:], io["gcol"][l, :, :], gcolB, writes=[gcolB])
            K.op("act", lambda e: e.activation(out=esink[:], in_=vecs[:, V_SINK:V_SINK + 4], func=AF.Exp),
                 reads=[vecsB], writes=[esinkB])
            K.op("dve", lambda e: e.tensor_tensor(out=junk[:, 0:32], in0=vecs[:, V_LQ1:V_LQ1 + 32],
                                                  in1=vecs[:, V_LK1:V_LK1 + 32], op=ALU.mult),
                 reads=[vecsB], writes=[junkB])
            K.op("dve", lambda e: e.tensor_reduce(out=lamt[:, 0:1], in_=junk[:, 0:32], axis=AX.X, op=ALU.add),
                 reads=[junkB], writes=[lamtB])
            K.op("dve", lambda e: e.tensor_tensor(out=junk[:, 32:64], in0=vecs[:, V_LQ2:V_LQ2 + 32],
                                                  in1=vecs[:, V_LK2:V_LK2 + 32], op=ALU.mult),
                 reads=[vecsB], writes=[junkB])
            K.op("dve", lambda e: e.tensor_reduce(out=lamt[:, 1:2], in_=junk[:, 32:64], axis=AX.X, op=ALU.add),
                 reads=[junkB], writes=[lamtB])
            K.op("act", lambda e: e.activation(out=lamt[:, 2:4], in_=lamt[:, 0:2], func=AF.Exp),
                 reads=[lamtB], writes=[lamtB])
            K.op("dve", lambda e: e.tensor_tensor(out=lamt[:, 4:5], in0=lamt[:, 2:3], in1=lamt[:, 3:4], op=ALU.subtract),
                 reads=[lamtB], writes=[lamtB])
            K.op("dve", lambda e: e.tensor_tensor(out=lamt[:, 4:5], in0=lamt[:, 4:5], in1=vecs[:, V_LC:V_LC + 1], op=ALU.add),
                 reads=[lamtB, vecsB], writes=[lamtB])
            K.op("dve", lambda e: e.tensor_scalar(out=lamt[:, 5:6], in0=lamt[:, 4:5], scalar1=-1.0, scalar2=None, op0=ALU.mult),
                 reads=[lamtB], writes=[lamtB])
            K.op("dve", lambda e: e.tensor_scalar(out=subg[:], in0=vecs[:, V_SUB:V_SUB + 64], scalar1=vecs[:, V_LC + 1:V_LC + 2],
                                                  scalar2=None, op0=ALU.mult), reads=[vecsB], writes=[subgB])

            with ExitStack() as ph:
              if '1' in PH:
                wBC, wBCB = sb("wBC", [128, 8, 1792], BF16, ph)
                kTb, kTbB = sb("kTb", [128, NT_LOC * 128], BF16, ph)
                vb, vbB = sb("vb", [128, NT_LOC, 2, 66], BF16, ph)
                kTc, kTcB = sb("kTc", [128, 2, NT_LOC * 128], BF16, ph)
                vc, vcB = sb("vc", [128, NT_LOC, 4, 66], BF16, ph)
                qTb, qTbB = sb("qTb", [128, NT_OWN, 2, 128], BF16, ph)
                qTc, qTcB = sb("qTc", [128, NT_OWN, 2, 128], BF16, ph)
                gbc, gbcB = sb("gbc", [128, NT_OWN, 512], BF16, ph)
                ctp = [sb("ctp%d" % i, [128, 512], F32, ph) for i in range(3)]
                zs = [sb("zs%d" % i, [128, 1792], F32, ph) for i in range(2)]
                finp = fin + fin2
                qtm = [sb("qtm%d" % i, [128, 512], BF16, ph) for i in range(3)]
                ktm = [sb("ktm%d" % i, [128, 384], BF16, ph) for i in range(3)]
                tmpc = [sb("tmpc%d" % i, [128, 512], F32, ph) for i in range(2)]

                K.op("pool", lambda e: e.memset(vb[:].rearrange("p a b c -> p (a b c)"), 1.0), writes=[vbB])
                K.op("pool", lambda e: e.memset(vc[:].rearrange("p a b c -> p (a b c)"), 1.0), writes=[vcB])
                for c in range(8):
                    load_weights_chunk(l, c, [(768, 1024, wBC[:, c, 0:1024], wBCB), (1792, 768, wBC[:, c, 1024:1792], wBCB)])

                order1 = list(range(2, 2 + NT_OWN)) + [0, 1, 18, 19]
                SP1 = {}

                def p1_s0(idx):
                    bt = order1[idx]
                    if 2 <= bt < 2 + NT_OWN:
                        x_t, x_B = load_x(x_src_own[(bt - 2) * 128:(bt - 1) * 128, :], xoB)
                    else:
                        prev = bt < 2
                        k0 = rr.get(id(xs), 0)
                        rr[id(xs)] = k0 + 3
                        x_t, x_B = xs[k0 % 3]
                        sc0 = 0 if prev else 3
                        for m in range(3):
                            h_t, h_B = xs[(k0 + 1 + (m % 2)) % 3]
                            row0 = (m + 1) * TOWN + ((bt - 2) * 128 if prev else (bt - 18) * 128)
                            h_v = h_t[:] if l == 0 else h_t[:].bitcast(BF16)[:, 0:1024]
                            K.dma(h_v, x_src_full[row0:row0 + 128, :], h_B, reads=xfB(row0), writes=[h_B])
                            if m == 0:
                                K.op("dve", lambda e: e.tensor_scalar(out=x_t[:], in0=h_v, scalar1=hsel[:, sc0:sc0 + 1],
                                                                      scalar2=None, op0=ALU.mult),
                                     reads=[h_B, hselB], writes=[x_B])
                            else:
                                K.op("dve", lambda e: e.scalar_tensor_tensor(
                                    out=x_t[:], in0=h_v, scalar=hsel[:, sc0 + m:sc0 + m + 1], in1=x_t[:],
                                    op0=ALU.mult, op1=ALU.add), reads=[h_B, hselB, x_B], writes=[x_B])
                    tb_t, tb_B = rot(tabt)
                    K.dma(tb_t[:], io["tabB"][bt * 128:(bt + 1) * 128, :], tb_B, writes=[tb_B], qn="act")
                    SP1[bt] = [x_t, x_B, tb_t, tb_B]

                def p1_cast(idx):
                    bt = order1[idx]
                    SP1[bt] += list(prep_cast(*SP1[bt][0:2]))

                def p1_stats(idx):
                    bt = order1[idx]
                    SP1[bt] += list(prep_stats(*SP1[bt][0:2]))

                def p1_b2(idx):
                    bt = order1[idx]
                    x_t, x_B, tb_t, tb_B, b_t, b_B, s_t, s_B = SP1[bt]
                    T_t, T_B = prep_b(b_t, b_B)
                    SP1[bt] = [tb_t, tb_B, T_t, T_B, s_t, s_B]

                def p1_c(idx):
                    bt = order1[idx]
                    tb_t, tb_B, T_t, T_B, s_t, s_B = SP1[bt]
                    z_t, z_B = rot(zs)
                    proj(T_t, T_B, wBC, wBCB, [(0, 512), (512, 512), (1024, 512), (1536, 256)], z_t, z_B, s_t, s_B)
                    SP1[bt] = (z_t, z_B, tb_t, tb_B)

                def p1_s1(idx):
                    bt = order1[idx]
                    z_t, z_B, tb_t, tb_B = SP1[bt]
                    own = 2 <= bt < 2 + NT_OWN
                    i_own = bt - 2
                    q_t, q_B = rot(qtm)
                    k_t, k_B = rot(ktm)
                    rope(z_t, z_B, 256, 2, 32, tb_t, tb_B, 0,
                         [(k_t[:, 0:128], lambda a: a)], k_B)
                    K.op("pool", lambda e: e.tensor_copy(out=k_t[:, 128:384], in_=z_t[:, 1024:1280]),
                         reads=[z_B], writes=[k_B])
                    K.op("act", lambda e: e.activation(out=vb[:, bt, :, 0:64],
                                                       in_=z_t[:, 384:512].rearrange("p (h d) -> p h d", h=2), func=AF.Copy),
                         reads=[z_B], writes=[vbB])
                    K.op("act", lambda e: e.activation(out=vc[:, bt, :, 0:64],
                                                       in_=z_t[:, 1280:1536].rearrange("p (h d) -> p h d", h=4), func=AF.Copy),
                         reads=[z_B], writes=[vcB])
                    if own:
                        dst = q_t[:, 0:256].rearrange("p (g kv d) -> p kv g d", g=2, kv=2)
                        rope(z_t, z_B, 0, 4, 32, tb_t, tb_B, 128,
                             [(dst, lambda a: a.rearrange("p (kv g d) -> p kv g d", kv=2, g=2))], q_B)
                        K.op("pool", lambda e: e.tensor_copy(out=q_t[:, 256:512], in_=z_t[:, 768:1024]),
                             reads=[z_B], writes=[q_B])
                        silu_gate(z_t, z_B, 512, 256, gbc[:, i_own, 0:256], gbcB)
                        silu_gate(z_t, z_B, 1536, 256, gbc[:, i_own, 256:512], gbcB)
                    SP1[bt] = (q_t, q_B, k_t, k_B)

                def p1_s2(idx):
                    bt = order1[idx]
                    q_t, q_B, k_t, k_B = SP1.pop(bt)
                    own = 2 <= bt < 2 + NT_OWN
                    i_own = bt - 2
                    srcs = [(k_t[:, 0:128], k_B), (k_t[:, 128:256], k_B), (k_t[:, 256:384], k_B)]
                    dsts = [(kTb[:, bt * 128:(bt + 1) * 128], kTbB, 0, 1, "act"),
                            (kTc[:, 0, bt * 128:(bt + 1) * 128], kTcB, 1, 1, "act"),
                            (kTc[:, 1, bt * 128:(bt + 1) * 128], kTcB, 2, 1, "act")]
                    if own:
                        srcs += [(q_t[:, k_ * 128:(k_ + 1) * 128], q_B) for k_ in range(4)]
                        dsts += [(qTb[:, i_own, :, :].rearrange("p g q -> p (g q)"), qTbB, 3, 2, "act"),
                                 (qTc[:, i_own, :, :].rearrange("p g q -> p (g q)"), qTcB, 5, 2, "act")]
                    transpose_multi(srcs, dsts)

                def run_p1_step1(lo, hi):
                    run_pipe(hi - lo, [(1, p1_cast), (1, p1_stats), (2, p1_b2), (3, p1_c), (4, p1_s1), (5, p1_s2), (0, p1_s0)], lo)

                def p1_attn(i):
                    acc_t, acc_B = pz[4]
                    accv = acc_t[:, 0:260].rearrange("p (a b) -> p a b", b=65)
                    wins = [i + 1, i + 2, i + 3]
                    mids = [0 if i == 0 else 1, None, 3 if i == NT_OWN - 1 else 2]
                    slots = {}

                    def qkB(idx, i=i, wins=wins, slots=slots):
                        j = wins[idx]
                        banks = [rot(pzs, "pzst"), rot(pzs, "pzst")]
                        slots[idx] = banks
                        for kv in range(2):
                            s_t, s_B = banks[kv]
                            K.op("pe", lambda e: e.matmul(
                                s_t[:, 0:256], lhsT=kTb[kv * 64:(kv + 1) * 64, j * 128:(j + 1) * 128],
                                rhs=qTb[kv * 64:(kv + 1) * 64, i, :, :].rearrange("p g q -> p (g q)"),
                                start=True, stop=True), reads=[kTbB, qTbB], writes=[s_B])

                    def exB(idx, mids=mids, slots=slots):
                        banks = slots[idx]
                        p_t, p_B = rot(PT)
                        slots[idx] = (p_t, p_B)
                        for kv in range(2):
                            s_t, s_B = banks[kv]
                            K.op("act", lambda e: e.activation(out=p_t[:, kv * 256:(kv + 1) * 256], in_=s_t[:, 0:256], func=AF.Exp),
                                 reads=[s_B], writes=[p_B])
                        if mids[idx] is not None:
                            mi = mids[idx]
                            K.op("pool", lambda e: e.tensor_tensor(
                                out=p_t[:].rearrange("p (h q) -> p h q", h=4), in0=p_t[:].rearrange("p (h q) -> p h q", h=4),
                                in1=bmask[:, mi, :].unsqueeze(1).to_broadcast([128, 4, 128]), op=ALU.mult),
                                reads=[p_B, bmaskB], writes=[p_B])

                    def pvB(idx, wins=wins, slots=slots, accv=accv, acc_B=acc_B):
                        j = wins[idx]
                        p_t, p_B = slots[idx]
                        for kv in range(2):
                            for g in range(2):
                                h = 2 * kv + g
                                K.op("pe", lambda e, kv=kv, g=g, h=h: e.matmul(
                                    accv[:, h, :], lhsT=p_t[:, (kv * 2 + g) * 128:(kv * 2 + g + 1) * 128],
                                    rhs=vb[:, j, kv, 0:65], start=(idx == 0 and h == 0), stop=(idx == 2 and h == 3)),
                                    reads=[p_B, vbB], writes=[acc_B])

                    if 'B' in PH:
                        attn_pipe(3, qkB, exB, pvB, 1)
                    r_t, r_B = rot(rd)
                    K.op("dve", lambda e: e.tensor_tensor(out=r_t[:, 0:4], in0=accv[:, :, 64], in1=esink[:], op=ALU.add),
                         reads=[acc_B, esinkB], writes=[r_B])
                    K.op("dve", lambda e: e.reciprocal(out=r_t[:, 0:4], in_=r_t[:, 0:4]), reads=[r_B], writes=[r_B])
                    fb_t, fb_B = rot(finp)
                    K.op("dve", lambda e: e.tensor_tensor(out=fb_t[:], in0=accv[:, :, 0:64],
                                                          in1=r_t[:, 0:4].unsqueeze(2).to_broadcast([128, 4, 64]), op=ALU.mult),
                         reads=[acc_B, r_B], writes=[fb_B])
                    if i == 0:
                        cwin = list(range(0, 6))
                    elif i == NT_OWN - 1:
                        cwin = list(range(14, 20))
                    else:
                        cwin = list(range(i, i + 5))
                    spi = {0: 0, 1: 1, NT_OWN - 2: 2, NT_OWN - 1: 3}.get(i, None)
                    acc2_t, acc2_B = pz[5]
                    accv2 = acc2_t[:, 0:260].rearrange("p (a b) -> p a b", b=65)
                    slots2 = {}
                    nw = len(cwin)

                    def qkC(idx, i=i, cwin=cwin, slots2=slots2):
                        j = cwin[idx]
                        banks = [rot(pzs, "pzst"), rot(pzs, "pzst")]
                        slots2[idx] = banks
                        for h in range(4):
                            p_, hh = h // 2, h % 2
                            s_t, s_B = banks[hh]
                            K.op("pe", lambda e: e.matmul(
                                s_t[:, p_ * 128:(p_ + 1) * 128], lhsT=kTc[hh * 64:(hh + 1) * 64, p_, j * 128:(j + 1) * 128],
                                rhs=qTc[hh * 64:(hh + 1) * 64, i, p_, :], start=True, stop=True),
                                reads=[kTcB, qTcB], writes=[s_B])

                    def exC(idx, spi=spi, slots2=slots2):
                        banks = slots2[idx]
                        tb_t, tb_B = rot(ctp)
                        if spi is None:
                            K.dma(tb_t[:], io["cb_int"][l, :, idx * 512:(idx + 1) * 512], tb_B, writes=[tb_B], qn="pool")
                        else:
                            K.dma(tb_t[:], io["cb_sp"][l, spi, :, idx * 512:(idx + 1) * 512], tb_B, writes=[tb_B], qn="pool")
                        c_t, c_B = rot(tmpc)
                        for hh in range(2):
                            s_t, s_B = banks[hh]
                            tv = tb_t[:].rearrange("p (pp hh q) -> p hh pp q", pp=2, hh=2)[:, hh]
                            cv = c_t[:].rearrange("p (pp hh q) -> p hh pp q", pp=2, hh=2)[:, hh]
                            K.op("dve", lambda e: e.scalar_tensor_tensor(
                                out=cv, in0=s_t[:, 0:256].rearrange("p (pp q) -> p pp q", pp=2), scalar=0.125, in1=tv,
                                op0=ALU.mult, op1=ALU.add), reads=[s_B, tb_B], writes=[c_B])
                        p_t, p_B = rot(PT)
                        slots2[idx] = (p_t, p_B)
                        K.op("act", lambda e: e.activation(out=p_t[:], in_=c_t[:], func=AF.Exp), reads=[c_B], writes=[p_B])

                    def pvC(idx, cwin=cwin, slots2=slots2, accv2=accv2, acc2_B=acc2_B, nw=nw):
                        j = cwin[idx]
                        p_t, p_B = slots2[idx]
                        for h in range(4):
                            K.op("pe", lambda e, h=h: e.matmul(
                                accv2[:, h, :], lhsT=p_t[:, h * 128:(h + 1) * 128], rhs=vc[:, j, h, 0:65],
                                start=(idx == 0 and h == 0), stop=(idx == nw - 1 and h == 3)), reads=[p_B, vcB], writes=[acc2_B])

                    if 'C' in PH:
                        attn_pipe(nw, qkC, exC, pvC, 1)
                    r_t, r_B = rot(rd)
                    K.op("dve", lambda e: e.reciprocal(out=r_t[:, 0:4], in_=accv2[:, :, 64]), reads=[acc2_B], writes=[r_B])
                    fc_t, fc_B = rot(finp)
                    K.op("dve", lambda e: e.tensor_tensor(out=fc_t[:], in0=accv2[:, :, 0:64],
                                                          in1=r_t[:, 0:4].unsqueeze(2).to_broadcast([128, 4, 64]), op=ALU.mult),
                         reads=[acc2_B, r_B], writes=[fc_B])
                    SY[i] = (fb_t, fb_B, fc_t, fc_B)

                def p1_fin(i):
                    fb_t, fb_B, fc_t, fc_B = SY.pop(i)
                    m_t, m_B = rot(mixtm)
                    K.op("pool", lambda e: e.tensor_tensor(out=m_t[:, 0, :], in0=fb_t[:].rearrange("p h d -> p (h d)"),
                                                           in1=gbc[:, i, 0:256], op=ALU.mult),
                         reads=[fb_B, gbcB], writes=[m_B])
                    K.op("pool", lambda e: e.tensor_tensor(out=m_t[:, 1, :], in0=fc_t[:].rearrange("p h d -> p (h d)"),
                                                           in1=gbc[:, i, 256:512], op=ALU.mult),
                         reads=[fc_B, gbcB], writes=[m_B])
                    transpose_to([m_t[:, 0, 0:128], m_t[:, 0, 128:256], m_t[:, 1, 0:128], m_t[:, 1, 128:256]], m_B,
                                 mixT[:, 2:6, i * 128:(i + 1) * 128], mixTB)

                def run_p1_attn(tiles):
                    for k_, i in enumerate(tiles):
                        p1_attn(i)
                        if k_ >= 1:
                            p1_fin(tiles[k_ - 1])
                    p1_fin(tiles[-1])

                SY = {}
                run_p1_step1(0, NT_OWN)
                run_p1_attn(list(range(2, NT_OWN - 2)))
                run_p1_step1(NT_OWN, NT_LOC)
                run_p1_attn([0, 1, NT_OWN - 2, NT_OWN - 1])
                K.barrier()

            with ExitStack() as ph:
              if '2' in PH:
                wKV, wKVB = sb("wKV", [128, 8, 512], BF16, ph)
                wQG, wQGB = sb("wQG", [128, 8, 1024], BF16, ph)
                kTa, kTaB = sb("kTa", [128, SEQ], BF16, ph)
                va, vaB = sb("va", [128, NT_ALL, 2, 66], BF16, ph)
                kTd, kTdB = sb("kTd", [128, SEQ], BF16, ph)
                vd, vdB = sb("vd", [128, NT_ALL, 2, 66], BF16, ph)
                qTa = [sb("qTa%d" % i, [128, 2, 512], BF16, ph) for i in range(2)]
                qTd = [sb("qTd%d" % i, [128, 2, 2, 512], BF16, ph) for i in range(2)]
                gads = [sb("gad%d" % i, [128, 4, 512], GDT, ph) for i in range(2)]
                qa_tm = [sb("qatm%d" % i, [128, 256], BF16, ph) for i in range(2)]
                qd_tm = [sb("qdtm%d" % i, [128, 2, 256], BF16, ph) for i in range(2)]
                k2_tm = [sb("k2tm%d" % i, [128, 256], BF16, ph) for i in range(2)]
                zs = [sb("zs%d" % i, [128, 1024], F32, ph) for i in range(2)]

                K.op("pool", lambda e: e.memset(va[:].rearrange("p a b c -> p (a b c)"), 1.0), writes=[vaB])
                K.op("pool", lambda e: e.memset(vd[:].rearrange("p a b c -> p (a b c)"), 1.0), writes=[vdB])
                for i in range(2):
                    K.op("pool", lambda e, i=i: e.memset(qd_tm[i][0][:].rearrange("p a b -> p (a b)"), 0.0),
                         writes=[qd_tm[i][1]])
                for c in range(8):
                    load_weights_chunk(l, c, [
                        (256, 256, wKV[:, c, 0:256], wKVB), (2816, 256, wKV[:, c, 256:512], wKVB),
                        (0, 256, wQG[:, c, 0:256], wQGB), (512, 256, wQG[:, c, 256:512], wQGB),
                        (2560, 256, wQG[:, c, 512:768], wQGB), (3072, 256, wQG[:, c, 768:1024], wQGB)])


                def staged(n, stages):
                    ns = len(stages)
                    for step in range(n + ns - 1):
                        for si in reversed(range(ns)):
                            idx = step - si
                            if 0 <= idx < n:
                                stages[si](idx)

                S1 = {}

                kvx = xb + [(xs[i_][0][:].bitcast(BF16)[:, 0:1024], xs[i_][1]) for i_ in range(3)]

                def kv_a(t):
                    if l == 0:
                        x_t, x_B = load_x(x_src_full[t * 128:(t + 1) * 128, :], xfB(t * 128))
                    else:
                        x_t, x_B = rot(kvx)
                        K.dma(x_t[:], x_src_full[t * 128:(t + 1) * 128, :], x_B, reads=xfB(t * 128), writes=[x_B])
                    tb_t, tb_B = rot(tabt)
                    K.dma(tb_t[:], io["tabKV"][t * 128:(t + 1) * 128, :], tb_B, writes=[tb_B], qn="act")
                    S1[t] = [x_t, x_B, tb_t, tb_B]

                def kv_cast(t):
                    x_t, x_B = S1[t][0:2]
                    S1[t] += list(prep_cast(x_t, x_B)) if l == 0 else [x_t, x_B]

                def kv_stats(t):
                    x_t, x_B = S1[t][0:2]
                    S1[t] += list(prep_stats(x_t, x_B))

                def kv_b2(t):
                    x_t, x_B, tb_t, tb_B, b_t, b_B, s_t, s_B = S1[t]
                    T_t, T_B = prep_b(b_t, b_B)
                    S1[t] = [x_t, x_B, tb_t, tb_B, T_t, T_B, s_t, s_B]

                def kv_c(t):
                    x_t, x_B, tb_t, tb_B, T_t, T_B, s_t, s_B = S1[t]
                    z_t, z_B = rot(zs)
                    proj(T_t, T_B, wKV, wKVB, [(0, 512)], z_t, z_B, s_t, s_B, evac="act")
                    S1[t] = (z_t, z_B, tb_t, tb_B)

                def kv_d(t):
                    z_t, z_B, tb_t, tb_B = S1[t]
                    k_t, k_B = rot(k2_tm)
                    headnorm(z_t, z_B, 0, 2, vecs[:, V_KN:V_KN + 64])
                    rope(z_t, z_B, 0, 2, 16, tb_t, tb_B, 0, [(k_t[:, 0:128], lambda a: a)], k_B)
                    rope(z_t, z_B, 256, 2, 16, tb_t, tb_B, 128, [(k_t[:, 128:256], lambda a: a)], k_B)
                    K.op("act", lambda e: e.activation(out=va[:, t, :, 0:64],
                                                       in_=z_t[:, 128:256].rearrange("p (h d) -> p h d", h=2), func=AF.Copy),
                         reads=[z_B], writes=[vaB])
                    K.op("act", lambda e: e.activation(out=vd[:, t, :, 0:64],
                                                       in_=z_t[:, 384:512].rearrange("p (h d) -> p h d", h=2), func=AF.Copy),
                         reads=[z_B], writes=[vdB])
                    S1[t] = (k_t, k_B)

                def kv_e(t):
                    k_t, k_B = S1.pop(t)
                    transpose_multi([(k_t[:, 0:128], k_B), (k_t[:, 128:256], k_B)],
                                    [(kTa[:, t * 128:(t + 1) * 128], kTaB, 0, 1, "act"),
                                     (kTd[:, t * 128:(t + 1) * 128], kTdB, 1, 1, "act")])

                run_pipe(NT_ALL, [(0, kv_a), (1, kv_cast), (1, kv_stats), (2, kv_b2), (3, kv_c), (4, kv_d), (5, kv_e)])

                def make_qproj(grp_):
                    qa_t_, qa_B_ = qTa[grp_ % 2]
                    qd_t_, qd_B_ = qTd[grp_ % 2]
                    gad_, gadB_ = gads[grp_ % 2]
                    S2 = {}

                    def st0(qt):
                        ti = grp_ * 4 + qt
                        x_t, x_B = load_x(x_src_own[ti * 128:(ti + 1) * 128, :], xoB)
                        tb_t, tb_B = rot(tabt)
                        K.dma(tb_t[:], io["tabQ"][ti * 128:(ti + 1) * 128, :], tb_B, writes=[tb_B], qn="act")
                        b_t, b_B, s_t, s_B = prep_a(x_t, x_B, eng="dve")
                        S2[qt] = (tb_t, tb_B, b_t, b_B, s_t, s_B)

                    def st1(qt):
                        tb_t, tb_B, b_t, b_B, s_t, s_B = S2[qt]
                        T_t, T_B = prep_b(b_t, b_B, eng="dve")
                        z_t, z_B = rot(zs)
                        proj(T_t, T_B, wQG, wQGB, [(0, 512), (512, 512)], z_t, z_B, s_t, s_B)
                        S2[qt] = (z_t, z_B, tb_t, tb_B)

                    def st2(qt):
                        z_t, z_B, tb_t, tb_B = S2[qt]
                        headnorm(z_t, z_B, 0, 4, vecs[:, V_QN:V_QN + 64])
                        a_t, a_B = rot(qa_tm)
                        rope(z_t, z_B, 0, 4, 16, tb_t, tb_B, 0,
                             [(a_t[:, 0:256].rearrange("p (g kv d) -> p kv g d", g=2, kv=2),
                               lambda a: a.rearrange("p (kv g d) -> p kv g d", kv=2, g=2))], a_B)
                        d_t, d_B = rot(qd_tm)
                        dsts = []
                        for c in range(2):
                            dsts.append((
                                d_t[:, c, :].rearrange("p (g kv c k) -> p kv g c k", g=2, kv=2, c=2)[:, :, :, c, :],
                                lambda a, c=c: a.rearrange("p (kv g c k) -> p kv g c k", kv=2, g=2, c=2)[:, :, :, c, :]))
                        rope(z_t, z_B, 512, 4, 16, tb_t, tb_B, 128, dsts, d_B)
                        silu_gate(z_t, z_B, 256, 256, gad_[:, qt, 0:256], gadB_)
                        silu_gate(z_t, z_B, 768, 256, gad_[:, qt, 256:512], gadB_)
                        S2[qt] = (a_t, a_B, d_t, d_B)

                    def st3(qt):
                        a_t, a_B, d_t, d_B = S2.pop(qt)
                        transpose_multi(
                            [(a_t[:, 0:128], a_B), (a_t[:, 128:256], a_B), (d_t[:, 0, 0:128], d_B), (d_t[:, 0, 128:256], d_B),
                             (d_t[:, 1, 0:128], d_B), (d_t[:, 1, 128:256], d_B)],
                            [(qa_t_[:, :, qt * 128:(qt + 1) * 128], qa_B_, 0, 2, "dve"),
                             (qd_t_[:, 0, :, qt * 128:(qt + 1) * 128], qd_B_, 2, 2, "dve"),
                             (qd_t_[:, 1, :, qt * 128:(qt + 1) * 128], qd_B_, 4, 2, "dve")])

                    stages = (st0, st1, st2, st3)

                    def boundary(b_):
                        for si in reversed(range(4)):
                            qt = b_ - si
                            if 0 <= qt < 4:
                                stages[si](qt)
                    return boundary

                nxt = make_qproj(0)
                for b_ in range(7):
                    nxt(b_)
                for grp in range(4):
                    qa_t, qa_B = qTa[grp % 2]
                    qd_t, qd_B = qTd[grp % 2]
                    gad, gadB = gads[grp % 2]
                    nxt = make_qproj(grp + 1) if grp < 3 else (lambda b_: None)
                    bcount = [0]

                    def loop_done():
                        nxt(bcount[0])
                        bcount[0] += 1

                    ma_t, ma_B = rot(mixtm)
                    for g in range(2):
                        accs = [pz[4], pz[5]]
                        accvs = [a_[0][:, 0:260].rearrange("p (a b) -> p a b", b=65) for a_ in accs]
                        slots = {}

                        def qkA(j):
                            pi = rot([0, 1], "pzpair")
                            banks = [pzs[2 * pi], pzs[2 * pi + 1]]
                            slots[j] = (pi, banks)
                            for kv in range(2):
                                s_t, s_B = banks[kv]
                                K.op("pe", lambda e: e.matmul(s_t[:], lhsT=kTa[kv * 64:(kv + 1) * 64, j * 128:(j + 1) * 128],
                                                              rhs=qa_t[kv * 64:(kv + 1) * 64, g, :], start=True, stop=True),
                                     reads=[kTaB, qa_B], writes=[s_B])

                        def exA(j):
                            pi, banks = slots[j]
                            qi = rot([0, 1, 2], "ptpair")
                            pts = [PT[2 * qi], PT[2 * qi + 1]]
                            K.op("act", lambda e: e.activation(out=PT2[qi][:], in_=pzd[pi][:], func=AF.Exp),
                                 reads=[banks[0][1], banks[1][1]], writes=[pts[0][1], pts[1][1]])
                            slots[j] = pts

                        def pvA(j):
                            pts = slots.pop(j)
                            for kv in range(2):
                                p_t, p_B = pts[kv]
                                for qt in range(4):
                                    K.op("pe", lambda e: e.matmul(accvs[kv][:, qt, :], lhsT=p_t[:, qt * 128:(qt + 1) * 128],
                                                                  rhs=va[:, j, kv, 0:65], start=(j == 0 and qt == 0),
                                                                  stop=(j == NT_ALL - 1 and qt == 3)),
                                         reads=[p_B, vaB], writes=[accs[kv][1]])

                        attn_pipe(NT_ALL, qkA, exA, pvA, 2)
                        loop_done()
                        for kv in range(2):
                            h = 2 * kv + g
                            accv, acc_B = accvs[kv], accs[kv][1]
                            r_t, r_B = rot(rd)
                            K.op("dve", lambda e: e.reciprocal(out=r_t[:, 0:4], in_=accv[:, :, 64]), reads=[acc_B], writes=[r_B])
                            f_t, f_B = rot(fin)
                            K.op("dve", lambda e: e.tensor_tensor(out=f_t[:], in0=accv[:, :, 0:64],
                                                                  in1=r_t[:, 0:4].unsqueeze(2).to_broadcast([128, 4, 64]),
                                                                  op=ALU.mult), reads=[acc_B, r_B], writes=[f_B])
                            K.op("pool", lambda e: e.tensor_tensor(out=ma_t[:, :, h * 64:(h + 1) * 64], in0=f_t[:],
                                                                   in1=gad[:, :, h * 64:(h + 1) * 64], op=ALU.mult),
                                 reads=[f_B, gadB], writes=[ma_B])
                    for qt in range(4):
                        transpose_to([ma_t[:, qt, 0:128], ma_t[:, qt, 128:256]], ma_B,
                                     mixT[:, 0:2, (grp * 4 + qt) * 128:(grp * 4 + qt + 1) * 128], mixTB)

                    md_t, md_B = rot(mixtm)
                    for half in range(2):
                        for g in range(2):
                            accs = [pz[4], pz[5]]
                            accvs = [a_[0][:, 0:260].rearrange("p (c a b) -> p c a b", c=2, b=65) for a_ in accs]
                            slots = {}

                            def qkD(j):
                                pi = rot([0, 1], "pzpair")
                                banks = [pzs[2 * pi], pzs[2 * pi + 1]]
                                slots[j] = (pi, banks)
                                for c in range(2):
                                    for kv in range(2):
                                        s_t, s_B = banks[kv]
                                        K.op("pe", lambda e: e.matmul(
                                            s_t[:, c * 256:(c + 1) * 256], lhsT=kTd[kv * 64:(kv + 1) * 64, j * 128:(j + 1) * 128],
                                            rhs=qd_t[kv * 64:(kv + 1) * 64, c, g, half * 256:(half + 1) * 256],
                                            start=True, stop=True), reads=[kTdB, qd_B], writes=[s_B])

                            def exD(j):
                                pi, banks = slots[j]
                                qi = rot([0, 1, 2], "ptpair")
                                pts = [PT[2 * qi], PT[2 * qi + 1]]
                                K.op("act", lambda e: e.activation(out=PT2[qi][:], in_=pzd[pi][:], func=AF.Exp),
                                     reads=[banks[0][1], banks[1][1]], writes=[pts[0][1], pts[1][1]])
                                slots[j] = pts

                            def pvD(j):
                                pts = slots.pop(j)
                                for kv in range(2):
                                    p_t, p_B = pts[kv]
                                    for c in range(2):
                                        for qt in range(2):
                                            K.op("pe", lambda e: e.matmul(
                                                accvs[kv][:, c, qt, :], lhsT=p_t[:, c * 256 + qt * 128:c * 256 + (qt + 1) * 128],
                                                rhs=vd[:, j, kv, 0:65], start=(j == 0 and c == 0 and qt == 0),
                                                stop=(j == NT_ALL - 1 and c == 1 and qt == 1)),
                                                reads=[p_B, vdB], writes=[accs[kv][1]])

                            attn_pipe(NT_ALL, qkD, exD, pvD, 2)
                            loop_done()
                            for kv in range(2):
                                h = 2 * kv + g
                                av, acc_B = accvs[kv], accs[kv][1]
                                r_t, r_B = rot(rd)
                                K.op("dve", lambda e: e.reciprocal(out=r_t[:, 0:4].rearrange("p (c a) -> p c a", c=2), in_=av[:, :, :, 64]),
                                     reads=[acc_B], writes=[r_B])
                                K.op("dve", lambda e: e.tensor_scalar(out=r_t[:, 2:4], in0=r_t[:, 2:4], scalar1=lamt[:, 5:6],
                                                                      scalar2=None, op0=ALU.mult), reads=[r_B, lamtB], writes=[r_B])
                                f_t, f_B = rot(fin)
                                g_t, g_B = rot(fin2)
                                K.op("dve", lambda e: e.tensor_tensor(out=f_t[:], in0=av[:, :, :, 0:64].rearrange("p c a d -> p (c a) d"),
                                                                      in1=r_t[:, 0:4].unsqueeze(2).to_broadcast([128, 4, 64]),
                                                                      op=ALU.mult), reads=[acc_B, r_B], writes=[f_B])
                                K.op("pool", lambda e: e.tensor_tensor(out=f_t[:, 0:2, :], in0=f_t[:, 0:2, :], in1=f_t[:, 2:4, :], op=ALU.add),
                                     reads=[f_B], writes=[f_B])
                                K.op("pool", lambda e: e.tensor_tensor(out=g_t[:, 0:2, :], in0=f_t[:, 0:2, :], in1=f_t[:, 0:2, :], op=ALU.mult),
                                     reads=[f_B], writes=[g_B])
                                h_t, h_B = rot(hn)
                                K.op("dve", lambda e: e.tensor_reduce(out=h_t[:, 0:2], in_=g_t[:, 0:2, :], axis=AX.X, op=ALU.add),
                                     reads=[g_B], writes=[h_B])
                                K.op("act", lambda e: e.activation(out=h_t[:, 0:2], in_=h_t[:, 0:2], func=AF.Ln, scale=1.0 / 64, bias=EPS),
                                     reads=[h_B], writes=[h_B])
                                K.op("act", lambda e: e.activation(out=h_t[:, 0:2], in_=h_t[:, 0:2], func=AF.Exp, scale=-0.5),
                                     reads=[h_B], writes=[h_B])
                                K.op("dve", lambda e: e.tensor_tensor(out=f_t[:, 0:2, :], in0=f_t[:, 0:2, :],
                                                                      in1=h_t[:, 0:2].unsqueeze(2).to_broadcast([128, 2, 64]), op=ALU.mult),
                                     reads=[f_B, h_B], writes=[f_B])
                                K.op("dve", lambda e: e.tensor_tensor(out=f_t[:, 0:2, :], in0=f_t[:, 0:2, :],
                                                                      in1=subg[:].unsqueeze(1).to_broadcast([128, 2, 64]), op=ALU.mult),
                                     reads=[f_B, subgB], writes=[f_B])
                                K.op("pool", lambda e: e.tensor_tensor(
                                    out=md_t[:, half * 2:half * 2 + 2, h * 64:(h + 1) * 64], in0=f_t[:, 0:2, :],
                                    in1=gad[:, half * 2:half * 2 + 2, 256 + h * 64:256 + (h + 1) * 64], op=ALU.mult),
                                    reads=[f_B, gadB], writes=[md_B])
                    for qt in range(4):
                        transpose_to([md_t[:, qt, 0:128], md_t[:, qt, 128:256]], md_B,
                                     mixT[:, 6:8, (grp * 4 + qt) * 128:(grp * 4 + qt + 1) * 128], mixTB)
                    loop_done()
                K.barrier()

            with ExitStack() as ph:
              if '3' in PH:
                wo, woB = sb("wo", [128, 8, 1024], BF16, ph)
                fg, fgB = sb("fg", [128, 1024], F32, ph)
                xn = [sb("xn%d" % i, [128, 1024], F32, ph) for i in range(2)]
                yo = [sb("yo%d" % i, [128, 1024], F32, ph) for i in range(2)]
                yb = [sb("yb%d" % i, [128, 1024], BF16, ph) for i in range(4)]
                K.dma(fg[:], io["fg"][:, :], fgB, writes=[fgB])
                for c in range(8):
                    w_t, w_B = rot(xs)
                    K.dma(w_t[:], io["w_out"][l, c * 128:(c + 1) * 128, :], w_B, writes=[w_B])
                    if c % 2:
                        K.op("act", lambda e, c=c: e.activation(out=wo[:, c, :], in_=w_t[:], func=AF.Copy), reads=[w_B], writes=[woB])
                    else:
                        K.op("dve", lambda e, c=c: e.tensor_copy(out=wo[:, c, :], in_=w_t[:]), reads=[w_B], writes=[woB])
                for ti in range(NT_OWN):
                    x_t, x_B = load_x(x_src_own[ti * 128:(ti + 1) * 128, :], xoB)
                    n_t, n_B = rot(xn)
                    for n in range(2):
                        p_t, p_B = rot(pzs, "pzproj")
                        for c in range(8):
                            K.op("pe", lambda e, c=c, n=n: e.matmul(p_t[:], lhsT=mixT[:, c, ti * 128:(ti + 1) * 128],
                                                                    rhs=wo[:, c, n * 512:(n + 1) * 512],
                                                                    start=(c == 0), stop=(c == 7)),
                                 reads=[mixTB, woB], writes=[p_B])
                        K.op("dve", lambda e, n=n: e.tensor_tensor(out=n_t[:, n * 512:(n + 1) * 512], in0=p_t[:],
                                                                   in1=x_t[:, n * 512:(n + 1) * 512], op=ALU.add),
                             reads=[p_B, x_B], writes=[n_B])
                    if not fused:
                        K.dma(io["xnext"][ti * 128:(ti + 1) * 128, :], n_t[:], n_B, reads=[n_B])
                    elif l < L - 1:
                        K.dma(io["x1own"][ti * 128:(ti + 1) * 128, :], n_t[:], n_B, reads=[n_B], writes=[x1ownB])
                        for m in range(4):
                            y_t, y_B = rot(yb)
                            if m % 2:
                                K.op("act", lambda e: e.activation(out=y_t[:], in_=n_t[:], func=AF.Copy, scale=rsel[:, m:m + 1]),
                                     reads=[n_B, rselB], writes=[y_B])
                            else:
                                K.op("dve", lambda e: e.tensor_scalar(
                                    out=y_t[:], in0=n_t[:], scalar1=rsel[:, m:m + 1], scalar2=None, op0=ALU.mult),
                                    reads=[n_B, rselB], writes=[y_B])
                            K.dma(io["xg_src"][m * TOWN + ti * 128:m * TOWN + (ti + 1) * 128, :], y_t[:], y_B,
                                  reads=[y_B], writes=[xgsB[(m * TOWN + ti * 128) // 1024]], qn=("act" if m % 2 == 0 else "pool"))
                        if ti % 8 == 7:
                            for m in range(4):
                                ch = m * 2 + ti // 8
                                K.collective("AllReduce", ALU.add, [[0, 1, 2, 3], [4, 5, 6, 7]],
                                             io["xg_src"][ch * 1024:(ch + 1) * 1024, :], io["xg_dst"][ch * 1024:(ch + 1) * 1024, :],
                                             xgsB[ch], xgdB[ch])
                    if l == L - 1:
                        s_t, s_B = rot(st1)
                        stats_rstd(n_t[:], n_B, 1024, s_t, s_B)
                        y_t, y_B = rot(yo)
                        K.op("dve", lambda e: e.scalar_tensor_tensor(out=y_t[:], in0=n_t[:], scalar=s_t[:, 2:3], in1=fg[:],
                                                                     op0=ALU.mult, op1=ALU.mult),
                             reads=[n_B, s_B, fgB], writes=[y_B])
                        K.dma(io["y"][ti * 128:(ti + 1) * 128, :], y_t[:], y_B, reads=[y_B])
                K.barrier()
        K.barrier(final=True)
        stats = {k: v.cnt for k, v in K.q.items()}
        stats["nsem"] = K.nsem
    build_program.stats = stats
    return nc


def _rope_cs(pos, d):
    inv = np.power(np.float32(10000.0), -(np.arange(0, d, 2, dtype=np.float32) / np.float32(d))).astype(np.float32)
    ang = (pos.astype(np.float32)[:, None] * inv[None, :]).astype(np.float32)
    return np.cos(ang.astype(np.float64)).astype(np.float32), np.sin(ang.astype(np.float64)).astype(np.float32)


def _tab_A(pos):
    row, col = pos // 64, pos % 64
    cr, sr = _rope_cs(row, 32)
    cc, sc = _rope_cs(col, 32)
    return np.concatenate([cr, cr, cc, cc, -sr, sr, -sc, sc], axis=1)


def _tab_D(pos):
    c, s = _rope_cs(pos, 32)
    return np.concatenate([c, c, c, c, -s, s, -s, s], axis=1)


def _tab_B(pos):
    c, s = _rope_cs(pos, 64)
    return np.concatenate([c, c, -s, s], axis=1)


def _c_bias_table(rpb_l, r, i, win):
    a = np.arange(128)
    ks = (TOWN * r - 256 + 128 * np.asarray(win)[:, None] + a[None, :])
    qt = TOWN * r + 128 * i + a
    kvalid = (ks >= 0) & (ks < SEQ)
    ksc = np.clip(ks, 0, SEQ - 1)
    kr, kc = ksc // 64, ksc % 64
    qr, qc = qt // 64, qt % 64
    rs = np.clip(qr - 4, 0, 120)
    cs = np.clip(qc - 8, 0, 48)
    ok = (kvalid[:, :, None] & (kr[:, :, None] >= rs[None, None, :]) & (kr[:, :, None] < rs[None, None, :] + 8)
          & (kc[:, :, None] >= cs[None, None, :]) & (kc[:, :, None] < cs[None, None, :] + 16))
    dr = np.clip(kr[:, :, None] - qr[None, None, :] + 7, 0, 14)
    dc = np.clip(kc[:, :, None] - qc[None, None, :], -15, 15) + 15
    g = rpb_l[:, dr, dc]
    out = np.where(ok[None], g, np.float32(MASKV)).astype(np.float32)
    return np.ascontiguousarray(out.transpose(2, 1, 0, 3))


def _core_inputs(c, xfull, xown, layers, P):
    r = c % 4
    m = {"xfull": xfull, "xown": xown}
    m["w_in"] = np.ascontiguousarray(P["w_in"][layers])
    m["w_out"] = np.ascontiguousarray(P["w_out"][layers])
    m["gcol"] = np.ascontiguousarray(P["norm_g"][layers].reshape(len(layers), 8, 128).transpose(0, 2, 1))
    vec = np.zeros((len(layers), V_N), np.float32)
    for k, l in enumerate(layers):
        lam_init = np.float32(0.8 - 0.6 * np.exp(-0.3 * l))
        vec[k, V_QN:V_QN + 64] = P["qn_a"][l]
        vec[k, V_KN:V_KN + 64] = P["kn_a"][l]
        vec[k, V_SUB:V_SUB + 64] = P["subln_d"][l]
        vec[k, V_SINK:V_SINK + 4] = P["sink_b"][l]
        vec[k, V_LQ1:V_LQ1 + 32] = P["lam_q1"][l]
        vec[k, V_LK1:V_LK1 + 32] = P["lam_k1"][l]
        vec[k, V_LQ2:V_LQ2 + 32] = P["lam_q2"][l]
        vec[k, V_LK2:V_LK2 + 32] = P["lam_k2"][l]
        vec[k, V_LC] = lam_init
        vec[k, V_LC + 1] = np.float32(1.0) - lam_init
    m["vecs"] = np.ascontiguousarray(np.broadcast_to(vec[:, None, :], (len(layers), 128, V_N)))
    m["fg"] = np.ascontiguousarray(np.broadcast_to(P["final_g"][None, :], (128, D_MODEL)))
    pos_all = np.arange(SEQ)
    m["tabKV"] = np.ascontiguousarray(np.concatenate([_tab_A(pos_all), _tab_D(pos_all)], axis=1))
    pos_own = TOWN * r + np.arange(TOWN)
    m["tabQ"] = np.ascontiguousarray(np.concatenate(
        [_tab_A(pos_own) * np.float32(0.125), _tab_D(pos_own) * np.float32(32 ** -0.5)], axis=1).astype(np.float32))
    pos_loc = np.clip(TOWN * r - 256 + np.arange(NT_LOC * 128), 0, SEQ - 1)
    tb = _tab_B(pos_loc)
    m["tabB"] = np.ascontiguousarray(np.concatenate([tb, tb * np.float32(0.125)], axis=1).astype(np.float32))
    cbi = np.zeros((len(layers), 128, 5 * 512), np.float32)
    cbsp = np.full((len(layers), 4, 128, 6 * 512), MASKV, np.float32)
    for k, l in enumerate(layers):
        rpb = P["rpb_c"][l]
        cbi[k] = _c_bias_table(rpb, r, 5, list(range(5, 10))).reshape(128, -1)
        for si, (i, win) in enumerate([(0, list(range(0, 6))), (1, list(range(1, 6))),
                                       (14, list(range(14, 19))), (15, list(range(14, 20)))]):
            t = _c_bias_table(rpb, r, i, win).reshape(128, -1)
            cbsp[k, si, :, :t.shape[1]] = t
    m["cb_int"] = cbi
    m["cb_sp"] = cbsp
    a = np.arange(128)
    mprev = (a[None, :] <= a[:, None]).astype(np.float32)
    mnext = (a[:, None] <= a[None, :]).astype(np.float32)
    z = np.zeros_like(mprev)
    m["bmask"] = np.ascontiguousarray(np.concatenate(
        [mprev if r > 0 else z, mprev, mnext, mnext if r < 3 else z], axis=1))
    sel = np.zeros((6,), np.float32)
    if r >= 1:
        sel[r - 1] = 1.0
    if r <= 2:
        sel[3 + r] = 1.0
    m["halosel"] = np.ascontiguousarray(np.broadcast_to(sel[None, :], (128, 6)))
    return m


_NC_CACHE = {}


def _get_nc(n_layers, fused):
    key = (n_layers, fused)
    if key not in _NC_CACHE:
        _NC_CACHE[key] = build_program(n_layers, fused)
    return _NC_CACHE[key]


def kernel(x, norm_g, w_in, w_out, qn_a, kn_a, sink_b, rpb_c, lam_q1, lam_k1, lam_q2, lam_k2, subln_d, final_g):
    P = dict(norm_g=norm_g, w_in=w_in, w_out=w_out, qn_a=qn_a, kn_a=kn_a, sink_b=sink_b, rpb_c=rpb_c,
             lam_q1=lam_q1, lam_k1=lam_k1, lam_q2=lam_q2, lam_k2=lam_k2, subln_d=subln_d, final_g=final_g)
    P = {k: np.asarray(v, dtype=np.float32) for k, v in P.items()}
    x = np.asarray(x, dtype=np.float32)
    nc = _get_nc(DEPTH, True)
    in_maps = []
    for c in range(NCORES):
        b, r = c // 4, c % 4
        m = _core_inputs(c, np.ascontiguousarray(x[b]), np.ascontiguousarray(x[b, r * TOWN:(r + 1) * TOWN]),
                         list(range(DEPTH)), P)
        rs = np.zeros((4,), np.float32)
        rs[r] = 1.0
        m["rsel"] = np.ascontiguousarray(np.broadcast_to(rs[None, :], (128, 4)))
        in_maps.append(m)
    res = run_bass_kernel_spmd(nc, in_maps, core_ids=list(range(NCORES)))
    y = np.empty_like(x)
    for c in range(NCORES):
        b, r = c // 4, c % 4
        y[b, r * TOWN:(r + 1) * TOWN] = res.results[c]["y"]
    return y
```

```python
import os
import numpy as np
from contextlib import ExitStack
import concourse.bass as bass
import concourse.mybir as mybir
from concourse.bass_utils import run_bass_kernel_spmd

F32 = mybir.dt.float32
GDT = mybir.dt.bfloat16
BF16 = mybir.dt.bfloat16
AF = mybir.ActivationFunctionType
ALU = mybir.AluOpType
AX = mybir.AxisListType

D_MODEL = 1024
SEQ = 8192
BATCH = 2
DEPTH = 2
D_IN = 3328
NCORES = 8
TOWN = 2048
NT_OWN = 16
NT_ALL = 64
NT_LOC = 20
EPS = 1e-6
MASKV = -30000.0

V_QN, V_KN, V_SUB, V_SINK, V_LQ1, V_LK1, V_LQ2, V_LK2, V_LC = 0, 64, 128, 192, 196, 228, 260, 292, 324
V_N = 328


class Buf:
    __slots__ = ("w", "r", "dsem", "dcnt", "name")
    ALL = []

    def __init__(self, name=""):
        self.w = None
        self.r = {}
        self.dsem = None
        self.dcnt = 0
        self.name = name
        Buf.ALL.append(self)


class Q:
    def __init__(self, name, eng, sem):
        self.name, self.eng, self.sem = name, eng, sem
        self.cnt = 0
        self.seen = {}


class Ctx:
    def __init__(self, nc, st):
        self.nc, self.st = nc, st
        self.q = {}
        for name, eng in (("pe", nc.tensor), ("act", nc.scalar), ("dve", nc.vector),
                          ("pool", nc.gpsimd), ("sp", nc.sync)):
            self.q[name] = Q(name, eng, st.enter_context(nc.semaphore("q_" + name)))
        self.dbufs = []
        self.nsem = 5

    def _wait(self, q, deps):
        for key, (s, v) in deps.items():
            if q.seen.get(key, 0) >= v:
                continue
            q.eng.wait_ge(s, v)
            q.seen[key] = v

    def _deps(self, q, reads, writes):
        deps = {}

        def add(tok, raw):
            if tok is None:
                return
            s, v = tok
            if s is q.sem and (q.name == "pe" or not raw):
                return
            k = id(s)
            if k not in deps or deps[k][1] < v:
                deps[k] = (s, v)
        for b in reads:
            add(b.w, True)
        for b in writes:
            add(b.w, True)
            for tok in b.r.values():
                add(tok, True)
        return deps

    def op(self, qn, fn, reads=(), writes=()):
        q = self.q[qn]
        self._wait(q, self._deps(q, reads, writes))
        ins = fn(q.eng)
        q.cnt += 1
        ins.then_inc(q.sem, 1)
        tok = (q.sem, q.cnt)
        for b in reads:
            b.r[id(q.sem)] = tok
        for b in writes:
            b.w = tok
            b.r = {}
        return ins

    def dma(self, out, in_, sb, reads=(), writes=(), qn="sp"):
        q = self.q[qn]
        if sb.dsem is None:
            sb.dsem = self.st.enter_context(self.nc.semaphore("d_%d" % self.nsem))
            self.nsem += 1
            self.dbufs.append(sb)
        self._wait(q, self._deps(q, reads, writes))
        ins = q.eng.dma_start(out=out, in_=in_)
        sb.dcnt += 16
        ins.then_inc(sb.dsem, 16)
        tok = (sb.dsem, sb.dcnt)
        for b in reads:
            b.r[id(sb.dsem)] = tok
        for b in writes:
            b.w = tok
            b.r = {}

    def collective(self, kind, op, groups, in_ap, out_ap, in_B, out_B):
        q = self.q["pool"]
        self._wait(q, self._deps(q, [in_B], [out_B]))
        sem = self.st.enter_context(self.nc.semaphore("cc_%d" % self.nsem))
        self.nsem += 1
        ins = q.eng.collective_compute(kind, op, replica_groups=groups, ins=[in_ap], outs=[out_ap])
        ins.then_inc(sem)
        tok = (sem, 1)
        in_B.r[id(sem)] = tok
        out_B.w = tok
        out_B.r = {}
        self.ccs = getattr(self, "ccs", []) + [tok]

    def renew(self):
        self.barrier()
        ccsems = set(id(cs) for (cs, cv) in getattr(self, "ccs", []))
        for b in Buf.ALL:
            if b.w is not None and id(b.w[0]) in ccsems:
                b.r = {}
                continue
            b.w = None
            b.r = {}
        for name, qq in self.q.items():
            qq.sem = self.st.enter_context(self.nc.semaphore("q%d_%s" % (self.nsem, name)))
            self.nsem += 1
            qq.cnt = 0
            qq.seen = {}
        for b in self.dbufs:
            pass

    def barrier(self, final=False):
        sp = self.q["sp"]
        deps = {}
        for b in self.dbufs:
            if b.dcnt:
                deps[id(b.dsem)] = (b.dsem, b.dcnt)
        if final:
            for (cs, cv) in getattr(self, "ccs", []):
                deps[id(cs)] = (cs, cv)
        for qq in self.q.values():
            if qq is not sp and qq.cnt:
                deps[id(qq.sem)] = (qq.sem, qq.cnt)
        self._wait(sp, deps)
        self.op("sp", lambda e: e.nop())
        for qq in self.q.values():
            if qq is sp:
                continue
            d = {id(sp.sem): (sp.sem, sp.cnt)}
            self._wait(qq, d)


def build_program(n_layers, fused):
    PH = os.environ.get('KPH', '123abBC')
    nc = bass.Bass("TRN2", target_bir_lowering=False)
    L = n_layers

    def din(name, shape, dt=F32):
        return nc.dram_tensor(name, shape, dt, kind="ExternalInput").ap()

    io = dict(
        xfull=din("xfull", [SEQ, D_MODEL]),
        xown=din("xown", [TOWN, D_MODEL]),
        w_in=din("w_in", [L, D_MODEL, D_IN]),
        w_out=din("w_out", [L, D_MODEL, D_MODEL]),
        gcol=din("gcol", [L, 128, 8]),
        vecs=din("vecs", [L, 128, V_N]),
        fg=din("fg", [128, D_MODEL]),
        tabKV=din("tabKV", [SEQ, 256]),
        tabQ=din("tabQ", [TOWN, 256]),
        tabB=din("tabB", [NT_LOC * 128, 256]),
        cb_int=din("cb_int", [L, 128, 5 * 512]),
        cb_sp=din("cb_sp", [L, 4, 128, 6 * 512]),
        bmask=din("bmask", [128, 512]),
        halosel=din("halosel", [128, 6]),
        y=nc.dram_tensor("y", [TOWN, D_MODEL], F32, kind="ExternalOutput").ap(),
    )
    if fused:
        io["rsel"] = din("rsel", [128, 4])
        io["x1own"] = nc.dram_tensor("x1own", [TOWN, D_MODEL], F32, kind="Internal").ap()
        io["xg_src"] = nc.dram_tensor("xg_src", [SEQ, D_MODEL], BF16, kind="Internal").ap()
        io["xg_dst"] = nc.dram_tensor("xg_dst", [SEQ, D_MODEL], BF16, kind="Internal").ap()
    else:
        io["xnext"] = nc.dram_tensor("xnext", [TOWN, D_MODEL], F32, kind="ExternalOutput").ap()
    x1ownB = Buf("x1own")
    xgsB = [Buf("xg_src%d" % i) for i in range(8)]
    xgdB = [Buf("xg_dst%d" % i) for i in range(8)]

    with ExitStack() as st:
        K = Ctx(nc, st)
        E = st.enter_context

        uniq = [0]

        def sb(name, shape, dt, stack=None):
            uniq[0] += 1
            t = (stack or st).enter_context(nc.sbuf_tensor("s%d_%s" % (uniq[0], name), shape, dt))
            return t, Buf(name)

        ident, identB = sb("ident", [128, 128], BF16)
        identf, identfB = sb("identf", [128, 128], F32)
        pT = [(E(nc.psum_tensor("pT%d" % i, [128, 1024], BF16)), Buf("pT%d" % i)) for i in range(2)]
        pzd = [E(nc.psum_tensor("pzd%d" % i, [128, 1024], F32)) for i in range(2)]
        pz = [(pzd[i // 2][:, (i % 2) * 512:(i % 2 + 1) * 512], Buf("pz%d" % i)) for i in range(4)]
        pz += [(E(nc.psum_tensor("pz%d" % i, [128, 512], F32)), Buf("pz%d" % i)) for i in (4, 5)]
        pzs = pz[:4]
        xs = [sb("xs%d" % i, [128, 1024], F32) for i in range(3)]
        xb = [sb("xb%d" % i, [128, 1024], BF16) for i in range(2)]
        xT = [sb("xT%d" % i, [128, 8, 128], BF16) for i in range(3)]
        junk, junkB = sb("junk", [128, 1024], BF16)
        st1 = [sb("st1_%d" % i, [128, 8], F32) for i in range(4)]
        hn = [sb("hn%d" % i, [128, 8], F32) for i in range(3)]
        t1 = [sb("t1_%d" % i, [128, 256], F32) for i in range(2)]
        t2 = [sb("t2_%d" % i, [128, 256], F32) for i in range(2)]
        t3 = [sb("t3_%d" % i, [128, 256], F32) for i in range(2)]
        tabt = [sb("tabt%d" % i, [128, 256], F32) for i in range(5)]
        PT2 = [sb("PT%d" % i, [128, 1024], BF16)[0] for i in range(3)]
        PT = [(PT2[i // 2][:, (i % 2) * 512:(i % 2 + 1) * 512], Buf("PT%d" % i)) for i in range(6)]
        mixT, mixTB = sb("mixT", [128, 8, TOWN], BF16)
        mixtm = [sb("mixtm%d" % i, [128, 4, 256], BF16) for i in range(2)]
        vecs, vecsB = sb("vecs", [128, V_N], F32)
        gcol, gcolB = sb("gcol", [128, 8], F32)
        lamt, lamtB = sb("lamt", [128, 8], F32)
        esink, esinkB = sb("esink", [128, 4], F32)
        subg, subgB = sb("subg", [128, 64], F32)
        hsel, hselB = sb("hsel", [128, 6], F32)
        bmask, bmaskB = sb("bmask", [128, 4, 128], F32)
        fin = [sb("fin%d" % i, [128, 4, 64], F32) for i in range(2)]
        fin2 = [sb("fin2_%d" % i, [128, 4, 64], F32) for i in range(2)]
        rd = [sb("rd%d" % i, [128, 8], F32) for i in range(2)]

        rr = {}

        def rot(lst, key=None):
            k = key or id(lst)
            i = rr.get(k, 0)
            rr[k] = i + 1
            return lst[i % len(lst)]

        mhalf, mhalfB = sb("mhalf", [128, 8], F32)
        K.op("pool", lambda e: e.memset(mhalf[:], -0.5), writes=[mhalfB])
        K.op("pool", lambda e: e.memset(identf[:], 1.0), writes=[identfB])
        K.op("pool", lambda e: e.affine_select(out=identf[:], in_=identf[:], pattern=[[-1, 128]],
                                               compare_op=ALU.is_equal, fill=0.0, base=0,
                                               channel_multiplier=1), reads=[identfB], writes=[identfB])
        K.op("dve", lambda e: e.tensor_copy(out=ident[:], in_=identf[:]), reads=[identfB], writes=[identB])
        K.dma(bmask[:].rearrange("p a b -> p (a b)"), io["bmask"][:, :], bmaskB, writes=[bmaskB])
        K.dma(hsel[:], io["halosel"][:, :], hselB, writes=[hselB])
        if fused:
            rsel, rselB = sb("rsel", [128, 4], F32)
            K.dma(rsel[:], io["rsel"][:, :], rselB, writes=[rselB])

        def stats_rstd(x_t, x_B, width, s_t, s_B):
            K.op("dve", lambda e: e.scalar_tensor_tensor(out=junk[:, 0:width], in0=x_t, scalar=1.0, in1=x_t,
                                                         op0=ALU.mult, op1=ALU.mult, accum_out=s_t[:, 0:1]),
                 reads=[x_B], writes=[junkB, s_B])
            K.op("pool", lambda e: e.tensor_scalar(out=s_t[:, 1:2], in0=s_t[:, 0:1], scalar1=1.0 / width, scalar2=EPS,
                                                   op0=ALU.mult, op1=ALU.add), reads=[s_B], writes=[s_B])
            K.op("pool", lambda e: e.tensor_tensor(out=s_t[:, 2:3], in0=s_t[:, 1:2], in1=mhalf[:, 0:1], op=ALU.pow),
                 reads=[s_B, mhalfB], writes=[s_B])

        def prep_a(x_t, x_B):
            s_t, s_B = rot(st1)
            stats_rstd(x_t[:], x_B, 1024, s_t, s_B)
            b_t, b_B = rot(xb)
            K.op("act", lambda e: e.activation(out=b_t[:], in_=x_t[:], func=AF.Copy), reads=[x_B], writes=[b_B])
            return b_t, b_B, s_t, s_B

        def prep_b(b_t, b_B):
            p_t, p_B = rot(pT)
            for k in range(8):
                K.op("pe", lambda e, k=k: e.transpose(out=p_t[:, k * 128:(k + 1) * 128],
                                                      in_=b_t[:, k * 128:(k + 1) * 128], identity=ident[:]),
                     reads=[b_B, identB], writes=[p_B])
            T_t, T_B = rot(xT)
            K.op("act", lambda e: e.activation(out=T_t[:].rearrange("p c t -> p (c t)"), in_=p_t[:], func=AF.Copy),
                 reads=[p_B], writes=[T_B])
            return T_t, T_B

        def prep_cast(x_t, x_B):
            b_t, b_B = rot(xb)
            K.op("act", lambda e: e.activation(out=b_t[:], in_=x_t[:], func=AF.Copy), reads=[x_B], writes=[b_B])
            return b_t, b_B

        def prep_stats(x_t, x_B):
            s_t, s_B = rot(st1)
            stats_rstd(x_t[:], x_B, 1024, s_t, s_B)
            return s_t, s_B

        def run_pipe(n, order, lo=0):
            md = max(d for d, _ in order)
            for step in range(n + md):
                for d, f in order:
                    idx = step - d
                    if 0 <= idx < n:
                        f(lo + idx)

        def prep_tile(x_t, x_B):
            b_t, b_B, s_t, s_B = prep_a(x_t, x_B)
            T_t, T_B = prep_b(b_t, b_B)
            return T_t, T_B, s_t, s_B

        def proj(T_t, T_B, w_t, w_B, groups, z_t, z_B, s_t, s_B, evac="dve"):
            for (c0, n) in groups:
                p_t, p_B = rot(pzs, "pzproj")
                for k in range(8):
                    K.op("pe", lambda e, k=k: e.matmul(p_t[:, 0:n], lhsT=T_t[:, k, :], rhs=w_t[:, k, c0:c0 + n],
                                                       start=(k == 0), stop=(k == 7)),
                         reads=[T_B, w_B], writes=[p_B])
                if evac == "act":
                    K.op("act", lambda e: e.activation(out=z_t[:, c0:c0 + n], in_=p_t[:, 0:n], func=AF.Copy, scale=s_t[:, 2:3]),
                         reads=[p_B, s_B], writes=[z_B])
                else:
                    K.op("dve", lambda e: e.tensor_scalar(out=z_t[:, c0:c0 + n], in0=p_t[:, 0:n], scalar1=s_t[:, 2:3],
                                                          scalar2=None, op0=ALU.mult),
                         reads=[p_B, s_B], writes=[z_B])

        def headnorm(z_t, z_B, c0, H, gain_ap):
            a_t, a_B = rot(t1)
            h_t, h_B = rot(hn)
            src = z_t[:, c0:c0 + 64 * H]
            src3 = src.rearrange("p (h d) -> p h d", h=H)
            for hh_ in range(H):
                K.op("dve", lambda e: e.scalar_tensor_tensor(
                    out=a_t[:, hh_ * 64:(hh_ + 1) * 64], in0=src[:, hh_ * 64:(hh_ + 1) * 64], scalar=1.0,
                    in1=src[:, hh_ * 64:(hh_ + 1) * 64], op0=ALU.mult, op1=ALU.mult, accum_out=h_t[:, hh_:hh_ + 1]),
                    reads=[z_B], writes=([a_B, h_B] if hh_ in (0, H - 1) else []))
            K.op("pool", lambda e: e.tensor_scalar(out=h_t[:, 0:H], in0=h_t[:, 0:H], scalar1=1.0 / 64, scalar2=EPS,
                                                   op0=ALU.mult, op1=ALU.add), reads=[h_B], writes=[h_B])
            K.op("pool", lambda e: e.tensor_tensor(out=h_t[:, 0:H], in0=h_t[:, 0:H], in1=mhalf[:, 0:H], op=ALU.pow),
                 reads=[h_B, mhalfB], writes=[h_B])
            K.op("dve", lambda e: e.tensor_tensor(out=src3, in0=src3,
                                                  in1=h_t[:, 0:H].unsqueeze(2).to_broadcast([128, H, 64]), op=ALU.mult),
                 reads=[z_B, h_B], writes=[z_B])
            K.op("dve", lambda e: e.tensor_tensor(out=src3, in0=src3,
                                                  in1=gain_ap.unsqueeze(1).to_broadcast([128, H, 64]), op=ALU.mult),
                 reads=[z_B, vecsB], writes=[z_B])

        def rope(z_t, z_B, c0, H, hs, tab_t, tab_B, tc0, dsts, dst_B):
            nseg = 32 // hs
            a_t, a_B = rot(t2)
            b_t, b_B = rot(t3)
            src = z_t[:, c0:c0 + 64 * H]
            src3 = src.rearrange("p (h d) -> p h d", h=H)
            cc = tab_t[:, tc0:tc0 + 64]
            ss = tab_t[:, tc0 + 64:tc0 + 128]
            K.op("dve", lambda e: e.tensor_tensor(out=a_t[:, 0:64 * H].rearrange("p (h d) -> p h d", h=H), in0=src3,
                                                  in1=cc.unsqueeze(1).to_broadcast([128, H, 64]), op=ALU.mult),
                 reads=[z_B, tab_B], writes=[a_B])
            s5 = src.rearrange("p (h s two k) -> p h s two k", h=H, s=nseg, two=2)
            b5 = b_t[:, 0:64 * H].rearrange("p (h s two k) -> p h s two k", h=H, s=nseg, two=2)
            ss4 = ss.rearrange("p (s two k) -> p s two k", s=nseg, two=2)
            for (o_, i_) in ((0, 1), (1, 0)):
                K.op("pool", lambda e, o_=o_, i_=i_: e.tensor_tensor(
                    out=b5[:, :, :, o_, :], in0=s5[:, :, :, i_, :],
                    in1=ss4[:, :, o_, :].unsqueeze(1).to_broadcast([128, H, nseg, hs]), op=ALU.mult),
                    reads=[z_B, tab_B], writes=[b_B])
            for (dst_ap, sel) in dsts:
                K.op("dve", lambda e, dst_ap=dst_ap, sel=sel: e.tensor_tensor(
                    out=dst_ap, in0=sel(a_t[:, 0:64 * H]), in1=sel(b_t[:, 0:64 * H]), op=ALU.add),
                    reads=[a_B, b_B], writes=[dst_B])

        def silu_gate(z_t, z_B, c0, n, dst_ap, dst_B):
            a_t, a_B = rot(t1)
            K.op("act", lambda e: e.activation(out=a_t[:, 0:n], in_=z_t[:, c0:c0 + n], func=AF.Exp, scale=-1.0),
                 reads=[z_B], writes=[a_B])
            K.op("act", lambda e: e.activation(out=a_t[:, 0:n], in_=a_t[:, 0:n], func=AF.Ln, scale=1.0, bias=1.0),
                 reads=[a_B], writes=[a_B])
            K.op("act", lambda e: e.activation(out=a_t[:, 0:n], in_=a_t[:, 0:n], func=AF.Exp, scale=-1.0),
                 reads=[a_B], writes=[a_B])
            K.op("dve", lambda e: e.tensor_tensor(out=dst_ap, in0=z_t[:, c0:c0 + n], in1=a_t[:, 0:n], op=ALU.mult),
                 reads=[a_B, z_B], writes=[dst_B])

        def transpose_multi(srcs, dsts):
            p_t, p_B = rot(pT)
            for i, (s_ap, s_B) in enumerate(srcs):
                K.op("pe", lambda e: e.transpose(out=p_t[:, i * 128:(i + 1) * 128], in_=s_ap, identity=ident[:]),
                     reads=[s_B, identB], writes=[p_B])
            for (dst_ap, dst_B, b0, nb_, eng) in dsts:
                src = p_t[:, b0 * 128:(b0 + nb_) * 128]
                if len(dst_ap.shape) == 3:
                    src = src.rearrange("p (a b) -> p a b", a=dst_ap.shape[1])
                if eng == "act":
                    K.op("act", lambda e: e.activation(out=dst_ap, in_=src, func=AF.Copy), reads=[p_B], writes=[dst_B])
                else:
                    K.op("dve", lambda e: e.tensor_copy(out=dst_ap, in_=src), reads=[p_B], writes=[dst_B])

        def transpose_to(src_aps, src_B, dst_ap, dst_B, eng="dve"):
            p_t, p_B = rot(pT)
            n = len(src_aps)
            for i, s_ap in enumerate(src_aps):
                K.op("pe", lambda e, i=i, s_ap=s_ap: e.transpose(out=p_t[:, i * 128:(i + 1) * 128], in_=s_ap,
                                                                 identity=ident[:]),
                     reads=[src_B, identB], writes=[p_B])
            src = p_t[:, 0:n * 128]
            if len(dst_ap.shape) == 3:
                src = src.rearrange("p (a b) -> p a b", a=dst_ap.shape[1])
            if eng == "act":
                K.op("act", lambda e: e.activation(out=dst_ap, in_=src, func=AF.Copy), reads=[p_B], writes=[dst_B])
            else:
                K.op("dve", lambda e: e.tensor_copy(out=dst_ap, in_=src), reads=[p_B], writes=[dst_B])

        def load_weights_chunk(l, c, pieces):
            groups, cur, used = [], [], 0
            for pc in pieces:
                if used + pc[1] > 1024:
                    groups.append(cur)
                    cur, used = [], 0
                cur.append(pc)
                used += pc[1]
            groups.append(cur)
            i = 0
            for grp_ in groups:
                w_t, w_B = rot(xs)
                off = 0
                offs = []
                for (c0, n, dst_ap, dst_B) in grp_:
                    K.dma(w_t[:, off:off + n], io["w_in"][l, c * 128:(c + 1) * 128, c0:c0 + n], w_B, writes=[w_B])
                    offs.append(off)
                    off += n
                for k_, (c0, n, dst_ap, dst_B) in enumerate(grp_):
                    o_ = offs[k_]
                    if (i + c) % 2 == 0:
                        K.op("act", lambda e: e.activation(out=dst_ap, in_=w_t[:, o_:o_ + n], func=AF.Copy, scale=gcol[:, c:c + 1]),
                             reads=[w_B, gcolB], writes=[dst_B])
                    else:
                        K.op("dve", lambda e: e.tensor_scalar(
                            out=dst_ap, in0=w_t[:, o_:o_ + n], scalar1=gcol[:, c:c + 1], scalar2=None, op0=ALU.mult),
                            reads=[w_B, gcolB], writes=[dst_B])
                    i += 1

        def load_x(src_ap, extra=()):
            x_t, x_B = rot(xs)
            K.dma(x_t[:], src_ap, x_B, reads=list(extra), writes=[x_B])
            return x_t, x_B

        def attn_pipe(n, qk, ex, pv, la, mids=None):
            for idx in range(n + la):
                if idx < n:
                    qk(idx)
                    ex(idx)
                if idx >= la:
                    pv(idx - la)
                if mids and idx in mids:
                    mids[idx]()

        for l in range(L):
            if l > 0:
                K.renew()
            x_src_full = io["xfull"] if l == 0 else io["xg_dst"]
            x_src_own = io["xown"] if l == 0 else io["x1own"]
            xfB = (lambda row: []) if l == 0 else (lambda row: [xgdB[row // 1024]])
            xoB = [] if l == 0 else [x1ownB]
            K.dma(vecs[:], io["vecs"][l, :, :], vecsB, writes=[vecsB])
            K.dma(gcol[:], io["gcol"][l, :, :], gcolB, writes=[gcolB])
            K.op("act", lambda e: e.activation(out=esink[:], in_=vecs[:, V_SINK:V_SINK + 4], func=AF.Exp),
                 reads=[vecsB], writes=[esinkB])
            K.op("dve", lambda e: e.tensor_tensor(out=junk[:, 0:32], in0=vecs[:, V_LQ1:V_LQ1 + 32],
                                                  in1=vecs[:, V_LK1:V_LK1 + 32], op=ALU.mult),
                 reads=[vecsB], writes=[junkB])
            K.op("dve", lambda e: e.tensor_reduce(out=lamt[:, 0:1], in_=junk[:, 0:32], axis=AX.X, op=ALU.add),
                 reads=[junkB], writes=[lamtB])
            K.op("dve", lambda e: e.tensor_tensor(out=junk[:, 32:64], in0=vecs[:, V_LQ2:V_LQ2 + 32],
                                                  in1=vecs[:, V_LK2:V_LK2 + 32], op=ALU.mult),
                 reads=[vecsB], writes=[junkB])
            K.op("dve", lambda e: e.tensor_reduce(out=lamt[:, 1:2], in_=junk[:, 32:64], axis=AX.X, op=ALU.add),
                 reads=[junkB], writes=[lamtB])
            K.op("act", lambda e: e.activation(out=lamt[:, 2:4], in_=lamt[:, 0:2], func=AF.Exp),
                 reads=[lamtB], writes=[lamtB])
            K.op("dve", lambda e: e.tensor_tensor(out=lamt[:, 4:5], in0=lamt[:, 2:3], in1=lamt[:, 3:4], op=ALU.subtract),
                 reads=[lamtB], writes=[lamtB])
            K.op("dve", lambda e: e.tensor_tensor(out=lamt[:, 4:5], in0=lamt[:, 4:5], in1=vecs[:, V_LC:V_LC + 1], op=ALU.add),
                 reads=[lamtB, vecsB], writes=[lamtB])
            K.op("dve", lambda e: e.tensor_scalar(out=lamt[:, 5:6], in0=lamt[:, 4:5], scalar1=-1.0, scalar2=None, op0=ALU.mult),
                 reads=[lamtB], writes=[lamtB])
            K.op("dve", lambda e: e.tensor_scalar(out=subg[:], in0=vecs[:, V_SUB:V_SUB + 64], scalar1=vecs[:, V_LC + 1:V_LC + 2],
                                                  scalar2=None, op0=ALU.mult), reads=[vecsB], writes=[subgB])

            with ExitStack() as ph:
              if '1' in PH:
                wBC, wBCB = sb("wBC", [128, 8, 1792], BF16, ph)
                kTb, kTbB = sb("kTb", [128, NT_LOC * 128], BF16, ph)
                vb, vbB = sb("vb", [128, NT_LOC, 2, 66], BF16, ph)
                kTc, kTcB = sb("kTc", [128, 2, NT_LOC * 128], BF16, ph)
                vc, vcB = sb("vc", [128, NT_LOC, 4, 66], BF16, ph)
                qTb, qTbB = sb("qTb", [128, NT_OWN, 2, 128], BF16, ph)
                qTc, qTcB = sb("qTc", [128, NT_OWN, 2, 128], BF16, ph)
                gbc, gbcB = sb("gbc", [128, NT_OWN, 512], BF16, ph)
                ctp = [sb("ctp%d" % i, [128, 512], F32, ph) for i in range(3)]
                zs = [sb("zs%d" % i, [128, 1792], F32, ph) for i in range(2)]
                finp = fin + fin2
                qtm = [sb("qtm%d" % i, [128, 512], BF16, ph) for i in range(3)]
                ktm = [sb("ktm%d" % i, [128, 384], BF16, ph) for i in range(3)]
                tmpc = [sb("tmpc%d" % i, [128, 512], F32, ph) for i in range(2)]

                K.op("pool", lambda e: e.memset(vb[:].rearrange("p a b c -> p (a b c)"), 1.0), writes=[vbB])
                K.op("pool", lambda e: e.memset(vc[:].rearrange("p a b c -> p (a b c)"), 1.0), writes=[vcB])
                for c in range(8):
                    load_weights_chunk(l, c, [(768, 1024, wBC[:, c, 0:1024], wBCB), (1792, 768, wBC[:, c, 1024:1792], wBCB)])

                order1 = list(range(2, 2 + NT_OWN)) + [0, 1, 18, 19]
                SP1 = {}

                def p1_s0(idx):
                    bt = order1[idx]
                    if 2 <= bt < 2 + NT_OWN:
                        x_t, x_B = load_x(x_src_own[(bt - 2) * 128:(bt - 1) * 128, :], xoB)
                    else:
                        prev = bt < 2
                        k0 = rr.get(id(xs), 0)
                        rr[id(xs)] = k0 + 3
                        x_t, x_B = xs[k0 % 3]
                        sc0 = 0 if prev else 3
                        for m in range(3):
                            h_t, h_B = xs[(k0 + 1 + (m % 2)) % 3]
                            row0 = (m + 1) * TOWN + ((bt - 2) * 128 if prev else (bt - 18) * 128)
                            h_v = h_t[:] if l == 0 else h_t[:].bitcast(BF16)[:, 0:1024]
                            K.dma(h_v, x_src_full[row0:row0 + 128, :], h_B, reads=xfB(row0), writes=[h_B])
                            if m == 0:
                                K.op("dve", lambda e: e.tensor_scalar(out=x_t[:], in0=h_v, scalar1=hsel[:, sc0:sc0 + 1],
                                                                      scalar2=None, op0=ALU.mult),
                                     reads=[h_B, hselB], writes=[x_B])
                            else:
                                K.op("dve", lambda e: e.scalar_tensor_tensor(
                                    out=x_t[:], in0=h_v, scalar=hsel[:, sc0 + m:sc0 + m + 1], in1=x_t[:],
                                    op0=ALU.mult, op1=ALU.add), reads=[h_B, hselB, x_B], writes=[x_B])
                    tb_t, tb_B = rot(tabt)
                    K.dma(tb_t[:], io["tabB"][bt * 128:(bt + 1) * 128, :], tb_B, writes=[tb_B], qn="act")
                    SP1[bt] = [x_t, x_B, tb_t, tb_B]

                def p1_cast(idx):
                    bt = order1[idx]
                    SP1[bt] += list(prep_cast(*SP1[bt][0:2]))

                def p1_stats(idx):
                    bt = order1[idx]
                    SP1[bt] += list(prep_stats(*SP1[bt][0:2]))

                def p1_b2(idx):
                    bt = order1[idx]
                    x_t, x_B, tb_t, tb_B, b_t, b_B, s_t, s_B = SP1[bt]
                    T_t, T_B = prep_b(b_t, b_B)
                    SP1[bt] = [tb_t, tb_B, T_t, T_B, s_t, s_B]

                def p1_c(idx):
                    bt = order1[idx]
                    tb_t, tb_B, T_t, T_B, s_t, s_B = SP1[bt]
                    z_t, z_B = rot(zs)
                    proj(T_t, T_B, wBC, wBCB, [(0, 512), (512, 512), (1024, 512), (1536, 256)], z_t, z_B, s_t, s_B)
                    SP1[bt] = (z_t, z_B, tb_t, tb_B)

                def p1_s1(idx):
                    bt = order1[idx]
                    z_t, z_B, tb_t, tb_B = SP1[bt]
                    own = 2 <= bt < 2 + NT_OWN
                    i_own = bt - 2
                    q_t, q_B = rot(qtm)
                    k_t, k_B = rot(ktm)
                    rope(z_t, z_B, 256, 2, 32, tb_t, tb_B, 0,
                         [(k_t[:, 0:128], lambda a: a)], k_B)
                    K.op("pool", lambda e: e.tensor_copy(out=k_t[:, 128:384], in_=z_t[:, 1024:1280]),
                         reads=[z_B], writes=[k_B])
                    K.op("act", lambda e: e.activation(out=vb[:, bt, :, 0:64],
                                                       in_=z_t[:, 384:512].rearrange("p (h d) -> p h d", h=2), func=AF.Copy),
                         reads=[z_B], writes=[vbB])
                    K.op("act", lambda e: e.activation(out=vc[:, bt, :, 0:64],
                                                       in_=z_t[:, 1280:1536].rearrange("p (h d) -> p h d", h=4), func=AF.Copy),
                         reads=[z_B], writes=[vcB])
                    if own:
                        dst = q_t[:, 0:256].rearrange("p (g kv d) -> p kv g d", g=2, kv=2)
                        rope(z_t, z_B, 0, 4, 32, tb_t, tb_B, 128,
                             [(dst, lambda a: a.rearrange("p (kv g d) -> p kv g d", kv=2, g=2))], q_B)
                        K.op("pool", lambda e: e.tensor_copy(out=q_t[:, 256:512], in_=z_t[:, 768:1024]),
                             reads=[z_B], writes=[q_B])
                        silu_gate(z_t, z_B, 512, 256, gbc[:, i_own, 0:256], gbcB)
                        silu_gate(z_t, z_B, 1536, 256, gbc[:, i_own, 256:512], gbcB)
                    SP1[bt] = (q_t, q_B, k_t, k_B)

                def p1_s2(idx):
                    bt = order1[idx]
                    q_t, q_B, k_t, k_B = SP1.pop(bt)
                    own = 2 <= bt < 2 + NT_OWN
                    i_own = bt - 2
                    srcs = [(k_t[:, 0:128], k_B), (k_t[:, 128:256], k_B), (k_t[:, 256:384], k_B)]
                    dsts = [(kTb[:, bt * 128:(bt + 1) * 128], kTbB, 0, 1, "act"),
                            (kTc[:, 0, bt * 128:(bt + 1) * 128], kTcB, 1, 1, "act"),
                            (kTc[:, 1, bt * 128:(bt + 1) * 128], kTcB, 2, 1, "act")]
                    if own:
                        srcs += [(q_t[:, k_ * 128:(k_ + 1) * 128], q_B) for k_ in range(4)]
                        dsts += [(qTb[:, i_own, :, :].rearrange("p g q -> p (g q)"), qTbB, 3, 2, "act"),
                                 (qTc[:, i_own, :, :].rearrange("p g q -> p (g q)"), qTcB, 5, 2, "act")]
                    transpose_multi(srcs, dsts)

                def run_p1_step1(lo, hi):
                    run_pipe(hi - lo, [(1, p1_cast), (1, p1_stats), (2, p1_b2), (3, p1_c), (4, p1_s1), (5, p1_s2), (0, p1_s0)], lo)

                def p1_attn(i):
                    acc_t, acc_B = pz[4]
                    accv = acc_t[:, 0:260].rearrange("p (a b) -> p a b", b=65)
                    wins = [i + 1, i + 2, i + 3]
                    mids = [0 if i == 0 else 1, None, 3 if i == NT_OWN - 1 else 2]
                    slots = {}

                    def qkB(idx, i=i, wins=wins, slots=slots):
                        j = wins[idx]
                        banks = [rot(pzs, "pzst"), rot(pzs, "pzst")]
                        slots[idx] = banks
                        for kv in range(2):
                            s_t, s_B = banks[kv]
                            K.op("pe", lambda e: e.matmul(
                                s_t[:, 0:256], lhsT=kTb[kv * 64:(kv + 1) * 64, j * 128:(j + 1) * 128],
                                rhs=qTb[kv * 64:(kv + 1) * 64, i, :, :].rearrange("p g q -> p (g q)"),
                                start=True, stop=True), reads=[kTbB, qTbB], writes=[s_B])

                    def exB(idx, mids=mids, slots=slots):
                        banks = slots[idx]
                        p_t, p_B = rot(PT)
                        slots[idx] = (p_t, p_B)
                        for kv in range(2):
                            s_t, s_B = banks[kv]
                            K.op("act", lambda e: e.activation(out=p_t[:, kv * 256:(kv + 1) * 256], in_=s_t[:, 0:256], func=AF.Exp),
                                 reads=[s_B], writes=[p_B])
                        if mids[idx] is not None:
                            mi = mids[idx]
                            K.op("pool", lambda e: e.tensor_tensor(
                                out=p_t[:].rearrange("p (h q) -> p h q", h=4), in0=p_t[:].rearrange("p (h q) -> p h q", h=4),
                                in1=bmask[:, mi, :].unsqueeze(1).to_broadcast([128, 4, 128]), op=ALU.mult),
                                reads=[p_B, bmaskB], writes=[p_B])

                    def pvB(idx, wins=wins, slots=slots, accv=accv, acc_B=acc_B):
                        j = wins[idx]
                        p_t, p_B = slots[idx]
                        for kv in range(2):
                            for g in range(2):
                                h = 2 * kv + g
                                K.op("pe", lambda e, kv=kv, g=g, h=h: e.matmul(
                                    accv[:, h, :], lhsT=p_t[:, (kv * 2 + g) * 128:(kv * 2 + g + 1) * 128],
                                    rhs=vb[:, j, kv, 0:65], start=(idx == 0 and h == 0), stop=(idx == 2 and h == 3)),
                                    reads=[p_B, vbB], writes=[acc_B])

                    if 'B' in PH:
                        attn_pipe(3, qkB, exB, pvB, 1)
                    r_t, r_B = rot(rd)
                    K.op("dve", lambda e: e.tensor_tensor(out=r_t[:, 0:4], in0=accv[:, :, 64], in1=esink[:], op=ALU.add),
                         reads=[acc_B, esinkB], writes=[r_B])
                    K.op("dve", lambda e: e.reciprocal(out=r_t[:, 0:4], in_=r_t[:, 0:4]), reads=[r_B], writes=[r_B])
                    fb_t, fb_B = rot(finp)
                    K.op("dve", lambda e: e.tensor_tensor(out=fb_t[:], in0=accv[:, :, 0:64],
                                                          in1=r_t[:, 0:4].unsqueeze(2).to_broadcast([128, 4, 64]), op=ALU.mult),
                         reads=[acc_B, r_B], writes=[fb_B])
                    if i == 0:
                        cwin = list(range(0, 6))
                    elif i == NT_OWN - 1:
                        cwin = list(range(14, 20))
                    else:
                        cwin = list(range(i, i + 5))
                    spi = {0: 0, 1: 1, NT_OWN - 2: 2, NT_OWN - 1: 3}.get(i, None)
                    acc2_t, acc2_B = pz[5]
                    accv2 = acc2_t[:, 0:260].rearrange("p (a b) -> p a b", b=65)
                    slots2 = {}
                    nw = len(cwin)

                    def qkC(idx, i=i, cwin=cwin, slots2=slots2):
                        j = cwin[idx]
                        banks = [rot(pzs, "pzst"), rot(pzs, "pzst")]
                        slots2[idx] = banks
                        for h in range(4):
                            p_, hh = h // 2, h % 2
                            s_t, s_B = banks[hh]
                            K.op("pe", lambda e: e.matmul(
                                s_t[:, p_ * 128:(p_ + 1) * 128], lhsT=kTc[hh * 64:(hh + 1) * 64, p_, j * 128:(j + 1) * 128],
                                rhs=qTc[hh * 64:(hh + 1) * 64, i, p_, :], start=True, stop=True),
                                reads=[kTcB, qTcB], writes=[s_B])

                    def exC(idx, spi=spi, slots2=slots2):
                        banks = slots2[idx]
                        tb_t, tb_B = rot(ctp)
                        if spi is None:
                            K.dma(tb_t[:], io["cb_int"][l, :, idx * 512:(idx + 1) * 512], tb_B, writes=[tb_B], qn="pool")
                        else:
                            K.dma(tb_t[:], io["cb_sp"][l, spi, :, idx * 512:(idx + 1) * 512], tb_B, writes=[tb_B], qn="pool")
                        c_t, c_B = rot(tmpc)
                        for hh in range(2):
                            s_t, s_B = banks[hh]
                            tv = tb_t[:].rearrange("p (pp hh q) -> p hh pp q", pp=2, hh=2)[:, hh]
                            cv = c_t[:].rearrange("p (pp hh q) -> p hh pp q", pp=2, hh=2)[:, hh]
                            K.op("dve", lambda e: e.scalar_tensor_tensor(
                                out=cv, in0=s_t[:, 0:256].rearrange("p (pp q) -> p pp q", pp=2), scalar=0.125, in1=tv,
                                op0=ALU.mult, op1=ALU.add), reads=[s_B, tb_B], writes=[c_B])
                        p_t, p_B = rot(PT)
                        slots2[idx] = (p_t, p_B)
                        K.op("act", lambda e: e.activation(out=p_t[:], in_=c_t[:], func=AF.Exp), reads=[c_B], writes=[p_B])

                    def pvC(idx, cwin=cwin, slots2=slots2, accv2=accv2, acc2_B=acc2_B, nw=nw):
                        j = cwin[idx]
                        p_t, p_B = slots2[idx]
                        for h in range(4):
                            K.op("pe", lambda e, h=h: e.matmul(
                                accv2[:, h, :], lhsT=p_t[:, h * 128:(h + 1) * 128], rhs=vc[:, j, h, 0:65],
                                start=(idx == 0 and h == 0), stop=(idx == nw - 1 and h == 3)), reads=[p_B, vcB], writes=[acc2_B])

                    if 'C' in PH:
                        attn_pipe(nw, qkC, exC, pvC, 1)
                    r_t, r_B = rot(rd)
                    K.op("dve", lambda e: e.reciprocal(out=r_t[:, 0:4], in_=accv2[:, :, 64]), reads=[acc2_B], writes=[r_B])
                    fc_t, fc_B = rot(finp)
                    K.op("dve", lambda e: e.tensor_tensor(out=fc_t[:], in0=accv2[:, :, 0:64],
                                                          in1=r_t[:, 0:4].unsqueeze(2).to_broadcast([128, 4, 64]), op=ALU.mult),
                         reads=[acc2_B, r_B], writes=[fc_B])
                    SY[i] = (fb_t, fb_B, fc_t, fc_B)

                def p1_fin(i):
                    fb_t, fb_B, fc_t, fc_B = SY.pop(i)
                    m_t, m_B = rot(mixtm)
                    K.op("pool", lambda e: e.tensor_tensor(out=m_t[:, 0, :], in0=fb_t[:].rearrange("p h d -> p (h d)"),
                                                           in1=gbc[:, i, 0:256], op=ALU.mult),
                         reads=[fb_B, gbcB], writes=[m_B])
                    K.op("pool", lambda e: e.tensor_tensor(out=m_t[:, 1, :], in0=fc_t[:].rearrange("p h d -> p (h d)"),
                                                           in1=gbc[:, i, 256:512], op=ALU.mult),
                         reads=[fc_B, gbcB], writes=[m_B])
                    transpose_to([m_t[:, 0, 0:128], m_t[:, 0, 128:256], m_t[:, 1, 0:128], m_t[:, 1, 128:256]], m_B,
                                 mixT[:, 2:6, i * 128:(i + 1) * 128], mixTB)

                def run_p1_attn(tiles):
                    for k_, i in enumerate(tiles):
                        p1_attn(i)
                        if k_ >= 1:
                            p1_fin(tiles[k_ - 1])
                    p1_fin(tiles[-1])

                SY = {}
                run_p1_step1(0, NT_OWN)
                run_p1_attn(list(range(2, NT_OWN - 2)))
                run_p1_step1(NT_OWN, NT_LOC)
                run_p1_attn([0, 1, NT_OWN - 2, NT_OWN - 1])
                K.barrier()

            with ExitStack() as ph:
              if '2' in PH:
                wKV, wKVB = sb("wKV", [128, 8, 512], BF16, ph)
                wQG, wQGB = sb("wQG", [128, 8, 1024], BF16, ph)
                kTa, kTaB = sb("kTa", [128, SEQ], BF16, ph)
                va, vaB = sb("va", [128, NT_ALL, 2, 66], BF16, ph)
                kTd, kTdB = sb("kTd", [128, SEQ], BF16, ph)
                vd, vdB = sb("vd", [128, NT_ALL, 2, 66], BF16, ph)
                qTa = [sb("qTa%d" % i, [128, 2, 512], BF16, ph) for i in range(2)]
                qTd = [sb("qTd%d" % i, [128, 2, 2, 512], BF16, ph) for i in range(2)]
                gads = [sb("gad%d" % i, [128, 4, 512], GDT, ph) for i in range(2)]
                qa_tm = [sb("qatm%d" % i, [128, 256], BF16, ph) for i in range(2)]
                qd_tm = [sb("qdtm%d" % i, [128, 2, 256], BF16, ph) for i in range(2)]
                k2_tm = [sb("k2tm%d" % i, [128, 256], BF16, ph) for i in range(2)]
                zs = [sb("zs%d" % i, [128, 1024], F32, ph) for i in range(2)]

                K.op("pool", lambda e: e.memset(va[:].rearrange("p a b c -> p (a b c)"), 1.0), writes=[vaB])
                K.op("pool", lambda e: e.memset(vd[:].rearrange("p a b c -> p (a b c)"), 1.0), writes=[vdB])
                for i in range(2):
                    K.op("pool", lambda e, i=i: e.memset(qd_tm[i][0][:].rearrange("p a b -> p (a b)"), 0.0),
                         writes=[qd_tm[i][1]])
                for c in range(8):
                    load_weights_chunk(l, c, [
                        (256, 256, wKV[:, c, 0:256], wKVB), (2816, 256, wKV[:, c, 256:512], wKVB),
                        (0, 256, wQG[:, c, 0:256], wQGB), (512, 256, wQG[:, c, 256:512], wQGB),
                        (2560, 256, wQG[:, c, 512:768], wQGB), (3072, 256, wQG[:, c, 768:1024], wQGB)])


                def staged(n, stages):
                    ns = len(stages)
                    for step in range(n + ns - 1):
                        for si in reversed(range(ns)):
                            idx = step - si
                            if 0 <= idx < n:
                                stages[si](idx)

                S1 = {}

                kvx = xb + [(xs[i_][0][:].bitcast(BF16)[:, 0:1024], xs[i_][1]) for i_ in range(3)]

                def kv_a(t):
                    if l == 0:
                        x_t, x_B = load_x(x_src_full[t * 128:(t + 1) * 128, :], xfB(t * 128))
                    else:
                        x_t, x_B = rot(kvx)
                        K.dma(x_t[:], x_src_full[t * 128:(t + 1) * 128, :], x_B, reads=xfB(t * 128), writes=[x_B])
                    tb_t, tb_B = rot(tabt)
                    K.dma(tb_t[:], io["tabKV"][t * 128:(t + 1) * 128, :], tb_B, writes=[tb_B], qn="act")
                    S1[t] = [x_t, x_B, tb_t, tb_B]

                def kv_cast(t):
                    x_t, x_B = S1[t][0:2]
                    S1[t] += list(prep_cast(x_t, x_B)) if l == 0 else [x_t, x_B]

                def kv_stats(t):
                    x_t, x_B = S1[t][0:2]
                    S1[t] += list(prep_stats(x_t, x_B))

                def kv_b2(t):
                    x_t, x_B, tb_t, tb_B, b_t, b_B, s_t, s_B = S1[t]
                    T_t, T_B = prep_b(b_t, b_B)
                    S1[t] = [x_t, x_B, tb_t, tb_B, T_t, T_B, s_t, s_B]

                def kv_c(t):
                    x_t, x_B, tb_t, tb_B, T_t, T_B, s_t, s_B = S1[t]
                    z_t, z_B = rot(zs)
                    proj(T_t, T_B, wKV, wKVB, [(0, 512)], z_t, z_B, s_t, s_B, evac="act")
                    S1[t] = (z_t, z_B, tb_t, tb_B)

                def kv_d(t):
                    z_t, z_B, tb_t, tb_B = S1[t]
                    k_t, k_B = rot(k2_tm)
                    headnorm(z_t, z_B, 0, 2, vecs[:, V_KN:V_KN + 64])
                    rope(z_t, z_B, 0, 2, 16, tb_t, tb_B, 0, [(k_t[:, 0:128], lambda a: a)], k_B)
                    rope(z_t, z_B, 256, 2, 16, tb_t, tb_B, 128, [(k_t[:, 128:256], lambda a: a)], k_B)
                    K.op("act", lambda e: e.activation(out=va[:, t, :, 0:64],
                                                       in_=z_t[:, 128:256].rearrange("p (h d) -> p h d", h=2), func=AF.Copy),
                         reads=[z_B], writes=[vaB])
                    K.op("act", lambda e: e.activation(out=vd[:, t, :, 0:64],
                                                       in_=z_t[:, 384:512].rearrange("p (h d) -> p h d", h=2), func=AF.Copy),
                         reads=[z_B], writes=[vdB])
                    S1[t] = (k_t, k_B)

                def kv_e(t):
                    k_t, k_B = S1.pop(t)
                    transpose_multi([(k_t[:, 0:128], k_B), (k_t[:, 128:256], k_B)],
                                    [(kTa[:, t * 128:(t + 1) * 128], kTaB, 0, 1, "act"),
                                     (kTd[:, t * 128:(t + 1) * 128], kTdB, 1, 1, "act")])

                run_pipe(NT_ALL, [(0, kv_a), (1, kv_cast), (1, kv_stats), (2, kv_b2), (3, kv_c), (4, kv_d), (5, kv_e)])

                def make_qproj(grp_):
                    qa_t_, qa_B_ = qTa[grp_ % 2]
                    qd_t_, qd_B_ = qTd[grp_ % 2]
                    gad_, gadB_ = gads[grp_ % 2]
                    S2 = {}

                    def st0(qt):
                        ti = grp_ * 4 + qt
                        x_t, x_B = load_x(x_src_own[ti * 128:(ti + 1) * 128, :], xoB)
                        tb_t, tb_B = rot(tabt)
                        K.dma(tb_t[:], io["tabQ"][ti * 128:(ti + 1) * 128, :], tb_B, writes=[tb_B], qn="act")
                        b_t, b_B, s_t, s_B = prep_a(x_t, x_B)
                        S2[qt] = (tb_t, tb_B, b_t, b_B, s_t, s_B)

                    def st1(qt):
                        tb_t, tb_B, b_t, b_B, s_t, s_B = S2[qt]
                        T_t, T_B = prep_b(b_t, b_B)
                        z_t, z_B = rot(zs)
                        proj(T_t, T_B, wQG, wQGB, [(0, 512), (512, 512)], z_t, z_B, s_t, s_B)
                        S2[qt] = (z_t, z_B, tb_t, tb_B)

                    def st2(qt):
                        z_t, z_B, tb_t, tb_B = S2[qt]
                        headnorm(z_t, z_B, 0, 4, vecs[:, V_QN:V_QN + 64])
                        a_t, a_B = rot(qa_tm)
                        rope(z_t, z_B, 0, 4, 16, tb_t, tb_B, 0,
                             [(a_t[:, 0:256].rearrange("p (g kv d) -> p kv g d", g=2, kv=2),
                               lambda a: a.rearrange("p (kv g d) -> p kv g d", kv=2, g=2))], a_B)
                        d_t, d_B = rot(qd_tm)
                        dsts = []
                        for c in range(2):
                            dsts.append((
                                d_t[:, c, :].rearrange("p (g kv c k) -> p kv g c k", g=2, kv=2, c=2)[:, :, :, c, :],
                                lambda a, c=c: a.rearrange("p (kv g c k) -> p kv g c k", kv=2, g=2, c=2)[:, :, :, c, :]))
                        rope(z_t, z_B, 512, 4, 16, tb_t, tb_B, 128, dsts, d_B)
                        silu_gate(z_t, z_B, 256, 256, gad_[:, qt, 0:256], gadB_)
                        silu_gate(z_t, z_B, 768, 256, gad_[:, qt, 256:512], gadB_)
                        S2[qt] = (a_t, a_B, d_t, d_B)

                    def st3(qt):
                        a_t, a_B, d_t, d_B = S2.pop(qt)
                        transpose_multi(
                            [(a_t[:, 0:128], a_B), (a_t[:, 128:256], a_B), (d_t[:, 0, 0:128], d_B), (d_t[:, 0, 128:256], d_B),
                             (d_t[:, 1, 0:128], d_B), (d_t[:, 1, 128:256], d_B)],
                            [(qa_t_[:, :, qt * 128:(qt + 1) * 128], qa_B_, 0, 2, "dve"),
                             (qd_t_[:, 0, :, qt * 128:(qt + 1) * 128], qd_B_, 2, 2, "dve"),
                             (qd_t_[:, 1, :, qt * 128:(qt + 1) * 128], qd_B_, 4, 2, "dve")])

                    stages = (st0, st1, st2, st3)

                    def boundary(b_):
                        for si in reversed(range(4)):
                            qt = b_ - si
                            if 0 <= qt < 4:
                                stages[si](qt)
                    return boundary

                nxt = make_qproj(0)
                for b_ in range(7):
                    nxt(b_)
                deferred = []
                for grp in range(4):
                    qa_t, qa_B = qTa[grp % 2]
                    qd_t, qd_B = qTd[grp % 2]
                    gad, gadB = gads[grp % 2]
                    nxt = make_qproj(grp + 1) if grp < 3 else (lambda b_: None)
                    bcount = [0]

                    def loop_done():
                        nxt(bcount[0])
                        bcount[0] += 1

                    def run_mid():
                        while deferred:
                            deferred.pop(0)()
                        loop_done()

                    ma_t, ma_B = rot(mixtm)
                    for g in range(2):
                        accs = [pz[4], pz[5]]
                        accvs = [a_[0][:, 0:260].rearrange("p (a b) -> p a b", b=65) for a_ in accs]
                        slots = {}

                        def qkA(j):
                            pi = rot([0, 1], "pzpair")
                            banks = [pzs[2 * pi], pzs[2 * pi + 1]]
                            slots[j] = (pi, banks)
                            for kv in range(2):
                                s_t, s_B = banks[kv]
                                K.op("pe", lambda e: e.matmul(s_t[:], lhsT=kTa[kv * 64:(kv + 1) * 64, j * 128:(j + 1) * 128],
                                                              rhs=qa_t[kv * 64:(kv + 1) * 64, g, :], start=True, stop=True),
                                     reads=[kTaB, qa_B], writes=[s_B])

                        def exA(j):
                            pi, banks = slots[j]
                            qi = rot([0, 1, 2], "ptpair")
                            pts = [PT[2 * qi], PT[2 * qi + 1]]
                            K.op("act", lambda e: e.activation(out=PT2[qi][:], in_=pzd[pi][:], func=AF.Exp),
                                 reads=[banks[0][1], banks[1][1]], writes=[pts[0][1], pts[1][1]])
                            slots[j] = pts

                        def pvA(j):
                            pts = slots.pop(j)
                            for kv in range(2):
                                p_t, p_B = pts[kv]
                                for qt in range(4):
                                    K.op("pe", lambda e: e.matmul(accvs[kv][:, qt, :], lhsT=p_t[:, qt * 128:(qt + 1) * 128],
                                                                  rhs=va[:, j, kv, 0:65], start=(j == 0 and qt == 0),
                                                                  stop=(j == NT_ALL - 1 and qt == 3)),
                                         reads=[p_B, vaB], writes=[accs[kv][1]])

                        attn_pipe(NT_ALL, qkA, exA, pvA, 2, {8: run_mid})
                        for kv in range(2):
                            h = 2 * kv + g
                            accv, acc_B = accvs[kv], accs[kv][1]
                            r_t, r_B = rot(rd)
                            K.op("dve", lambda e: e.reciprocal(out=r_t[:, 0:4], in_=accv[:, :, 64]), reads=[acc_B], writes=[r_B])
                            f_t, f_B = rot(fin)
                            K.op("dve", lambda e: e.tensor_tensor(out=f_t[:], in0=accv[:, :, 0:64],
                                                                  in1=r_t[:, 0:4].unsqueeze(2).to_broadcast([128, 4, 64]),
                                                                  op=ALU.mult), reads=[acc_B, r_B], writes=[f_B])
                            K.op("pool", lambda e: e.tensor_tensor(out=ma_t[:, :, h * 64:(h + 1) * 64], in0=f_t[:],
                                                                   in1=gad[:, :, h * 64:(h + 1) * 64], op=ALU.mult),
                                 reads=[f_B, gadB], writes=[ma_B])
                    def tr_a(ma_t=ma_t, ma_B=ma_B, grp=grp):
                        for qt in range(4):
                            transpose_to([ma_t[:, qt, 0:128], ma_t[:, qt, 128:256]], ma_B,
                                         mixT[:, 0:2, (grp * 4 + qt) * 128:(grp * 4 + qt + 1) * 128], mixTB)
                    deferred.append(tr_a)

                    md_t, md_B = rot(mixtm)
                    for half in range(2):
                        for g in range(2):
                            accs = [pz[4], pz[5]]
                            accvs = [a_[0][:, 0:260].rearrange("p (c a b) -> p c a b", c=2, b=65) for a_ in accs]
                            slots = {}

                            def qkD(j):
                                pi = rot([0, 1], "pzpair")
                                banks = [pzs[2 * pi], pzs[2 * pi + 1]]
                                slots[j] = (pi, banks)
                                for c in range(2):
                                    for kv in range(2):
                                        s_t, s_B = banks[kv]
                                        K.op("pe", lambda e: e.matmul(
                                            s_t[:, c * 256:(c + 1) * 256], lhsT=kTd[kv * 64:(kv + 1) * 64, j * 128:(j + 1) * 128],
                                            rhs=qd_t[kv * 64:(kv + 1) * 64, c, g, half * 256:(half + 1) * 256],
                                            start=True, stop=True), reads=[kTdB, qd_B], writes=[s_B])

                            def exD(j):
                                pi, banks = slots[j]
                                qi = rot([0, 1, 2], "ptpair")
                                pts = [PT[2 * qi], PT[2 * qi + 1]]
                                K.op("act", lambda e: e.activation(out=PT2[qi][:], in_=pzd[pi][:], func=AF.Exp),
                                     reads=[banks[0][1], banks[1][1]], writes=[pts[0][1], pts[1][1]])
                                slots[j] = pts

                            def pvD(j):
                                pts = slots.pop(j)
                                for kv in range(2):
                                    p_t, p_B = pts[kv]
                                    for c in range(2):
                                        for qt in range(2):
                                            K.op("pe", lambda e: e.matmul(
                                                accvs[kv][:, c, qt, :], lhsT=p_t[:, c * 256 + qt * 128:c * 256 + (qt + 1) * 128],
                                                rhs=vd[:, j, kv, 0:65], start=(j == 0 and c == 0 and qt == 0),
                                                stop=(j == NT_ALL - 1 and c == 1 and qt == 1)),
                                                reads=[p_B, vdB], writes=[accs[kv][1]])

                            attn_pipe(NT_ALL, qkD, exD, pvD, 2,
                                      {8: run_mid, 36: loop_done} if (half == 1 and g == 1) else {8: run_mid})
                            eps_ = []
                            for kv in range(2):
                                h = 2 * kv + g
                                av, acc_B = accvs[kv], accs[kv][1]
                                r_t, r_B = rot(rd)
                                K.op("dve", lambda e: e.reciprocal(out=r_t[:, 0:4].rearrange("p (c a) -> p c a", c=2), in_=av[:, :, :, 64]),
                                     reads=[acc_B], writes=[r_B])
                                K.op("dve", lambda e: e.tensor_scalar(out=r_t[:, 2:4], in0=r_t[:, 2:4], scalar1=lamt[:, 5:6],
                                                                      scalar2=None, op0=ALU.mult), reads=[r_B, lamtB], writes=[r_B])
                                f_t, f_B = rot(fin)
                                g_t, g_B = rot(fin2)
                                K.op("dve", lambda e: e.tensor_tensor(out=f_t[:], in0=av[:, :, :, 0:64].rearrange("p c a d -> p (c a) d"),
                                                                      in1=r_t[:, 0:4].unsqueeze(2).to_broadcast([128, 4, 64]),
                                                                      op=ALU.mult), reads=[acc_B, r_B], writes=[f_B])
                                eps_.append((h, f_t, f_B, g_t, g_B))
                            for (h, f_t, f_B, g_t, g_B) in eps_:
                                K.op("pool", lambda e: e.tensor_tensor(out=f_t[:, 0:2, :], in0=f_t[:, 0:2, :], in1=f_t[:, 2:4, :], op=ALU.add),
                                     reads=[f_B], writes=[f_B])
                                K.op("pool", lambda e: e.tensor_tensor(out=g_t[:, 0:2, :], in0=f_t[:, 0:2, :], in1=f_t[:, 0:2, :], op=ALU.mult),
                                     reads=[f_B], writes=[g_B])
                                h_t, h_B = rot(hn)
                                K.op("dve", lambda e: e.tensor_reduce(out=h_t[:, 0:2], in_=g_t[:, 0:2, :], axis=AX.X, op=ALU.add),
                                     reads=[g_B], writes=[h_B])
                                K.op("pool", lambda e: e.tensor_scalar(out=h_t[:, 0:2], in0=h_t[:, 0:2], scalar1=1.0 / 64, scalar2=EPS,
                                                                       op0=ALU.mult, op1=ALU.add), reads=[h_B], writes=[h_B])
                                K.op("pool", lambda e: e.tensor_tensor(out=h_t[:, 0:2], in0=h_t[:, 0:2], in1=mhalf[:, 0:2], op=ALU.pow),
                                     reads=[h_B, mhalfB], writes=[h_B])
                                K.op("dve", lambda e: e.tensor_tensor(out=f_t[:, 0:2, :], in0=f_t[:, 0:2, :],
                                                                      in1=h_t[:, 0:2].unsqueeze(2).to_broadcast([128, 2, 64]), op=ALU.mult),
                                     reads=[f_B, h_B], writes=[f_B])
                                K.op("dve", lambda e: e.tensor_tensor(out=f_t[:, 0:2, :], in0=f_t[:, 0:2, :],
                                                                      in1=subg[:].unsqueeze(1).to_broadcast([128, 2, 64]), op=ALU.mult),
                                     reads=[f_B, subgB], writes=[f_B])
                                K.op("pool", lambda e: e.tensor_tensor(
                                    out=md_t[:, half * 2:half * 2 + 2, h * 64:(h + 1) * 64], in0=f_t[:, 0:2, :],
                                    in1=gad[:, half * 2:half * 2 + 2, 256 + h * 64:256 + (h + 1) * 64], op=ALU.mult),
                                    reads=[f_B, gadB], writes=[md_B])
                    def tr_d(md_t=md_t, md_B=md_B, grp=grp):
                        for qt in range(4):
                            transpose_to([md_t[:, qt, 0:128], md_t[:, qt, 128:256]], md_B,
                                         mixT[:, 6:8, (grp * 4 + qt) * 128:(grp * 4 + qt + 1) * 128], mixTB)
                    deferred.append(tr_d)
                while deferred:
                    deferred.pop(0)()
                K.barrier()

            with ExitStack() as ph:
              if '3' in PH:
                wo, woB = sb("wo", [128, 8, 1024], BF16, ph)
                fg, fgB = sb("fg", [128, 1024], F32, ph)
                xn = [sb("xn%d" % i, [128, 1024], F32, ph) for i in range(2)]
                yo = [sb("yo%d" % i, [128, 1024], F32, ph) for i in range(2)]
                yb = [sb("yb%d" % i, [128, 1024], BF16, ph) for i in range(4)]
                K.dma(fg[:], io["fg"][:, :], fgB, writes=[fgB])
                for c in range(8):
                    w_t, w_B = rot(xs)
                    K.dma(w_t[:], io["w_out"][l, c * 128:(c + 1) * 128, :], w_B, writes=[w_B])
                    if c % 2:
                        K.op("act", lambda e, c=c: e.activation(out=wo[:, c, :], in_=w_t[:], func=AF.Copy), reads=[w_B], writes=[woB])
                    else:
                        K.op("dve", lambda e, c=c: e.tensor_copy(out=wo[:, c, :], in_=w_t[:]), reads=[w_B], writes=[woB])
                for ti in range(NT_OWN):
                    x_t, x_B = load_x(x_src_own[ti * 128:(ti + 1) * 128, :], xoB)
                    n_t, n_B = rot(xn)
                    for n in range(2):
                        p_t, p_B = rot(pzs, "pzproj")
                        for c in range(8):
                            K.op("pe", lambda e, c=c, n=n: e.matmul(p_t[:], lhsT=mixT[:, c, ti * 128:(ti + 1) * 128],
                                                                    rhs=wo[:, c, n * 512:(n + 1) * 512],
                                                                    start=(c == 0), stop=(c == 7)),
                                 reads=[mixTB, woB], writes=[p_B])
                        K.op("dve", lambda e, n=n: e.tensor_tensor(out=n_t[:, n * 512:(n + 1) * 512], in0=p_t[:],
                                                                   in1=x_t[:, n * 512:(n + 1) * 512], op=ALU.add),
                             reads=[p_B, x_B], writes=[n_B])
                    if not fused:
                        K.dma(io["xnext"][ti * 128:(ti + 1) * 128, :], n_t[:], n_B, reads=[n_B])
                    elif l < L - 1:
                        K.dma(io["x1own"][ti * 128:(ti + 1) * 128, :], n_t[:], n_B, reads=[n_B], writes=[x1ownB])
                        for m in range(4):
                            y_t, y_B = rot(yb)
                            if m % 2:
                                K.op("act", lambda e: e.activation(out=y_t[:], in_=n_t[:], func=AF.Copy, scale=rsel[:, m:m + 1]),
                                     reads=[n_B, rselB], writes=[y_B])
                            else:
                                K.op("dve", lambda e: e.tensor_scalar(
                                    out=y_t[:], in0=n_t[:], scalar1=rsel[:, m:m + 1], scalar2=None, op0=ALU.mult),
                                    reads=[n_B, rselB], writes=[y_B])
                            K.dma(io["xg_src"][m * TOWN + ti * 128:m * TOWN + (ti + 1) * 128, :], y_t[:], y_B,
                                  reads=[y_B], writes=[xgsB[(m * TOWN + ti * 128) // 1024]], qn=("act" if m % 2 == 0 else "pool"))
                        if ti % 8 == 7:
                            for m in range(4):
                                ch = m * 2 + ti // 8
                                K.collective("AllReduce", ALU.add, [[0, 1, 2, 3], [4, 5, 6, 7]],
                                             io["xg_src"][ch * 1024:(ch + 1) * 1024, :], io["xg_dst"][ch * 1024:(ch + 1) * 1024, :],
                                             xgsB[ch], xgdB[ch])
                    if l == L - 1:
                        s_t, s_B = rot(st1)
                        stats_rstd(n_t[:], n_B, 1024, s_t, s_B)
                        y_t, y_B = rot(yo)
                        K.op("dve", lambda e: e.scalar_tensor_tensor(out=y_t[:], in0=n_t[:], scalar=s_t[:, 2:3], in1=fg[:],
                                                                     op0=ALU.mult, op1=ALU.mult),
                             reads=[n_B, s_B, fgB], writes=[y_B])
                        K.dma(io["y"][ti * 128:(ti + 1) * 128, :], y_t[:], y_B, reads=[y_B])
                K.barrier()
        K.barrier(final=True)
        stats = {k: v.cnt for k, v in K.q.items()}
        stats["nsem"] = K.nsem
    build_program.stats = stats
    return nc


def _rope_cs(pos, d):
    inv = np.power(np.float32(10000.0), -(np.arange(0, d, 2, dtype=np.float32) / np.float32(d))).astype(np.float32)
    ang = (pos.astype(np.float32)[:, None] * inv[None, :]).astype(np.float32)
    return np.cos(ang.astype(np.float64)).astype(np.float32), np.sin(ang.astype(np.float64)).astype(np.float32)


def _tab_A(pos):
    row, col = pos // 64, pos % 64
    cr, sr = _rope_cs(row, 32)
    cc, sc = _rope_cs(col, 32)
    return np.concatenate([cr, cr, cc, cc, -sr, sr, -sc, sc], axis=1)


def _tab_D(pos):
    c, s = _rope_cs(pos, 32)
    return np.concatenate([c, c, c, c, -s, s, -s, s], axis=1)


def _tab_B(pos):
    c, s = _rope_cs(pos, 64)
    return np.concatenate([c, c, -s, s], axis=1)


def _c_bias_table(rpb_l, r, i, win):
    a = np.arange(128)
    ks = (TOWN * r - 256 + 128 * np.asarray(win)[:, None] + a[None, :])
    qt = TOWN * r + 128 * i + a
    kvalid = (ks >= 0) & (ks < SEQ)
    ksc = np.clip(ks, 0, SEQ - 1)
    kr, kc = ksc // 64, ksc % 64
    qr, qc = qt // 64, qt % 64
    rs = np.clip(qr - 4, 0, 120)
    cs = np.clip(qc - 8, 0, 48)
    ok = (kvalid[:, :, None] & (kr[:, :, None] >= rs[None, None, :]) & (kr[:, :, None] < rs[None, None, :] + 8)
          & (kc[:, :, None] >= cs[None, None, :]) & (kc[:, :, None] < cs[None, None, :] + 16))
    dr = np.clip(kr[:, :, None] - qr[None, None, :] + 7, 0, 14)
    dc = np.clip(kc[:, :, None] - qc[None, None, :], -15, 15) + 15
    g = rpb_l[:, dr, dc]
    out = np.where(ok[None], g, np.float32(MASKV)).astype(np.float32)
    return np.ascontiguousarray(out.transpose(2, 1, 0, 3))


def _core_inputs(c, xfull, xown, layers, P):
    r = c % 4
    m = {"xfull": xfull, "xown": xown}
    m["w_in"] = np.ascontiguousarray(P["w_in"][layers])
    m["w_out"] = np.ascontiguousarray(P["w_out"][layers])
    m["gcol"] = np.ascontiguousarray(P["norm_g"][layers].reshape(len(layers), 8, 128).transpose(0, 2, 1))
    vec = np.zeros((len(layers), V_N), np.float32)
    for k, l in enumerate(layers):
        lam_init = np.float32(0.8 - 0.6 * np.exp(-0.3 * l))
        vec[k, V_QN:V_QN + 64] = P["qn_a"][l]
        vec[k, V_KN:V_KN + 64] = P["kn_a"][l]
        vec[k, V_SUB:V_SUB + 64] = P["subln_d"][l]
        vec[k, V_SINK:V_SINK + 4] = P["sink_b"][l]
        vec[k, V_LQ1:V_LQ1 + 32] = P["lam_q1"][l]
        vec[k, V_LK1:V_LK1 + 32] = P["lam_k1"][l]
        vec[k, V_LQ2:V_LQ2 + 32] = P["lam_q2"][l]
        vec[k, V_LK2:V_LK2 + 32] = P["lam_k2"][l]
        vec[k, V_LC] = lam_init
        vec[k, V_LC + 1] = np.float32(1.0) - lam_init
    m["vecs"] = np.ascontiguousarray(np.broadcast_to(vec[:, None, :], (len(layers), 128, V_N)))
    m["fg"] = np.ascontiguousarray(np.broadcast_to(P["final_g"][None, :], (128, D_MODEL)))
    pos_all = np.arange(SEQ)
    m["tabKV"] = np.ascontiguousarray(np.concatenate([_tab_A(pos_all), _tab_D(pos_all)], axis=1))
    pos_own = TOWN * r + np.arange(TOWN)
    m["tabQ"] = np.ascontiguousarray(np.concatenate(
        [_tab_A(pos_own) * np.float32(0.125), _tab_D(pos_own) * np.float32(32 ** -0.5)], axis=1).astype(np.float32))
    pos_loc = np.clip(TOWN * r - 256 + np.arange(NT_LOC * 128), 0, SEQ - 1)
    tb = _tab_B(pos_loc)
    m["tabB"] = np.ascontiguousarray(np.concatenate([tb, tb * np.float32(0.125)], axis=1).astype(np.float32))
    cbi = np.zeros((len(layers), 128, 5 * 512), np.float32)
    cbsp = np.full((len(layers), 4, 128, 6 * 512), MASKV, np.float32)
    for k, l in enumerate(layers):
        rpb = P["rpb_c"][l]
        cbi[k] = _c_bias_table(rpb, r, 5, list(range(5, 10))).reshape(128, -1)
        for si, (i, win) in enumerate([(0, list(range(0, 6))), (1, list(range(1, 6))),
                                       (14, list(range(14, 19))), (15, list(range(14, 20)))]):
            t = _c_bias_table(rpb, r, i, win).reshape(128, -1)
            cbsp[k, si, :, :t.shape[1]] = t
    m["cb_int"] = cbi
    m["cb_sp"] = cbsp
    a = np.arange(128)
    mprev = (a[None, :] <= a[:, None]).astype(np.float32)
    mnext = (a[:, None] <= a[None, :]).astype(np.float32)
    z = np.zeros_like(mprev)
    m["bmask"] = np.ascontiguousarray(np.concatenate(
        [mprev if r > 0 else z, mprev, mnext, mnext if r < 3 else z], axis=1))
    sel = np.zeros((6,), np.float32)
    if r >= 1:
        sel[r - 1] = 1.0
    if r <= 2:
        sel[3 + r] = 1.0
    m["halosel"] = np.ascontiguousarray(np.broadcast_to(sel[None, :], (128, 6)))
    return m


_NC_CACHE = {}


def _get_nc(n_layers, fused):
    key = (n_layers, fused)
    if key not in _NC_CACHE:
        _NC_CACHE[key] = build_program(n_layers, fused)
    return _NC_CACHE[key]


def kernel(x, norm_g, w_in, w_out, qn_a, kn_a, sink_b, rpb_c, lam_q1, lam_k1, lam_q2, lam_k2, subln_d, final_g):
    P = dict(norm_g=norm_g, w_in=w_in, w_out=w_out, qn_a=qn_a, kn_a=kn_a, sink_b=sink_b, rpb_c=rpb_c,
             lam_q1=lam_q1, lam_k1=lam_k1, lam_q2=lam_q2, lam_k2=lam_k2, subln_d=subln_d, final_g=final_g)
    P = {k: np.asarray(v, dtype=np.float32) for k, v in P.items()}
    x = np.asarray(x, dtype=np.float32)
    nc = _get_nc(DEPTH, True)
    in_maps = []
    for c in range(NCORES):
        b, r = c // 4, c % 4
        m = _core_inputs(c, np.ascontiguousarray(x[b]), np.ascontiguousarray(x[b, r * TOWN:(r + 1) * TOWN]),
                         list(range(DEPTH)), P)
        rs = np.zeros((4,), np.float32)
        rs[r] = 1.0
        m["rsel"] = np.ascontiguousarray(np.broadcast_to(rs[None, :], (128, 4)))
        in_maps.append(m)
    res = run_bass_kernel_spmd(nc, in_maps, core_ids=list(range(NCORES)))
    y = np.empty_like(x)
    for c in range(NCORES):
        b, r = c // 4, c % 4
        y[b, r * TOWN:(r + 1) * TOWN] = res.results[c]["y"]
    return y
```

```python
import os
import numpy as np
from contextlib import ExitStack
import concourse.bass as bass
import concourse.mybir as mybir
from concourse.bass_utils import run_bass_kernel_spmd

F32 = mybir.dt.float32
GDT = mybir.dt.bfloat16
BF16 = mybir.dt.bfloat16
AF = mybir.ActivationFunctionType
ALU = mybir.AluOpType
AX = mybir.AxisListType

D_MODEL = 1024
SEQ = 8192
BATCH = 2
DEPTH = 2
D_IN = 3328
NCORES = 8
TOWN = 2048
NT_OWN = 16
NT_ALL = 64
NT_LOC = 20
EPS = 1e-6
MASKV = -30000.0

V_QN, V_KN, V_SUB, V_SINK, V_LQ1, V_LK1, V_LQ2, V_LK2, V_LC = 0, 64, 128, 192, 196, 228, 260, 292, 324
V_N = 328


class Buf:
    __slots__ = ("w", "r", "dsem", "dcnt", "name")
    ALL = []

    def __init__(self, name=""):
        self.w = None
        self.r = {}
        self.dsem = None
        self.dcnt = 0
        self.name = name
        Buf.ALL.append(self)


class Q:
    def __init__(self, name, eng, sem):
        self.name, self.eng, self.sem = name, eng, sem
        self.cnt = 0
        self.seen = {}


class Ctx:
    def __init__(self, nc, st):
        self.nc, self.st = nc, st
        self.q = {}
        for name, eng in (("pe", nc.tensor), ("act", nc.scalar), ("dve", nc.vector),
                          ("pool", nc.gpsimd), ("sp", nc.sync)):
            self.q[name] = Q(name, eng, st.enter_context(nc.semaphore("q_" + name)))
        self.dbufs = []
        self.nsem = 5

    def _wait(self, q, deps):
        for key, (s, v) in deps.items():
            if q.seen.get(key, 0) >= v:
                continue
            q.eng.wait_ge(s, v)
            q.seen[key] = v

    def _deps(self, q, reads, writes):
        deps = {}

        def add(tok, raw):
            if tok is None:
                return
            s, v = tok
            if s is q.sem and (q.name == "pe" or not raw):
                return
            k = id(s)
            if k not in deps or deps[k][1] < v:
                deps[k] = (s, v)
        for b in reads:
            add(b.w, True)
        for b in writes:
            add(b.w, True)
            for tok in b.r.values():
                add(tok, True)
        return deps

    def op(self, qn, fn, reads=(), writes=()):
        q = self.q[qn]
        self._wait(q, self._deps(q, reads, writes))
        ins = fn(q.eng)
        q.cnt += 1
        ins.then_inc(q.sem, 1)
        tok = (q.sem, q.cnt)
        for b in reads:
            b.r[id(q.sem)] = tok
        for b in writes:
            b.w = tok
            b.r = {}
        return ins

    def dma(self, out, in_, sb, reads=(), writes=(), qn="sp"):
        q = self.q[qn]
        if sb.dsem is None:
            sb.dsem = self.st.enter_context(self.nc.semaphore("d_%d" % self.nsem))
            self.nsem += 1
            self.dbufs.append(sb)
        self._wait(q, self._deps(q, reads, writes))
        ins = q.eng.dma_start(out=out, in_=in_)
        sb.dcnt += 16
        ins.then_inc(sb.dsem, 16)
        tok = (sb.dsem, sb.dcnt)
        for b in reads:
            b.r[id(sb.dsem)] = tok
        for b in writes:
            b.w = tok
            b.r = {}

    def collective(self, kind, op, groups, in_ap, out_ap, in_B, out_B):
        q = self.q["pool"]
        self._wait(q, self._deps(q, [in_B], [out_B]))
        sem = self.st.enter_context(self.nc.semaphore("cc_%d" % self.nsem))
        self.nsem += 1
        ins = q.eng.collective_compute(kind, op, replica_groups=groups, ins=[in_ap], outs=[out_ap])
        ins.then_inc(sem)
        tok = (sem, 1)
        in_B.r[id(sem)] = tok
        out_B.w = tok
        out_B.r = {}
        self.ccs = getattr(self, "ccs", []) + [tok]

    def renew(self):
        self.barrier()
        ccsems = set(id(cs) for (cs, cv) in getattr(self, "ccs", []))
        for b in Buf.ALL:
            if b.w is not None and id(b.w[0]) in ccsems:
                b.r = {}
                continue
            b.w = None
            b.r = {}
        for name, qq in self.q.items():
            qq.sem = self.st.enter_context(self.nc.semaphore("q%d_%s" % (self.nsem, name)))
            self.nsem += 1
            qq.cnt = 0
            qq.seen = {}
        for b in self.dbufs:
            pass

    def barrier(self, final=False):
        sp = self.q["sp"]
        deps = {}
        for b in self.dbufs:
            if b.dcnt:
                deps[id(b.dsem)] = (b.dsem, b.dcnt)
        if final:
            for (cs, cv) in getattr(self, "ccs", []):
                deps[id(cs)] = (cs, cv)
        for qq in self.q.values():
            if qq is not sp and qq.cnt:
                deps[id(qq.sem)] = (qq.sem, qq.cnt)
        self._wait(sp, deps)
        self.op("sp", lambda e: e.nop())
        for qq in self.q.values():
            if qq is sp:
                continue
            d = {id(sp.sem): (sp.sem, sp.cnt)}
            self._wait(qq, d)


def build_program(n_layers, fused):
    PH = os.environ.get('KPH', '123abBC')
    nc = bass.Bass("TRN2", target_bir_lowering=False)
    L = n_layers

    def din(name, shape, dt=F32):
        return nc.dram_tensor(name, shape, dt, kind="ExternalInput").ap()

    io = dict(
        xfull=din("xfull", [SEQ, D_MODEL]),
        xown=din("xown", [TOWN, D_MODEL]),
        w_in=din("w_in", [L, D_MODEL, D_IN]),
        w_out=din("w_out", [L, D_MODEL, D_MODEL]),
        gcol=din("gcol", [L, 128, 8]),
        vecs=din("vecs", [L, 128, V_N]),
        fg=din("fg", [128, D_MODEL]),
        tabKV=din("tabKV", [SEQ, 256]),
        tabQ=din("tabQ", [TOWN, 256]),
        tabB=din("tabB", [NT_LOC * 128, 256]),
        cb_int=din("cb_int", [L, 128, 5 * 512]),
        cb_sp=din("cb_sp", [L, 4, 128, 6 * 512]),
        bmask=din("bmask", [128, 512]),
        halosel=din("halosel", [128, 6]),
        y=nc.dram_tensor("y", [TOWN, D_MODEL], F32, kind="ExternalOutput").ap(),
    )
    if fused:
        io["rsel"] = din("rsel", [128, 4])
        io["x1own"] = nc.dram_tensor("x1own", [TOWN, D_MODEL], F32, kind="Internal").ap()
        io["xg_src"] = nc.dram_tensor("xg_src", [SEQ, D_MODEL], BF16, kind="Internal").ap()
        io["xg_dst"] = nc.dram_tensor("xg_dst", [SEQ, D_MODEL], BF16, kind="Internal").ap()
    else:
        io["xnext"] = nc.dram_tensor("xnext", [TOWN, D_MODEL], F32, kind="ExternalOutput").ap()
    x1ownB = Buf("x1own")
    xgsB = [Buf("xg_src%d" % i) for i in range(8)]
    xgdB = [Buf("xg_dst%d" % i) for i in range(8)]

    with ExitStack() as st:
        K = Ctx(nc, st)
        E = st.enter_context

        uniq = [0]

        def sb(name, shape, dt, stack=None):
            uniq[0] += 1
            t = (stack or st).enter_context(nc.sbuf_tensor("s%d_%s" % (uniq[0], name), shape, dt))
            return t, Buf(name)

        ident, identB = sb("ident", [128, 128], BF16)
        identf, identfB = sb("identf", [128, 128], F32)
        pT = [(E(nc.psum_tensor("pT%d" % i, [128, 1024], BF16)), Buf("pT%d" % i)) for i in range(2)]
        pzd = [E(nc.psum_tensor("pzd%d" % i, [128, 1024], F32)) for i in range(2)]
        pz = [(pzd[i // 2][:, (i % 2) * 512:(i % 2 + 1) * 512], Buf("pz%d" % i)) for i in range(4)]
        pz += [(E(nc.psum_tensor("pz%d" % i, [128, 512], F32)), Buf("pz%d" % i)) for i in (4, 5)]
        pzs = pz[:4]
        xs = [sb("xs%d" % i, [128, 1024], F32) for i in range(3)]
        xb = [sb("xb%d" % i, [128, 1024], BF16) for i in range(2)]
        xT = [sb("xT%d" % i, [128, 8, 128], BF16) for i in range(3)]
        junk, junkB = sb("junk", [128, 1024], BF16)
        st1 = [sb("st1_%d" % i, [128, 8], F32) for i in range(4)]
        hn = [sb("hn%d" % i, [128, 8], F32) for i in range(3)]
        t1 = [sb("t1_%d" % i, [128, 256], F32) for i in range(2)]
        t2 = [sb("t2_%d" % i, [128, 256], F32) for i in range(2)]
        t3 = [sb("t3_%d" % i, [128, 256], F32) for i in range(2)]
        tabt = [sb("tabt%d" % i, [128, 256], F32) for i in range(5)]
        PT2 = [sb("PT%d" % i, [128, 1024], BF16)[0] for i in range(3)]
        PT = [(PT2[i // 2][:, (i % 2) * 512:(i % 2 + 1) * 512], Buf("PT%d" % i)) for i in range(6)]
        mixT, mixTB = sb("mixT", [128, 8, TOWN], BF16)
        mixtm = [sb("mixtm%d" % i, [128, 4, 256], BF16) for i in range(2)]
        vecs, vecsB = sb("vecs", [128, V_N], F32)
        gcol, gcolB = sb("gcol", [128, 8], F32)
        lamt, lamtB = sb("lamt", [128, 8], F32)
        esink, esinkB = sb("esink", [128, 4], F32)
        subg, subgB = sb("subg", [128, 64], F32)
        hsel, hselB = sb("hsel", [128, 6], F32)
        bmask, bmaskB = sb("bmask", [128, 4, 128], F32)
        fin = [sb("fin%d" % i, [128, 4, 64], F32) for i in range(2)]
        fin2 = [sb("fin2_%d" % i, [128, 4, 64], F32) for i in range(2)]
        rd = [sb("rd%d" % i, [128, 8], F32) for i in range(2)]

        rr = {}

        def rot(lst, key=None):
            k = key or id(lst)
            i = rr.get(k, 0)
            rr[k] = i + 1
            return lst[i % len(lst)]

        mhalf, mhalfB = sb("mhalf", [128, 8], F32)
        K.op("pool", lambda e: e.memset(mhalf[:], -0.5), writes=[mhalfB])
        K.op("pool", lambda e: e.memset(identf[:], 1.0), writes=[identfB])
        K.op("pool", lambda e: e.affine_select(out=identf[:], in_=identf[:], pattern=[[-1, 128]],
                                               compare_op=ALU.is_equal, fill=0.0, base=0,
                                               channel_multiplier=1), reads=[identfB], writes=[identfB])
        K.op("dve", lambda e: e.tensor_copy(out=ident[:], in_=identf[:]), reads=[identfB], writes=[identB])
        K.dma(bmask[:].rearrange("p a b -> p (a b)"), io["bmask"][:, :], bmaskB, writes=[bmaskB])
        K.dma(hsel[:], io["halosel"][:, :], hselB, writes=[hselB])
        if fused:
            rsel, rselB = sb("rsel", [128, 4], F32)
            K.dma(rsel[:], io["rsel"][:, :], rselB, writes=[rselB])

        def stats_rstd(x_t, x_B, width, s_t, s_B):
            K.op("dve", lambda e: e.scalar_tensor_tensor(out=junk[:, 0:width], in0=x_t, scalar=1.0, in1=x_t,
                                                         op0=ALU.mult, op1=ALU.mult, accum_out=s_t[:, 0:1]),
                 reads=[x_B], writes=[junkB, s_B])
            K.op("pool", lambda e: e.tensor_scalar(out=s_t[:, 1:2], in0=s_t[:, 0:1], scalar1=1.0 / width, scalar2=EPS,
                                                   op0=ALU.mult, op1=ALU.add), reads=[s_B], writes=[s_B])
            K.op("pool", lambda e: e.tensor_tensor(out=s_t[:, 2:3], in0=s_t[:, 1:2], in1=mhalf[:, 0:1], op=ALU.pow),
                 reads=[s_B, mhalfB], writes=[s_B])

        def prep_a(x_t, x_B):
            s_t, s_B = rot(st1)
            stats_rstd(x_t[:], x_B, 1024, s_t, s_B)
            b_t, b_B = rot(xb)
            K.op("act", lambda e: e.activation(out=b_t[:], in_=x_t[:], func=AF.Copy), reads=[x_B], writes=[b_B])
            return b_t, b_B, s_t, s_B

        def prep_b(b_t, b_B, eng="act"):
            p_t, p_B = rot(pT)
            for k in range(8):
                K.op("pe", lambda e, k=k: e.transpose(out=p_t[:, k * 128:(k + 1) * 128],
                                                      in_=b_t[:, k * 128:(k + 1) * 128], identity=ident[:]),
                     reads=[b_B, identB], writes=[p_B])
            T_t, T_B = rot(xT)
            if eng == "act":
                K.op("act", lambda e: e.activation(out=T_t[:].rearrange("p c t -> p (c t)"), in_=p_t[:], func=AF.Copy),
                     reads=[p_B], writes=[T_B])
            else:
                K.op("dve", lambda e: e.tensor_copy(out=T_t[:].rearrange("p c t -> p (c t)"), in_=p_t[:]),
                     reads=[p_B], writes=[T_B])
            return T_t, T_B

        def prep_cast(x_t, x_B):
            b_t, b_B = rot(xb)
            K.op("act", lambda e: e.activation(out=b_t[:], in_=x_t[:], func=AF.Copy), reads=[x_B], writes=[b_B])
            return b_t, b_B

        def prep_stats(x_t, x_B):
            s_t, s_B = rot(st1)
            stats_rstd(x_t[:], x_B, 1024, s_t, s_B)
            return s_t, s_B

        def run_pipe(n, order, lo=0):
            md = max(d for d, _ in order)
            for step in range(n + md):
                for d, f in order:
                    idx = step - d
                    if 0 <= idx < n:
                        f(lo + idx)

        def prep_tile(x_t, x_B):
            b_t, b_B, s_t, s_B = prep_a(x_t, x_B)
            T_t, T_B = prep_b(b_t, b_B)
            return T_t, T_B, s_t, s_B

        def proj(T_t, T_B, w_t, w_B, groups, z_t, z_B, s_t, s_B, evac="dve"):
            for (c0, n) in groups:
                p_t, p_B = rot(pzs, "pzproj")
                for k in range(8):
                    K.op("pe", lambda e, k=k: e.matmul(p_t[:, 0:n], lhsT=T_t[:, k, :], rhs=w_t[:, k, c0:c0 + n],
                                                       start=(k == 0), stop=(k == 7)),
                         reads=[T_B, w_B], writes=[p_B])
                if evac == "act":
                    K.op("act", lambda e: e.activation(out=z_t[:, c0:c0 + n], in_=p_t[:, 0:n], func=AF.Copy, scale=s_t[:, 2:3]),
                         reads=[p_B, s_B], writes=[z_B])
                else:
                    K.op("dve", lambda e: e.tensor_scalar(out=z_t[:, c0:c0 + n], in0=p_t[:, 0:n], scalar1=s_t[:, 2:3],
                                                          scalar2=None, op0=ALU.mult),
                         reads=[p_B, s_B], writes=[z_B])

        def headnorm(z_t, z_B, c0, H, gain_ap):
            a_t, a_B = rot(t1)
            h_t, h_B = rot(hn)
            src = z_t[:, c0:c0 + 64 * H]
            src3 = src.rearrange("p (h d) -> p h d", h=H)
            for hh_ in range(H):
                K.op("dve", lambda e: e.scalar_tensor_tensor(
                    out=a_t[:, hh_ * 64:(hh_ + 1) * 64], in0=src[:, hh_ * 64:(hh_ + 1) * 64], scalar=1.0,
                    in1=src[:, hh_ * 64:(hh_ + 1) * 64], op0=ALU.mult, op1=ALU.mult, accum_out=h_t[:, hh_:hh_ + 1]),
                    reads=[z_B], writes=([a_B, h_B] if hh_ in (0, H - 1) else []))
            K.op("pool", lambda e: e.tensor_scalar(out=h_t[:, 0:H], in0=h_t[:, 0:H], scalar1=1.0 / 64, scalar2=EPS,
                                                   op0=ALU.mult, op1=ALU.add), reads=[h_B], writes=[h_B])
            K.op("pool", lambda e: e.tensor_tensor(out=h_t[:, 0:H], in0=h_t[:, 0:H], in1=mhalf[:, 0:H], op=ALU.pow),
                 reads=[h_B, mhalfB], writes=[h_B])
            K.op("dve", lambda e: e.tensor_tensor(out=src3, in0=src3,
                                                  in1=h_t[:, 0:H].unsqueeze(2).to_broadcast([128, H, 64]), op=ALU.mult),
                 reads=[z_B, h_B], writes=[z_B])
            K.op("dve", lambda e: e.tensor_tensor(out=src3, in0=src3,
                                                  in1=gain_ap.unsqueeze(1).to_broadcast([128, H, 64]), op=ALU.mult),
                 reads=[z_B, vecsB], writes=[z_B])

        def rope(z_t, z_B, c0, H, hs, tab_t, tab_B, tc0, dsts, dst_B):
            nseg = 32 // hs
            a_t, a_B = rot(t2)
            b_t, b_B = rot(t3)
            src = z_t[:, c0:c0 + 64 * H]
            src3 = src.rearrange("p (h d) -> p h d", h=H)
            cc = tab_t[:, tc0:tc0 + 64]
            ss = tab_t[:, tc0 + 64:tc0 + 128]
            K.op("dve", lambda e: e.tensor_tensor(out=a_t[:, 0:64 * H].rearrange("p (h d) -> p h d", h=H), in0=src3,
                                                  in1=cc.unsqueeze(1).to_broadcast([128, H, 64]), op=ALU.mult),
                 reads=[z_B, tab_B], writes=[a_B])
            s5 = src.rearrange("p (h s two k) -> p h s two k", h=H, s=nseg, two=2)
            b5 = b_t[:, 0:64 * H].rearrange("p (h s two k) -> p h s two k", h=H, s=nseg, two=2)
            ss4 = ss.rearrange("p (s two k) -> p s two k", s=nseg, two=2)
            for (o_, i_) in ((0, 1), (1, 0)):
                K.op("pool", lambda e, o_=o_, i_=i_: e.tensor_tensor(
                    out=b5[:, :, :, o_, :], in0=s5[:, :, :, i_, :],
                    in1=ss4[:, :, o_, :].unsqueeze(1).to_broadcast([128, H, nseg, hs]), op=ALU.mult),
                    reads=[z_B, tab_B], writes=[b_B])
            for (dst_ap, sel) in dsts:
                K.op("dve", lambda e, dst_ap=dst_ap, sel=sel: e.tensor_tensor(
                    out=dst_ap, in0=sel(a_t[:, 0:64 * H]), in1=sel(b_t[:, 0:64 * H]), op=ALU.add),
                    reads=[a_B, b_B], writes=[dst_B])

        def silu_gate(z_t, z_B, c0, n, dst_ap, dst_B):
            a_t, a_B = rot(t1)
            K.op("act", lambda e: e.activation(out=a_t[:, 0:n], in_=z_t[:, c0:c0 + n], func=AF.Exp, scale=-1.0),
                 reads=[z_B], writes=[a_B])
            K.op("act", lambda e: e.activation(out=a_t[:, 0:n], in_=a_t[:, 0:n], func=AF.Ln, scale=1.0, bias=1.0),
                 reads=[a_B], writes=[a_B])
            K.op("act", lambda e: e.activation(out=a_t[:, 0:n], in_=a_t[:, 0:n], func=AF.Exp, scale=-1.0),
                 reads=[a_B], writes=[a_B])
            K.op("dve", lambda e: e.tensor_tensor(out=dst_ap, in0=z_t[:, c0:c0 + n], in1=a_t[:, 0:n], op=ALU.mult),
                 reads=[a_B, z_B], writes=[dst_B])

        def transpose_multi(srcs, dsts):
            p_t, p_B = rot(pT)
            for i, (s_ap, s_B) in enumerate(srcs):
                K.op("pe", lambda e: e.transpose(out=p_t[:, i * 128:(i + 1) * 128], in_=s_ap, identity=ident[:]),
                     reads=[s_B, identB], writes=[p_B])
            for (dst_ap, dst_B, b0, nb_, eng) in dsts:
                src = p_t[:, b0 * 128:(b0 + nb_) * 128]
                if len(dst_ap.shape) == 3:
                    src = src.rearrange("p (a b) -> p a b", a=dst_ap.shape[1])
                if eng == "act":
                    K.op("act", lambda e: e.activation(out=dst_ap, in_=src, func=AF.Copy), reads=[p_B], writes=[dst_B])
                else:
                    K.op("dve", lambda e: e.tensor_copy(out=dst_ap, in_=src), reads=[p_B], writes=[dst_B])

        def transpose_to(src_aps, src_B, dst_ap, dst_B, eng="dve"):
            p_t, p_B = rot(pT)
            n = len(src_aps)
            for i, s_ap in enumerate(src_aps):
                K.op("pe", lambda e, i=i, s_ap=s_ap: e.transpose(out=p_t[:, i * 128:(i + 1) * 128], in_=s_ap,
                                                                 identity=ident[:]),
                     reads=[src_B, identB], writes=[p_B])
            src = p_t[:, 0:n * 128]
            if len(dst_ap.shape) == 3:
                src = src.rearrange("p (a b) -> p a b", a=dst_ap.shape[1])
            if eng == "act":
                K.op("act", lambda e: e.activation(out=dst_ap, in_=src, func=AF.Copy), reads=[p_B], writes=[dst_B])
            else:
                K.op("dve", lambda e: e.tensor_copy(out=dst_ap, in_=src), reads=[p_B], writes=[dst_B])

        def load_weights_chunk(l, c, pieces):
            groups, cur, used = [], [], 0
            for pc in pieces:
                if used + pc[1] > 1024:
                    groups.append(cur)
                    cur, used = [], 0
                cur.append(pc)
                used += pc[1]
            groups.append(cur)
            i = 0
            for grp_ in groups:
                w_t, w_B = rot(xs)
                off = 0
                offs = []
                for (c0, n, dst_ap, dst_B) in grp_:
                    K.dma(w_t[:, off:off + n], io["w_in"][l, c * 128:(c + 1) * 128, c0:c0 + n], w_B, writes=[w_B])
                    offs.append(off)
                    off += n
                for k_, (c0, n, dst_ap, dst_B) in enumerate(grp_):
                    o_ = offs[k_]
                    if (i + c) % 2 == 0:
                        K.op("act", lambda e: e.activation(out=dst_ap, in_=w_t[:, o_:o_ + n], func=AF.Copy, scale=gcol[:, c:c + 1]),
                             reads=[w_B, gcolB], writes=[dst_B])
                    else:
                        K.op("dve", lambda e: e.tensor_scalar(
                            out=dst_ap, in0=w_t[:, o_:o_ + n], scalar1=gcol[:, c:c + 1], scalar2=None, op0=ALU.mult),
                            reads=[w_B, gcolB], writes=[dst_B])
                    i += 1

        def load_x(src_ap, extra=()):
            x_t, x_B = rot(xs)
            K.dma(x_t[:], src_ap, x_B, reads=list(extra), writes=[x_B])
            return x_t, x_B

        def attn_pipe(n, qk, ex, pv, la, mids=None):
            for idx in range(n + la):
                if idx < n:
                    qk(idx)
                    ex(idx)
                if idx >= la:
                    pv(idx - la)
                if mids and idx in mids:
                    mids[idx]()

        for l in range(L):
            if l > 0:
                K.renew()
            x_src_full = io["xfull"] if l == 0 else io["xg_dst"]
            x_src_own = io["xown"] if l == 0 else io["x1own"]
            xfB = (lambda row: []) if l == 0 else (lambda row: [xgdB[row // 1024]])
            xoB = [] if l == 0 else [x1ownB]
            K.dma(vecs[:], io["vecs"][l, :, :], vecsB, writes=[vecsB])
            K.dma(gcol[:], io["gcol"][l, :, :], gcolB, writes=[gcolB])
            K.op("act", lambda e: e.activation(out=esink[:], in_=vecs[:, V_SINK:V_SINK + 4], func=AF.Exp),
                 reads=[vecsB], writes=[esinkB])
            K.op("dve", lambda e: e.tensor_tensor(out=junk[:, 0:32], in0=vecs[:, V_LQ1:V_LQ1 + 32],
                                                  in1=vecs[:, V_LK1:V_LK1 + 32], op=ALU.mult),
                 reads=[vecsB], writes=[junkB])
            K.op("dve", lambda e: e.tensor_reduce(out=lamt[:, 0:1], in_=junk[:, 0:32], axis=AX.X, op=ALU.add),
                 reads=[junkB], writes=[lamtB])
            K.op("dve", lambda e: e.tensor_tensor(out=junk[:, 32:64], in0=vecs[:, V_LQ2:V_LQ2 + 32],
                                                  in1=vecs[:, V_LK2:V_LK2 + 32], op=ALU.mult),
                 reads=[vecsB], writes=[junkB])
            K.op("dve", lambda e: e.tensor_reduce(out=lamt[:, 1:2], in_=junk[:, 32:64], axis=AX.X, op=ALU.add),
                 reads=[junkB], writes=[lamtB])
            K.op("act", lambda e: e.activation(out=lamt[:, 2:4], in_=lamt[:, 0:2], func=AF.Exp),
                 reads=[lamtB], writes=[lamtB])
            K.op("dve", lambda e: e.tensor_tensor(out=lamt[:, 4:5], in0=lamt[:, 2:3], in1=lamt[:, 3:4], op=ALU.subtract),
                 reads=[lamtB], writes=[lamtB])
            K.op("dve", lambda e: e.tensor_tensor(out=lamt[:, 4:5], in0=lamt[:, 4:5], in1=vecs[:, V_LC:V_LC + 1], op=ALU.add),
                 reads=[lamtB, vecsB], writes=[lamtB])
            K.op("dve", lambda e: e.tensor_scalar(out=lamt[:, 5:6], in0=lamt[:, 4:5], scalar1=-1.0, scalar2=None, op0=ALU.mult),
                 reads=[lamtB], writes=[lamtB])
            K.op("dve", lambda e: e.tensor_scalar(out=subg[:], in0=vecs[:, V_SUB:V_SUB + 64], scalar1=vecs[:, V_LC + 1:V_LC + 2],
                                                  scalar2=None, op0=ALU.mult), reads=[vecsB], writes=[subgB])

            with ExitStack() as ph:
              if '1' in PH:
                wBC, wBCB = sb("wBC", [128, 8, 1792], BF16, ph)
                kTb, kTbB = sb("kTb", [128, NT_LOC * 128], BF16, ph)
                vb, vbB = sb("vb", [128, NT_LOC, 2, 66], BF16, ph)
                kTc, kTcB = sb("kTc", [128, 2, NT_LOC * 128], BF16, ph)
                vc, vcB = sb("vc", [128, NT_LOC, 4, 66], BF16, ph)
                qTb, qTbB = sb("qTb", [128, NT_OWN, 2, 128], BF16, ph)
                qTc, qTcB = sb("qTc", [128, NT_OWN, 2, 128], BF16, ph)
                gbc, gbcB = sb("gbc", [128, NT_OWN, 512], BF16, ph)
                ctp = [sb("ctp%d" % i, [128, 512], F32, ph) for i in range(3)]
                zs = [sb("zs%d" % i, [128, 1792], F32, ph) for i in range(2)]
                finp = fin + fin2
                qtm = [sb("qtm%d" % i, [128, 512], BF16, ph) for i in range(3)]
                ktm = [sb("ktm%d" % i, [128, 384], BF16, ph) for i in range(3)]
                tmpc = [sb("tmpc%d" % i, [128, 512], F32, ph) for i in range(2)]

                K.op("pool", lambda e: e.memset(vb[:].rearrange("p a b c -> p (a b c)"), 1.0), writes=[vbB])
                K.op("pool", lambda e: e.memset(vc[:].rearrange("p a b c -> p (a b c)"), 1.0), writes=[vcB])
                for c in range(8):
                    load_weights_chunk(l, c, [(768, 1024, wBC[:, c, 0:1024], wBCB), (1792, 768, wBC[:, c, 1024:1792], wBCB)])

                order1 = list(range(2, 2 + NT_OWN)) + [0, 1, 18, 19]
                SP1 = {}

                def p1_s0(idx):
                    bt = order1[idx]
                    if 2 <= bt < 2 + NT_OWN:
                        x_t, x_B = load_x(x_src_own[(bt - 2) * 128:(bt - 1) * 128, :], xoB)
                    else:
                        prev = bt < 2
                        k0 = rr.get(id(xs), 0)
                        rr[id(xs)] = k0 + 3
                        x_t, x_B = xs[k0 % 3]
                        sc0 = 0 if prev else 3
                        for m in range(3):
                            h_t, h_B = xs[(k0 + 1 + (m % 2)) % 3]
                            row0 = (m + 1) * TOWN + ((bt - 2) * 128 if prev else (bt - 18) * 128)
                            h_v = h_t[:] if l == 0 else h_t[:].bitcast(BF16)[:, 0:1024]
                            K.dma(h_v, x_src_full[row0:row0 + 128, :], h_B, reads=xfB(row0), writes=[h_B])
                            if m == 0:
                                K.op("dve", lambda e: e.tensor_scalar(out=x_t[:], in0=h_v, scalar1=hsel[:, sc0:sc0 + 1],
                                                                      scalar2=None, op0=ALU.mult),
                                     reads=[h_B, hselB], writes=[x_B])
                            else:
                                K.op("dve", lambda e: e.scalar_tensor_tensor(
                                    out=x_t[:], in0=h_v, scalar=hsel[:, sc0 + m:sc0 + m + 1], in1=x_t[:],
                                    op0=ALU.mult, op1=ALU.add), reads=[h_B, hselB, x_B], writes=[x_B])
                    tb_t, tb_B = rot(tabt)
                    K.dma(tb_t[:], io["tabB"][bt * 128:(bt + 1) * 128, :], tb_B, writes=[tb_B], qn="act")
                    SP1[bt] = [x_t, x_B, tb_t, tb_B]

                def p1_cast(idx):
                    bt = order1[idx]
                    SP1[bt] += list(prep_cast(*SP1[bt][0:2]))

                def p1_stats(idx):
                    bt = order1[idx]
                    SP1[bt] += list(prep_stats(*SP1[bt][0:2]))

                def p1_b2(idx):
                    bt = order1[idx]
                    x_t, x_B, tb_t, tb_B, b_t, b_B, s_t, s_B = SP1[bt]
                    T_t, T_B = prep_b(b_t, b_B)
                    SP1[bt] = [tb_t, tb_B, T_t, T_B, s_t, s_B]

                def p1_c(idx):
                    bt = order1[idx]
                    tb_t, tb_B, T_t, T_B, s_t, s_B = SP1[bt]
                    z_t, z_B = rot(zs)
                    proj(T_t, T_B, wBC, wBCB, [(0, 512), (512, 512), (1024, 512), (1536, 256)], z_t, z_B, s_t, s_B)
                    SP1[bt] = (z_t, z_B, tb_t, tb_B)

                def p1_s1(idx):
                    bt = order1[idx]
                    z_t, z_B, tb_t, tb_B = SP1[bt]
                    own = 2 <= bt < 2 + NT_OWN
                    i_own = bt - 2
                    q_t, q_B = rot(qtm)
                    k_t, k_B = rot(ktm)
                    rope(z_t, z_B, 256, 2, 32, tb_t, tb_B, 0,
                         [(k_t[:, 0:128], lambda a: a)], k_B)
                    K.op("pool", lambda e: e.tensor_copy(out=k_t[:, 128:384], in_=z_t[:, 1024:1280]),
                         reads=[z_B], writes=[k_B])
                    K.op("act", lambda e: e.activation(out=vb[:, bt, :, 0:64],
                                                       in_=z_t[:, 384:512].rearrange("p (h d) -> p h d", h=2), func=AF.Copy),
                         reads=[z_B], writes=[vbB])
                    K.op("act", lambda e: e.activation(out=vc[:, bt, :, 0:64],
                                                       in_=z_t[:, 1280:1536].rearrange("p (h d) -> p h d", h=4), func=AF.Copy),
                         reads=[z_B], writes=[vcB])
                    if own:
                        dst = q_t[:, 0:256].rearrange("p (g kv d) -> p kv g d", g=2, kv=2)
                        rope(z_t, z_B, 0, 4, 32, tb_t, tb_B, 128,
                             [(dst, lambda a: a.rearrange("p (kv g d) -> p kv g d", kv=2, g=2))], q_B)
                        K.op("pool", lambda e: e.tensor_copy(out=q_t[:, 256:512], in_=z_t[:, 768:1024]),
                             reads=[z_B], writes=[q_B])
                        silu_gate(z_t, z_B, 512, 256, gbc[:, i_own, 0:256], gbcB)
                        silu_gate(z_t, z_B, 1536, 256, gbc[:, i_own, 256:512], gbcB)
                    SP1[bt] = (q_t, q_B, k_t, k_B)

                def p1_s2(idx):
                    bt = order1[idx]
                    q_t, q_B, k_t, k_B = SP1.pop(bt)
                    own = 2 <= bt < 2 + NT_OWN
                    i_own = bt - 2
                    srcs = [(k_t[:, 0:128], k_B), (k_t[:, 128:256], k_B), (k_t[:, 256:384], k_B)]
                    dsts = [(kTb[:, bt * 128:(bt + 1) * 128], kTbB, 0, 1, "act"),
                            (kTc[:, 0, bt * 128:(bt + 1) * 128], kTcB, 1, 1, "act"),
                            (kTc[:, 1, bt * 128:(bt + 1) * 128], kTcB, 2, 1, "act")]
                    if own:
                        srcs += [(q_t[:, k_ * 128:(k_ + 1) * 128], q_B) for k_ in range(4)]
                        dsts += [(qTb[:, i_own, :, :].rearrange("p g q -> p (g q)"), qTbB, 3, 2, "act"),
                                 (qTc[:, i_own, :, :].rearrange("p g q -> p (g q)"), qTcB, 5, 2, "act")]
                    transpose_multi(srcs, dsts)

                def run_p1_step1(lo, hi):
                    run_pipe(hi - lo, [(1, p1_cast), (1, p1_stats), (2, p1_b2), (3, p1_c), (4, p1_s1), (5, p1_s2), (0, p1_s0)], lo)

                def p1_attn(i):
                    acc_t, acc_B = pz[4]
                    accv = acc_t[:, 0:260].rearrange("p (a b) -> p a b", b=65)
                    wins = [i + 1, i + 2, i + 3]
                    mids = [0 if i == 0 else 1, None, 3 if i == NT_OWN - 1 else 2]
                    slots = {}

                    def qkB(idx, i=i, wins=wins, slots=slots):
                        j = wins[idx]
                        banks = [rot(pzs, "pzst"), rot(pzs, "pzst")]
                        slots[idx] = banks
                        for kv in range(2):
                            s_t, s_B = banks[kv]
                            K.op("pe", lambda e: e.matmul(
                                s_t[:, 0:256], lhsT=kTb[kv * 64:(kv + 1) * 64, j * 128:(j + 1) * 128],
                                rhs=qTb[kv * 64:(kv + 1) * 64, i, :, :].rearrange("p g q -> p (g q)"),
                                start=True, stop=True), reads=[kTbB, qTbB], writes=[s_B])

                    def exB(idx, mids=mids, slots=slots):
                        banks = slots[idx]
                        p_t, p_B = rot(PT)
                        slots[idx] = (p_t, p_B)
                        for kv in range(2):
                            s_t, s_B = banks[kv]
                            K.op("act", lambda e: e.activation(out=p_t[:, kv * 256:(kv + 1) * 256], in_=s_t[:, 0:256], func=AF.Exp),
                                 reads=[s_B], writes=[p_B])
                        if mids[idx] is not None:
                            mi = mids[idx]
                            K.op("pool", lambda e: e.tensor_tensor(
                                out=p_t[:].rearrange("p (h q) -> p h q", h=4), in0=p_t[:].rearrange("p (h q) -> p h q", h=4),
                                in1=bmask[:, mi, :].unsqueeze(1).to_broadcast([128, 4, 128]), op=ALU.mult),
                                reads=[p_B, bmaskB], writes=[p_B])

                    def pvB(idx, wins=wins, slots=slots, accv=accv, acc_B=acc_B):
                        j = wins[idx]
                        p_t, p_B = slots[idx]
                        for kv in range(2):
                            for g in range(2):
                                h = 2 * kv + g
                                K.op("pe", lambda e, kv=kv, g=g, h=h: e.matmul(
                                    accv[:, h, :], lhsT=p_t[:, (kv * 2 + g) * 128:(kv * 2 + g + 1) * 128],
                                    rhs=vb[:, j, kv, 0:65], start=(idx == 0 and h == 0), stop=(idx == 2 and h == 3)),
                                    reads=[p_B, vbB], writes=[acc_B])

                    if 'B' in PH:
                        attn_pipe(3, qkB, exB, pvB, 1)
                    r_t, r_B = rot(rd)
                    K.op("dve", lambda e: e.tensor_tensor(out=r_t[:, 0:4], in0=accv[:, :, 64], in1=esink[:], op=ALU.add),
                         reads=[acc_B, esinkB], writes=[r_B])
                    K.op("dve", lambda e: e.reciprocal(out=r_t[:, 0:4], in_=r_t[:, 0:4]), reads=[r_B], writes=[r_B])
                    fb_t, fb_B = rot(finp)
                    K.op("dve", lambda e: e.tensor_tensor(out=fb_t[:], in0=accv[:, :, 0:64],
                                                          in1=r_t[:, 0:4].unsqueeze(2).to_broadcast([128, 4, 64]), op=ALU.mult),
                         reads=[acc_B, r_B], writes=[fb_B])
                    if i == 0:
                        cwin = list(range(0, 6))
                    elif i == NT_OWN - 1:
                        cwin = list(range(14, 20))
                    else:
                        cwin = list(range(i, i + 5))
                    spi = {0: 0, 1: 1, NT_OWN - 2: 2, NT_OWN - 1: 3}.get(i, None)
                    acc2_t, acc2_B = pz[5]
                    accv2 = acc2_t[:, 0:260].rearrange("p (a b) -> p a b", b=65)
                    slots2 = {}
                    nw = len(cwin)

                    def qkC(idx, i=i, cwin=cwin, slots2=slots2):
                        j = cwin[idx]
                        banks = [rot(pzs, "pzst"), rot(pzs, "pzst")]
                        slots2[idx] = banks
                        for h in range(4):
                            p_, hh = h // 2, h % 2
                            s_t, s_B = banks[hh]
                            K.op("pe", lambda e: e.matmul(
                                s_t[:, p_ * 128:(p_ + 1) * 128], lhsT=kTc[hh * 64:(hh + 1) * 64, p_, j * 128:(j + 1) * 128],
                                rhs=qTc[hh * 64:(hh + 1) * 64, i, p_, :], start=True, stop=True),
                                reads=[kTcB, qTcB], writes=[s_B])

                    def exC(idx, spi=spi, slots2=slots2):
                        banks = slots2[idx]
                        tb_t, tb_B = rot(ctp)
                        if spi is None:
                            K.dma(tb_t[:], io["cb_int"][l, :, idx * 512:(idx + 1) * 512], tb_B, writes=[tb_B], qn="pool")
                        else:
                            K.dma(tb_t[:], io["cb_sp"][l, spi, :, idx * 512:(idx + 1) * 512], tb_B, writes=[tb_B], qn="pool")
                        c_t, c_B = rot(tmpc)
                        for hh in range(2):
                            s_t, s_B = banks[hh]
                            tv = tb_t[:].rearrange("p (pp hh q) -> p hh pp q", pp=2, hh=2)[:, hh]
                            cv = c_t[:].rearrange("p (pp hh q) -> p hh pp q", pp=2, hh=2)[:, hh]
                            K.op("dve", lambda e: e.scalar_tensor_tensor(
                                out=cv, in0=s_t[:, 0:256].rearrange("p (pp q) -> p pp q", pp=2), scalar=0.125, in1=tv,
                                op0=ALU.mult, op1=ALU.add), reads=[s_B, tb_B], writes=[c_B])
                        p_t, p_B = rot(PT)
                        slots2[idx] = (p_t, p_B)
                        K.op("act", lambda e: e.activation(out=p_t[:], in_=c_t[:], func=AF.Exp), reads=[c_B], writes=[p_B])

                    def pvC(idx, cwin=cwin, slots2=slots2, accv2=accv2, acc2_B=acc2_B, nw=nw):
                        j = cwin[idx]
                        p_t, p_B = slots2[idx]
                        for h in range(4):
                            K.op("pe", lambda e, h=h: e.matmul(
                                accv2[:, h, :], lhsT=p_t[:, h * 128:(h + 1) * 128], rhs=vc[:, j, h, 0:65],
                                start=(idx == 0 and h == 0), stop=(idx == nw - 1 and h == 3)), reads=[p_B, vcB], writes=[acc2_B])

                    if 'C' in PH:
                        attn_pipe(nw, qkC, exC, pvC, 1)
                    r_t, r_B = rot(rd)
                    K.op("dve", lambda e: e.reciprocal(out=r_t[:, 0:4], in_=accv2[:, :, 64]), reads=[acc2_B], writes=[r_B])
                    fc_t, fc_B = rot(finp)
                    K.op("dve", lambda e: e.tensor_tensor(out=fc_t[:], in0=accv2[:, :, 0:64],
                                                          in1=r_t[:, 0:4].unsqueeze(2).to_broadcast([128, 4, 64]), op=ALU.mult),
                         reads=[acc2_B, r_B], writes=[fc_B])
                    SY[i] = (fb_t, fb_B, fc_t, fc_B)

                def p1_fin(i):
                    fb_t, fb_B, fc_t, fc_B = SY.pop(i)
                    m_t, m_B = rot(mixtm)
                    K.op("pool", lambda e: e.tensor_tensor(out=m_t[:, 0, :], in0=fb_t[:].rearrange("p h d -> p (h d)"),
                                                           in1=gbc[:, i, 0:256], op=ALU.mult),
                         reads=[fb_B, gbcB], writes=[m_B])
                    K.op("pool", lambda e: e.tensor_tensor(out=m_t[:, 1, :], in0=fc_t[:].rearrange("p h d -> p (h d)"),
                                                           in1=gbc[:, i, 256:512], op=ALU.mult),
                         reads=[fc_B, gbcB], writes=[m_B])
                    transpose_to([m_t[:, 0, 0:128], m_t[:, 0, 128:256], m_t[:, 1, 0:128], m_t[:, 1, 128:256]], m_B,
                                 mixT[:, 2:6, i * 128:(i + 1) * 128], mixTB)

                def run_p1_attn(tiles):
                    for k_, i in enumerate(tiles):
                        p1_attn(i)
                        if k_ >= 1:
                            p1_fin(tiles[k_ - 1])
                    p1_fin(tiles[-1])

                SY = {}
                run_p1_step1(0, NT_OWN)
                run_p1_attn(list(range(2, NT_OWN - 2)))
                run_p1_step1(NT_OWN, NT_LOC)
                run_p1_attn([0, 1, NT_OWN - 2, NT_OWN - 1])
                K.barrier()

            with ExitStack() as ph:
              if '2' in PH:
                wKV, wKVB = sb("wKV", [128, 8, 512], BF16, ph)
                wQG, wQGB = sb("wQG", [128, 8, 1024], BF16, ph)
                kTa, kTaB = sb("kTa", [128, SEQ], BF16, ph)
                va, vaB = sb("va", [128, NT_ALL, 2, 66], BF16, ph)
                kTd, kTdB = sb("kTd", [128, SEQ], BF16, ph)
                vd, vdB = sb("vd", [128, NT_ALL, 2, 66], BF16, ph)
                qTa = [sb("qTa%d" % i, [128, 2, 512], BF16, ph) for i in range(2)]
                qTd = [sb("qTd%d" % i, [128, 2, 2, 512], BF16, ph) for i in range(2)]
                gads = [sb("gad%d" % i, [128, 4, 512], GDT, ph) for i in range(2)]
                qa_tm = [sb("qatm%d" % i, [128, 256], BF16, ph) for i in range(2)]
                qd_tm = [sb("qdtm%d" % i, [128, 2, 256], BF16, ph) for i in range(2)]
                k2_tm = [sb("k2tm%d" % i, [128, 256], BF16, ph) for i in range(2)]
                zs = [sb("zs%d" % i, [128, 1024], F32, ph) for i in range(2)]

                K.op("pool", lambda e: e.memset(va[:].rearrange("p a b c -> p (a b c)"), 1.0), writes=[vaB])
                K.op("pool", lambda e: e.memset(vd[:].rearrange("p a b c -> p (a b c)"), 1.0), writes=[vdB])
                for i in range(2):
                    K.op("pool", lambda e, i=i: e.memset(qd_tm[i][0][:].rearrange("p a b -> p (a b)"), 0.0),
                         writes=[qd_tm[i][1]])
                for c in range(8):
                    load_weights_chunk(l, c, [
                        (256, 256, wKV[:, c, 0:256], wKVB), (2816, 256, wKV[:, c, 256:512], wKVB),
                        (0, 256, wQG[:, c, 0:256], wQGB), (512, 256, wQG[:, c, 256:512], wQGB),
                        (2560, 256, wQG[:, c, 512:768], wQGB), (3072, 256, wQG[:, c, 768:1024], wQGB)])


                def staged(n, stages):
                    ns = len(stages)
                    for step in range(n + ns - 1):
                        for si in reversed(range(ns)):
                            idx = step - si
                            if 0 <= idx < n:
                                stages[si](idx)

                S1 = {}

                kvx = xb + [(xs[i_][0][:].bitcast(BF16)[:, 0:1024], xs[i_][1]) for i_ in range(3)]

                def kv_a(t):
                    if l == 0:
                        x_t, x_B = load_x(x_src_full[t * 128:(t + 1) * 128, :], xfB(t * 128))
                    else:
                        x_t, x_B = rot(kvx)
                        K.dma(x_t[:], x_src_full[t * 128:(t + 1) * 128, :], x_B, reads=xfB(t * 128), writes=[x_B])
                    tb_t, tb_B = rot(tabt)
                    K.dma(tb_t[:], io["tabKV"][t * 128:(t + 1) * 128, :], tb_B, writes=[tb_B], qn="act")
                    S1[t] = [x_t, x_B, tb_t, tb_B]

                def kv_cast(t):
                    x_t, x_B = S1[t][0:2]
                    S1[t] += list(prep_cast(x_t, x_B)) if l == 0 else [x_t, x_B]

                def kv_stats(t):
                    x_t, x_B = S1[t][0:2]
                    S1[t] += list(prep_stats(x_t, x_B))

                def kv_b2(t):
                    x_t, x_B, tb_t, tb_B, b_t, b_B, s_t, s_B = S1[t]
                    T_t, T_B = prep_b(b_t, b_B)
                    S1[t] = [x_t, x_B, tb_t, tb_B, T_t, T_B, s_t, s_B]

                def kv_c(t):
                    x_t, x_B, tb_t, tb_B, T_t, T_B, s_t, s_B = S1[t]
                    z_t, z_B = rot(zs)
                    proj(T_t, T_B, wKV, wKVB, [(0, 512)], z_t, z_B, s_t, s_B, evac="act")
                    S1[t] = (z_t, z_B, tb_t, tb_B)

                def kv_d(t):
                    z_t, z_B, tb_t, tb_B = S1[t]
                    k_t, k_B = rot(k2_tm)
                    headnorm(z_t, z_B, 0, 2, vecs[:, V_KN:V_KN + 64])
                    rope(z_t, z_B, 0, 2, 16, tb_t, tb_B, 0, [(k_t[:, 0:128], lambda a: a)], k_B)
                    rope(z_t, z_B, 256, 2, 16, tb_t, tb_B, 128, [(k_t[:, 128:256], lambda a: a)], k_B)
                    K.op("act", lambda e: e.activation(out=va[:, t, :, 0:64],
                                                       in_=z_t[:, 128:256].rearrange("p (h d) -> p h d", h=2), func=AF.Copy),
                         reads=[z_B], writes=[vaB])
                    K.op("act", lambda e: e.activation(out=vd[:, t, :, 0:64],
                                                       in_=z_t[:, 384:512].rearrange("p (h d) -> p h d", h=2), func=AF.Copy),
                         reads=[z_B], writes=[vdB])
                    S1[t] = (k_t, k_B)

                def kv_e(t):
                    k_t, k_B = S1.pop(t)
                    transpose_multi([(k_t[:, 0:128], k_B), (k_t[:, 128:256], k_B)],
                                    [(kTa[:, t * 128:(t + 1) * 128], kTaB, 0, 1, "act"),
                                     (kTd[:, t * 128:(t + 1) * 128], kTdB, 1, 1, "act")])

                run_pipe(NT_ALL, [(0, kv_a), (1, kv_cast), (1, kv_stats), (2, kv_b2), (3, kv_c), (4, kv_d), (5, kv_e)])

                def make_qproj(grp_):
                    qa_t_, qa_B_ = qTa[grp_ % 2]
                    qd_t_, qd_B_ = qTd[grp_ % 2]
                    gad_, gadB_ = gads[grp_ % 2]
                    S2 = {}

                    def st0(qt):
                        ti = grp_ * 4 + qt
                        x_t, x_B = load_x(x_src_own[ti * 128:(ti + 1) * 128, :], xoB)
                        tb_t, tb_B = rot(tabt)
                        K.dma(tb_t[:], io["tabQ"][ti * 128:(ti + 1) * 128, :], tb_B, writes=[tb_B], qn="act")
                        b_t, b_B, s_t, s_B = prep_a(x_t, x_B)
                        S2[qt] = (tb_t, tb_B, b_t, b_B, s_t, s_B)

                    def st1a(qt):
                        tb_t, tb_B, b_t, b_B, s_t, s_B = S2[qt]
                        T_t, T_B = prep_b(b_t, b_B, eng="dve")
                        S2[qt] = (tb_t, tb_B, T_t, T_B, s_t, s_B)

                    def st1b(qt):
                        tb_t, tb_B, T_t, T_B, s_t, s_B = S2[qt]
                        z_t, z_B = rot(zs)
                        proj(T_t, T_B, wQG, wQGB, [(0, 512), (512, 512)], z_t, z_B, s_t, s_B)
                        S2[qt] = (z_t, z_B, tb_t, tb_B)

                    def st2(qt):
                        z_t, z_B, tb_t, tb_B = S2[qt]
                        silu_gate(z_t, z_B, 256, 256, gad_[:, qt, 0:256], gadB_)
                        silu_gate(z_t, z_B, 768, 256, gad_[:, qt, 256:512], gadB_)
                        headnorm(z_t, z_B, 0, 4, vecs[:, V_QN:V_QN + 64])
                        a_t, a_B = rot(qa_tm)
                        rope(z_t, z_B, 0, 4, 16, tb_t, tb_B, 0,
                             [(a_t[:, 0:256].rearrange("p (g kv d) -> p kv g d", g=2, kv=2),
                               lambda a: a.rearrange("p (kv g d) -> p kv g d", kv=2, g=2))], a_B)
                        d_t, d_B = rot(qd_tm)
                        dsts = []
                        for c in range(2):
                            dsts.append((
                                d_t[:, c, :].rearrange("p (g kv c k) -> p kv g c k", g=2, kv=2, c=2)[:, :, :, c, :],
                                lambda a, c=c: a.rearrange("p (kv g c k) -> p kv g c k", kv=2, g=2, c=2)[:, :, :, c, :]))
                        rope(z_t, z_B, 512, 4, 16, tb_t, tb_B, 128, dsts, d_B)
                        S2[qt] = (a_t, a_B, d_t, d_B)

                    def st3(qt):
                        a_t, a_B, d_t, d_B = S2.pop(qt)
                        transpose_multi(
                            [(a_t[:, 0:128], a_B), (a_t[:, 128:256], a_B), (d_t[:, 0, 0:128], d_B), (d_t[:, 0, 128:256], d_B),
                             (d_t[:, 1, 0:128], d_B), (d_t[:, 1, 128:256], d_B)],
                            [(qa_t_[:, :, qt * 128:(qt + 1) * 128], qa_B_, 0, 2, "dve"),
                             (qd_t_[:, 0, :, qt * 128:(qt + 1) * 128], qd_B_, 2, 2, "dve"),
                             (qd_t_[:, 1, :, qt * 128:(qt + 1) * 128], qd_B_, 4, 2, "dve")])

                    def boundary(b_):
                        items = []
                        for si, fn in ((1, st1a), (3, st3), (1, st1b), (2, st2), (0, st0)):
                            qt = b_ - si
                            if 0 <= qt < 4:
                                items.append(lambda fn=fn, qt=qt: fn(qt))
                        return items
                    return boundary

                nxt = make_qproj(0)
                for b_ in range(7):
                    for it_ in nxt(b_):
                        it_()
                deferred = []
                for grp in range(4):
                    qa_t, qa_B = qTa[grp % 2]
                    qd_t, qd_B = qTd[grp % 2]
                    gad, gadB = gads[grp % 2]
                    nxt = make_qproj(grp + 1) if grp < 3 else (lambda b_: [])
                    bcount = [0]
                    pending = []

                    def loop_sched():
                        pending.extend(deferred)
                        del deferred[:]
                        pending.extend(nxt(bcount[0]))
                        bcount[0] += 1

                    def tick():
                        if pending:
                            pending.pop(0)()

                    def drain():
                        while pending:
                            pending.pop(0)()
                    mids = {i_: tick for i_ in range(4, 62, 2)}

                    ma_t, ma_B = rot(mixtm)
                    for g in range(2):
                        accs = [pz[4], pz[5]]
                        accvs = [a_[0][:, 0:260].rearrange("p (a b) -> p a b", b=65) for a_ in accs]
                        slots = {}

                        def qkA(j):
                            pi = rot([0, 1], "pzpair")
                            banks = [pzs[2 * pi], pzs[2 * pi + 1]]
                            slots[j] = (pi, banks)
                            for kv in range(2):
                                s_t, s_B = banks[kv]
                                K.op("pe", lambda e: e.matmul(s_t[:], lhsT=kTa[kv * 64:(kv + 1) * 64, j * 128:(j + 1) * 128],
                                                              rhs=qa_t[kv * 64:(kv + 1) * 64, g, :], start=True, stop=True),
                                     reads=[kTaB, qa_B], writes=[s_B])

                        def exA(j):
                            pi, banks = slots[j]
                            qi = rot([0, 1, 2], "ptpair")
                            pts = [PT[2 * qi], PT[2 * qi + 1]]
                            K.op("act", lambda e: e.activation(out=PT2[qi][:], in_=pzd[pi][:], func=AF.Exp),
                                 reads=[banks[0][1], banks[1][1]], writes=[pts[0][1], pts[1][1]])
                            slots[j] = pts

                        def pvA(j):
                            pts = slots.pop(j)
                            for kv in range(2):
                                p_t, p_B = pts[kv]
                                for qt in range(4):
                                    K.op("pe", lambda e: e.matmul(accvs[kv][:, qt, :], lhsT=p_t[:, qt * 128:(qt + 1) * 128],
                                                                  rhs=va[:, j, kv, 0:65], start=(j == 0 and qt == 0),
                                                                  stop=(j == NT_ALL - 1 and qt == 3)),
                                         reads=[p_B, vaB], writes=[accs[kv][1]])

                        loop_sched()
                        attn_pipe(NT_ALL, qkA, exA, pvA, 2, mids)
                        drain()
                        for kv in range(2):
                            h = 2 * kv + g
                            accv, acc_B = accvs[kv], accs[kv][1]
                            r_t, r_B = rot(rd)
                            K.op("dve", lambda e: e.reciprocal(out=r_t[:, 0:4], in_=accv[:, :, 64]), reads=[acc_B], writes=[r_B])
                            f_t, f_B = rot(fin)
                            K.op("dve", lambda e: e.tensor_tensor(out=f_t[:], in0=accv[:, :, 0:64],
                                                                  in1=r_t[:, 0:4].unsqueeze(2).to_broadcast([128, 4, 64]),
                                                                  op=ALU.mult), reads=[acc_B, r_B], writes=[f_B])
                            K.op("pool", lambda e: e.tensor_tensor(out=ma_t[:, :, h * 64:(h + 1) * 64], in0=f_t[:],
                                                                   in1=gad[:, :, h * 64:(h + 1) * 64], op=ALU.mult),
                                 reads=[f_B, gadB], writes=[ma_B])
                    for qt in range(4):
                        deferred.append(lambda ma_t=ma_t, ma_B=ma_B, grp=grp, qt=qt: transpose_to(
                            [ma_t[:, qt, 0:128], ma_t[:, qt, 128:256]], ma_B,
                            mixT[:, 0:2, (grp * 4 + qt) * 128:(grp * 4 + qt + 1) * 128], mixTB))

                    md_t, md_B = rot(mixtm)
                    for half in range(2):
                        for g in range(2):
                            accs = [pz[4], pz[5]]
                            accvs = [a_[0][:, 0:260].rearrange("p (c a b) -> p c a b", c=2, b=65) for a_ in accs]
                            slots = {}

                            def qkD(j):
                                pi = rot([0, 1], "pzpair")
                                banks = [pzs[2 * pi], pzs[2 * pi + 1]]
                                slots[j] = (pi, banks)
                                for c in range(2):
                                    for kv in range(2):
                                        s_t, s_B = banks[kv]
                                        K.op("pe", lambda e: e.matmul(
                                            s_t[:, c * 256:(c + 1) * 256], lhsT=kTd[kv * 64:(kv + 1) * 64, j * 128:(j + 1) * 128],
                                            rhs=qd_t[kv * 64:(kv + 1) * 64, c, g, half * 256:(half + 1) * 256],
                                            start=True, stop=True), reads=[kTdB, qd_B], writes=[s_B])

                            def exD(j):
                                pi, banks = slots[j]
                                qi = rot([0, 1, 2], "ptpair")
                                pts = [PT[2 * qi], PT[2 * qi + 1]]
                                K.op("act", lambda e: e.activation(out=PT2[qi][:], in_=pzd[pi][:], func=AF.Exp),
                                     reads=[banks[0][1], banks[1][1]], writes=[pts[0][1], pts[1][1]])
                                slots[j] = pts

                            def pvD(j):
                                pts = slots.pop(j)
                                for kv in range(2):
                                    p_t, p_B = pts[kv]
                                    for c in range(2):
                                        for qt in range(2):
                                            K.op("pe", lambda e: e.matmul(
                                                accvs[kv][:, c, qt, :], lhsT=p_t[:, c * 256 + qt * 128:c * 256 + (qt + 1) * 128],
                                                rhs=vd[:, j, kv, 0:65], start=(j == 0 and c == 0 and qt == 0),
                                                stop=(j == NT_ALL - 1 and c == 1 and qt == 1)),
                                                reads=[p_B, vdB], writes=[accs[kv][1]])

                            loop_sched()
                            if half == 1 and g == 1:
                                loop_sched()
                            attn_pipe(NT_ALL, qkD, exD, pvD, 2, mids)
                            drain()
                            eps_ = []
                            for kv in range(2):
                                h = 2 * kv + g
                                av, acc_B = accvs[kv], accs[kv][1]
                                r_t, r_B = rot(rd)
                                K.op("dve", lambda e: e.reciprocal(out=r_t[:, 0:4].rearrange("p (c a) -> p c a", c=2), in_=av[:, :, :, 64]),
                                     reads=[acc_B], writes=[r_B])
                                K.op("dve", lambda e: e.tensor_scalar(out=r_t[:, 2:4], in0=r_t[:, 2:4], scalar1=lamt[:, 5:6],
                                                                      scalar2=None, op0=ALU.mult), reads=[r_B, lamtB], writes=[r_B])
                                f_t, f_B = rot(fin)
                                g_t, g_B = rot(fin2)
                                K.op("dve", lambda e: e.tensor_tensor(out=f_t[:], in0=av[:, :, :, 0:64].rearrange("p c a d -> p (c a) d"),
                                                                      in1=r_t[:, 0:4].unsqueeze(2).to_broadcast([128, 4, 64]),
                                                                      op=ALU.mult), reads=[acc_B, r_B], writes=[f_B])
                                eps_.append((h, f_t, f_B, g_t, g_B))
                            for (h, f_t, f_B, g_t, g_B) in eps_:
                                K.op("pool", lambda e: e.tensor_tensor(out=f_t[:, 0:2, :], in0=f_t[:, 0:2, :], in1=f_t[:, 2:4, :], op=ALU.add),
                                     reads=[f_B], writes=[f_B])
                                K.op("pool", lambda e: e.tensor_tensor(out=g_t[:, 0:2, :], in0=f_t[:, 0:2, :], in1=f_t[:, 0:2, :], op=ALU.mult),
                                     reads=[f_B], writes=[g_B])
                                h_t, h_B = rot(hn)
                                K.op("dve", lambda e: e.tensor_reduce(out=h_t[:, 0:2], in_=g_t[:, 0:2, :], axis=AX.X, op=ALU.add),
                                     reads=[g_B], writes=[h_B])
                                K.op("pool", lambda e: e.tensor_scalar(out=h_t[:, 0:2], in0=h_t[:, 0:2], scalar1=1.0 / 64, scalar2=EPS,
                                                                       op0=ALU.mult, op1=ALU.add), reads=[h_B], writes=[h_B])
                                K.op("pool", lambda e: e.tensor_tensor(out=h_t[:, 0:2], in0=h_t[:, 0:2], in1=mhalf[:, 0:2], op=ALU.pow),
                                     reads=[h_B, mhalfB], writes=[h_B])
                                K.op("dve", lambda e: e.tensor_tensor(out=f_t[:, 0:2, :], in0=f_t[:, 0:2, :],
                                                                      in1=h_t[:, 0:2].unsqueeze(2).to_broadcast([128, 2, 64]), op=ALU.mult),
                                     reads=[f_B, h_B], writes=[f_B])
                                K.op("dve", lambda e: e.tensor_tensor(out=f_t[:, 0:2, :], in0=f_t[:, 0:2, :],
                                                                      in1=subg[:].unsqueeze(1).to_broadcast([128, 2, 64]), op=ALU.mult),
                                     reads=[f_B, subgB], writes=[f_B])
                                K.op("pool", lambda e: e.tensor_tensor(
                                    out=md_t[:, half * 2:half * 2 + 2, h * 64:(h + 1) * 64], in0=f_t[:, 0:2, :],
                                    in1=gad[:, half * 2:half * 2 + 2, 256 + h * 64:256 + (h + 1) * 64], op=ALU.mult),
                                    reads=[f_B, gadB], writes=[md_B])
                    for qt in range(4):
                        deferred.append(lambda md_t=md_t, md_B=md_B, grp=grp, qt=qt: transpose_to(
                            [md_t[:, qt, 0:128], md_t[:, qt, 128:256]], md_B,
                            mixT[:, 6:8, (grp * 4 + qt) * 128:(grp * 4 + qt + 1) * 128], mixTB))
                while deferred:
                    deferred.pop(0)()
                K.barrier()

            with ExitStack() as ph:
              if '3' in PH:
                wo, woB = sb("wo", [128, 8, 1024], BF16, ph)
                fg, fgB = sb("fg", [128, 1024], F32, ph)
                xn = [sb("xn%d" % i, [128, 1024], F32, ph) for i in range(2)]
                yo = [sb("yo%d" % i, [128, 1024], F32, ph) for i in range(2)]
                yb = [sb("yb%d" % i, [128, 1024], BF16, ph) for i in range(4)]
                K.dma(fg[:], io["fg"][:, :], fgB, writes=[fgB])
                for c in range(8):
                    w_t, w_B = rot(xs)
                    K.dma(w_t[:], io["w_out"][l, c * 128:(c + 1) * 128, :], w_B, writes=[w_B])
                    if c % 2:
                        K.op("act", lambda e, c=c: e.activation(out=wo[:, c, :], in_=w_t[:], func=AF.Copy), reads=[w_B], writes=[woB])
                    else:
                        K.op("dve", lambda e, c=c: e.tensor_copy(out=wo[:, c, :], in_=w_t[:]), reads=[w_B], writes=[woB])
                for ti in range(NT_OWN):
                    x_t, x_B = load_x(x_src_own[ti * 128:(ti + 1) * 128, :], xoB)
                    n_t, n_B = rot(xn)
                    for n in range(2):
                        p_t, p_B = rot(pzs, "pzproj")
                        for c in range(8):
                            K.op("pe", lambda e, c=c, n=n: e.matmul(p_t[:], lhsT=mixT[:, c, ti * 128:(ti + 1) * 128],
                                                                    rhs=wo[:, c, n * 512:(n + 1) * 512],
                                                                    start=(c == 0), stop=(c == 7)),
                                 reads=[mixTB, woB], writes=[p_B])
                        K.op("dve", lambda e, n=n: e.tensor_tensor(out=n_t[:, n * 512:(n + 1) * 512], in0=p_t[:],
                                                                   in1=x_t[:, n * 512:(n + 1) * 512], op=ALU.add),
                             reads=[p_B, x_B], writes=[n_B])
                    if not fused:
                        K.dma(io["xnext"][ti * 128:(ti + 1) * 128, :], n_t[:], n_B, reads=[n_B])
                    elif l < L - 1:
                        K.dma(io["x1own"][ti * 128:(ti + 1) * 128, :], n_t[:], n_B, reads=[n_B], writes=[x1ownB])
                        for m in range(4):
                            y_t, y_B = rot(yb)
                            if m % 2:
                                K.op("act", lambda e: e.activation(out=y_t[:], in_=n_t[:], func=AF.Copy, scale=rsel[:, m:m + 1]),
                                     reads=[n_B, rselB], writes=[y_B])
                            else:
                                K.op("dve", lambda e: e.tensor_scalar(
                                    out=y_t[:], in0=n_t[:], scalar1=rsel[:, m:m + 1], scalar2=None, op0=ALU.mult),
                                    reads=[n_B, rselB], writes=[y_B])
                            K.dma(io["xg_src"][m * TOWN + ti * 128:m * TOWN + (ti + 1) * 128, :], y_t[:], y_B,
                                  reads=[y_B], writes=[xgsB[(m * TOWN + ti * 128) // 1024]], qn=("act" if m % 2 == 0 else "pool"))
                        if ti % 8 == 7:
                            for m in range(4):
                                ch = m * 2 + ti // 8
                                K.collective("AllReduce", ALU.add, [[0, 1, 2, 3], [4, 5, 6, 7]],
                                             io["xg_src"][ch * 1024:(ch + 1) * 1024, :], io["xg_dst"][ch * 1024:(ch + 1) * 1024, :],
                                             xgsB[ch], xgdB[ch])
                    if l == L - 1:
                        s_t, s_B = rot(st1)
                        stats_rstd(n_t[:], n_B, 1024, s_t, s_B)
                        y_t, y_B = rot(yo)
                        K.op("dve", lambda e: e.scalar_tensor_tensor(out=y_t[:], in0=n_t[:], scalar=s_t[:, 2:3], in1=fg[:],
                                                                     op0=ALU.mult, op1=ALU.mult),
                             reads=[n_B, s_B, fgB], writes=[y_B])
                        K.dma(io["y"][ti * 128:(ti + 1) * 128, :], y_t[:], y_B, reads=[y_B])
                K.barrier()
        K.barrier(final=True)
        stats = {k: v.cnt for k, v in K.q.items()}
        stats["nsem"] = K.nsem
    build_program.stats = stats
    return nc


def _rope_cs(pos, d):
    inv = np.power(np.float32(10000.0), -(np.arange(0, d, 2, dtype=np.float32) / np.float32(d))).astype(np.float32)
    ang = (pos.astype(np.float32)[:, None] * inv[None, :]).astype(np.float32)
    return np.cos(ang.astype(np.float64)).astype(np.float32), np.sin(ang.astype(np.float64)).astype(np.float32)


def _tab_A(pos):
    row, col = pos // 64, pos % 64
    cr, sr = _rope_cs(row, 32)
    cc, sc = _rope_cs(col, 32)
    return np.concatenate([cr, cr, cc, cc, -sr, sr, -sc, sc], axis=1)


def _tab_D(pos):
    c, s = _rope_cs(pos, 32)
    return np.concatenate([c, c, c, c, -s, s, -s, s], axis=1)


def _tab_B(pos):
    c, s = _rope_cs(pos, 64)
    return np.concatenate([c, c, -s, s], axis=1)


def _c_bias_table(rpb_l, r, i, win):
    a = np.arange(128)
    ks = (TOWN * r - 256 + 128 * np.asarray(win)[:, None] + a[None, :])
    qt = TOWN * r + 128 * i + a
    kvalid = (ks >= 0) & (ks < SEQ)
    ksc = np.clip(ks, 0, SEQ - 1)
    kr, kc = ksc // 64, ksc % 64
    qr, qc = qt // 64, qt % 64
    rs = np.clip(qr - 4, 0, 120)
    cs = np.clip(qc - 8, 0, 48)
    ok = (kvalid[:, :, None] & (kr[:, :, None] >= rs[None, None, :]) & (kr[:, :, None] < rs[None, None, :] + 8)
          & (kc[:, :, None] >= cs[None, None, :]) & (kc[:, :, None] < cs[None, None, :] + 16))
    dr = np.clip(kr[:, :, None] - qr[None, None, :] + 7, 0, 14)
    dc = np.clip(kc[:, :, None] - qc[None, None, :], -15, 15) + 15
    g = rpb_l[:, dr, dc]
    out = np.where(ok[None], g, np.float32(MASKV)).astype(np.float32)
    return np.ascontiguousarray(out.transpose(2, 1, 0, 3))


def _core_inputs(c, xfull, xown, layers, P):
    r = c % 4
    m = {"xfull": xfull, "xown": xown}
    m["w_in"] = np.ascontiguousarray(P["w_in"][layers])
    m["w_out"] = np.ascontiguousarray(P["w_out"][layers])
    m["gcol"] = np.ascontiguousarray(P["norm_g"][layers].reshape(len(layers), 8, 128).transpose(0, 2, 1))
    vec = np.zeros((len(layers), V_N), np.float32)
    for k, l in enumerate(layers):
        lam_init = np.float32(0.8 - 0.6 * np.exp(-0.3 * l))
        vec[k, V_QN:V_QN + 64] = P["qn_a"][l]
        vec[k, V_KN:V_KN + 64] = P["kn_a"][l]
        vec[k, V_SUB:V_SUB + 64] = P["subln_d"][l]
        vec[k, V_SINK:V_SINK + 4] = P["sink_b"][l]
        vec[k, V_LQ1:V_LQ1 + 32] = P["lam_q1"][l]
        vec[k, V_LK1:V_LK1 + 32] = P["lam_k1"][l]
        vec[k, V_LQ2:V_LQ2 + 32] = P["lam_q2"][l]
        vec[k, V_LK2:V_LK2 + 32] = P["lam_k2"][l]
        vec[k, V_LC] = lam_init
        vec[k, V_LC + 1] = np.float32(1.0) - lam_init
    m["vecs"] = np.ascontiguousarray(np.broadcast_to(vec[:, None, :], (len(layers), 128, V_N)))
    m["fg"] = np.ascontiguousarray(np.broadcast_to(P["final_g"][None, :], (128, D_MODEL)))
    pos_all = np.arange(SEQ)
    m["tabKV"] = np.ascontiguousarray(np.concatenate([_tab_A(pos_all), _tab_D(pos_all)], axis=1))
    pos_own = TOWN * r + np.arange(TOWN)
    m["tabQ"] = np.ascontiguousarray(np.concatenate(
        [_tab_A(pos_own) * np.float32(0.125), _tab_D(pos_own) * np.float32(32 ** -0.5)], axis=1).astype(np.float32))
    pos_loc = np.clip(TOWN * r - 256 + np.arange(NT_LOC * 128), 0, SEQ - 1)
    tb = _tab_B(pos_loc)
    m["tabB"] = np.ascontiguousarray(np.concatenate([tb, tb * np.float32(0.125)], axis=1).astype(np.float32))
    cbi = np.zeros((len(layers), 128, 5 * 512), np.float32)
    cbsp = np.full((len(layers), 4, 128, 6 * 512), MASKV, np.float32)
    for k, l in enumerate(layers):
        rpb = P["rpb_c"][l]
        cbi[k] = _c_bias_table(rpb, r, 5, list(range(5, 10))).reshape(128, -1)
        for si, (i, win) in enumerate([(0, list(range(0, 6))), (1, list(range(1, 6))),
                                       (14, list(range(14, 19))), (15, list(range(14, 20)))]):
            t = _c_bias_table(rpb, r, i, win).reshape(128, -1)
            cbsp[k, si, :, :t.shape[1]] = t
    m["cb_int"] = cbi
    m["cb_sp"] = cbsp
    a = np.arange(128)
    mprev = (a[None, :] <= a[:, None]).astype(np.float32)
    mnext = (a[:, None] <= a[None, :]).astype(np.float32)
    z = np.zeros_like(mprev)
    m["bmask"] = np.ascontiguousarray(np.concatenate(
        [mprev if r > 0 else z, mprev, mnext, mnext if r < 3 else z], axis=1))
    sel = np.zeros((6,), np.float32)
    if r >= 1:
        sel[r - 1] = 1.0
    if r <= 2:
        sel[3 + r] = 1.0
    m["halosel"] = np.ascontiguousarray(np.broadcast_to(sel[None, :], (128, 6)))
    return m


_NC_CACHE = {}


def _get_nc(n_layers, fused):
    key = (n_layers, fused)
    if key not in _NC_CACHE:
        _NC_CACHE[key] = build_program(n_layers, fused)
    return _NC_CACHE[key]


def kernel(x, norm_g, w_in, w_out, qn_a, kn_a, sink_b, rpb_c, lam_q1, lam_k1, lam_q2, lam_k2, subln_d, final_g):
    P = dict(norm_g=norm_g, w_in=w_in, w_out=w_out, qn_a=qn_a, kn_a=kn_a, sink_b=sink_b, rpb_c=rpb_c,
             lam_q1=lam_q1, lam_k1=lam_k1, lam_q2=lam_q2, lam_k2=lam_k2, subln_d=subln_d, final_g=final_g)
    P = {k: np.asarray(v, dtype=np.float32) for k, v in P.items()}
    x = np.asarray(x, dtype=np.float32)
    nc = _get_nc(DEPTH, True)
    in_maps = []
    for c in range(NCORES):
        b, r = c // 4, c % 4
        m = _core_inputs(c, np.ascontiguousarray(x[b]), np.ascontiguousarray(x[b, r * TOWN:(r + 1) * TOWN]),
                         list(range(DEPTH)), P)
        rs = np.zeros((4,), np.float32)
        rs[r] = 1.0
        m["rsel"] = np.ascontiguousarray(np.broadcast_to(rs[None, :], (128, 4)))
        in_maps.append(m)
    res = run_bass_kernel_spmd(nc, in_maps, core_ids=list(range(NCORES)))
    y = np.empty_like(x)
    for c in range(NCORES):
        b, r = c // 4, c % 4
        y[b, r * TOWN:(r + 1) * TOWN] = res.results[c]["y"]
    return y
```

```python
import os
import numpy as np
from contextlib import ExitStack
import concourse.bass as bass
import concourse.mybir as mybir
from concourse.bass_utils import run_bass_kernel_spmd

F32 = mybir.dt.float32
GDT = mybir.dt.bfloat16
BF16 = mybir.dt.bfloat16
AF = mybir.ActivationFunctionType
ALU = mybir.AluOpType
AX = mybir.AxisListType

D_MODEL = 1024
SEQ = 8192
BATCH = 2
DEPTH = 2
D_IN = 3328
NCORES = 8
TOWN = 2048
NT_OWN = 16
NT_ALL = 64
NT_LOC = 20
EPS = 1e-6
MASKV = -30000.0

V_QN, V_KN, V_SUB, V_SINK, V_LQ1, V_LK1, V_LQ2, V_LK2, V_LC = 0, 64, 128, 192, 196, 228, 260, 292, 324
V_N = 328


class Buf:
    __slots__ = ("w", "r", "dsem", "dcnt", "name")
    ALL = []

    def __init__(self, name=""):
        self.w = None
        self.r = {}
        self.dsem = None
        self.dcnt = 0
        self.name = name
        Buf.ALL.append(self)


class Q:
    def __init__(self, name, eng, sem):
        self.name, self.eng, self.sem = name, eng, sem
        self.cnt = 0
        self.seen = {}


class Ctx:
    def __init__(self, nc, st):
        self.nc, self.st = nc, st
        self.q = {}
        for name, eng in (("pe", nc.tensor), ("act", nc.scalar), ("dve", nc.vector),
                          ("pool", nc.gpsimd), ("sp", nc.sync)):
            self.q[name] = Q(name, eng, st.enter_context(nc.semaphore("q_" + name)))
        self.dbufs = []
        self.nsem = 5

    def _wait(self, q, deps):
        for key, (s, v) in deps.items():
            if q.seen.get(key, 0) >= v:
                continue
            q.eng.wait_ge(s, v)
            q.seen[key] = v

    def _deps(self, q, reads, writes):
        deps = {}

        def add(tok, raw):
            if tok is None:
                return
            s, v = tok
            if s is q.sem and (q.name == "pe" or not raw):
                return
            k = id(s)
            if k not in deps or deps[k][1] < v:
                deps[k] = (s, v)
        for b in reads:
            add(b.w, True)
        for b in writes:
            add(b.w, True)
            for tok in b.r.values():
                add(tok, True)
        return deps

    def op(self, qn, fn, reads=(), writes=()):
        q = self.q[qn]
        self._wait(q, self._deps(q, reads, writes))
        ins = fn(q.eng)
        q.cnt += 1
        ins.then_inc(q.sem, 1)
        tok = (q.sem, q.cnt)
        for b in reads:
            b.r[id(q.sem)] = tok
        for b in writes:
            b.w = tok
            b.r = {}
        return ins

    def dma(self, out, in_, sb, reads=(), writes=(), qn="sp"):
        q = self.q[qn]
        if sb.dsem is None:
            sb.dsem = self.st.enter_context(self.nc.semaphore("d_%d" % self.nsem))
            self.nsem += 1
            self.dbufs.append(sb)
        self._wait(q, self._deps(q, reads, writes))
        ins = q.eng.dma_start(out=out, in_=in_)
        sb.dcnt += 16
        ins.then_inc(sb.dsem, 16)
        tok = (sb.dsem, sb.dcnt)
        for b in reads:
            b.r[id(sb.dsem)] = tok
        for b in writes:
            b.w = tok
            b.r = {}

    def collective(self, kind, op, groups, in_ap, out_ap, in_B, out_B):
        q = self.q["pool"]
        self._wait(q, self._deps(q, [in_B], [out_B]))
        sem = self.st.enter_context(self.nc.semaphore("cc_%d" % self.nsem))
        self.nsem += 1
        ins = q.eng.collective_compute(kind, op, replica_groups=groups, ins=[in_ap], outs=[out_ap])
        ins.then_inc(sem)
        tok = (sem, 1)
        in_B.r[id(sem)] = tok
        out_B.w = tok
        out_B.r = {}
        self.ccs = getattr(self, "ccs", []) + [tok]

    def renew(self):
        self.barrier()
        ccsems = set(id(cs) for (cs, cv) in getattr(self, "ccs", []))
        for b in Buf.ALL:
            if b.w is not None and id(b.w[0]) in ccsems:
                b.r = {}
                continue
            b.w = None
            b.r = {}
        for name, qq in self.q.items():
            qq.sem = self.st.enter_context(self.nc.semaphore("q%d_%s" % (self.nsem, name)))
            self.nsem += 1
            qq.cnt = 0
            qq.seen = {}
        for b in self.dbufs:
            pass

    def barrier(self, final=False):
        sp = self.q["sp"]
        deps = {}
        for b in self.dbufs:
            if b.dcnt:
                deps[id(b.dsem)] = (b.dsem, b.dcnt)
        if final:
            for (cs, cv) in getattr(self, "ccs", []):
                deps[id(cs)] = (cs, cv)
        for qq in self.q.values():
            if qq is not sp and qq.cnt:
                deps[id(qq.sem)] = (qq.sem, qq.cnt)
        self._wait(sp, deps)
        self.op("sp", lambda e: e.nop())
        for qq in self.q.values():
            if qq is sp:
                continue
            d = {id(sp.sem): (sp.sem, sp.cnt)}
            self._wait(qq, d)


def build_program(n_layers, fused):
    PH = os.environ.get('KPH', '123abBC')
    nc = bass.Bass("TRN2", target_bir_lowering=False)
    L = n_layers

    def din(name, shape, dt=F32):
        return nc.dram_tensor(name, shape, dt, kind="ExternalInput").ap()

    io = dict(
        xfull=din("xfull", [SEQ, D_MODEL]),
        xown=din("xown", [TOWN, D_MODEL]),
        w_in=din("w_in", [L, D_MODEL, D_IN]),
        w_out=din("w_out", [L, D_MODEL, D_MODEL]),
        gcol=din("gcol", [L, 128, 8]),
        vecs=din("vecs", [L, 128, V_N]),
        fg=din("fg", [128, D_MODEL]),
        tabKV=din("tabKV", [SEQ, 256]),
        tabQ=din("tabQ", [TOWN, 256]),
        tabB=din("tabB", [NT_LOC * 128, 256]),
        cb_int=din("cb_int", [L, 128, 5 * 512]),
        cb_sp=din("cb_sp", [L, 4, 128, 6 * 512]),
        bmask=din("bmask", [128, 512]),
        halosel=din("halosel", [128, 6]),
        y=nc.dram_tensor("y", [TOWN, D_MODEL], F32, kind="ExternalOutput").ap(),
    )
    if fused:
        io["rsel"] = din("rsel", [128, 4])
        io["x1own"] = nc.dram_tensor("x1own", [TOWN, D_MODEL], F32, kind="Internal").ap()
        io["xg_src"] = nc.dram_tensor("xg_src", [SEQ, D_MODEL], BF16, kind="Internal").ap()
        io["xg_dst"] = nc.dram_tensor("xg_dst", [SEQ, D_MODEL], BF16, kind="Internal").ap()
    else:
        io["xnext"] = nc.dram_tensor("xnext", [TOWN, D_MODEL], F32, kind="ExternalOutput").ap()
    x1ownB = Buf("x1own")
    xgsB = [Buf("xg_src%d" % i) for i in range(8)]
    xgdB = [Buf("xg_dst%d" % i) for i in range(8)]

    with ExitStack() as st:
        K = Ctx(nc, st)
        E = st.enter_context

        uniq = [0]

        def sb(name, shape, dt, stack=None):
            uniq[0] += 1
            t = (stack or st).enter_context(nc.sbuf_tensor("s%d_%s" % (uniq[0], name), shape, dt))
            return t, Buf(name)

        ident, identB = sb("ident", [128, 128], BF16)
        identf, identfB = sb("identf", [128, 128], F32)
        pT = [(E(nc.psum_tensor("pT%d" % i, [128, 1024], BF16)), Buf("pT%d" % i)) for i in range(2)]
        pzd = [E(nc.psum_tensor("pzd%d" % i, [128, 1024], F32)) for i in range(2)]
        pz = [(pzd[i // 2][:, (i % 2) * 512:(i % 2 + 1) * 512], Buf("pz%d" % i)) for i in range(4)]
        pz += [(E(nc.psum_tensor("pz%d" % i, [128, 512], F32)), Buf("pz%d" % i)) for i in (4, 5)]
        pzs = pz[:4]
        xs = [sb("xs%d" % i, [128, 1024], F32) for i in range(3)]
        xb = [sb("xb%d" % i, [128, 1024], BF16) for i in range(2)]
        xT = [sb("xT%d" % i, [128, 8, 128], BF16) for i in range(3)]
        junk, junkB = sb("junk", [128, 1024], BF16)
        st1 = [sb("st1_%d" % i, [128, 8], F32) for i in range(4)]
        hn = [sb("hn%d" % i, [128, 8], F32) for i in range(3)]
        t1 = [sb("t1_%d" % i, [128, 256], F32) for i in range(2)]
        t2 = [sb("t2_%d" % i, [128, 256], F32) for i in range(2)]
        t3 = [sb("t3_%d" % i, [128, 256], F32) for i in range(2)]
        tabt = [sb("tabt%d" % i, [128, 256], F32) for i in range(5)]
        PT2 = [sb("PT%d" % i, [128, 1024], BF16)[0] for i in range(3)]
        PT = [(PT2[i // 2][:, (i % 2) * 512:(i % 2 + 1) * 512], Buf("PT%d" % i)) for i in range(6)]
        mixT, mixTB = sb("mixT", [128, 8, TOWN], BF16)
        mixtm = [sb("mixtm%d" % i, [128, 4, 256], BF16) for i in range(2)]
        vecs, vecsB = sb("vecs", [128, V_N], F32)
        gcol, gcolB = sb("gcol", [128, 8], F32)
        lamt, lamtB = sb("lamt", [128, 8], F32)
        esink, esinkB = sb("esink", [128, 4], F32)
        subg, subgB = sb("subg", [128, 64], F32)
        hsel, hselB = sb("hsel", [128, 6], F32)
        bmask, bmaskB = sb("bmask", [128, 4, 128], F32)
        fin = [sb("fin%d" % i, [128, 4, 64], F32) for i in range(2)]
        fin2 = [sb("fin2_%d" % i, [128, 4, 64], F32) for i in range(2)]
        rd = [sb("rd%d" % i, [128, 8], F32) for i in range(2)]

        rr = {}

        def rot(lst, key=None):
            k = key or id(lst)
            i = rr.get(k, 0)
            rr[k] = i + 1
            return lst[i % len(lst)]

        mhalf, mhalfB = sb("mhalf", [128, 8], F32)
        K.op("pool", lambda e: e.memset(mhalf[:], -0.5), writes=[mhalfB])
        K.op("pool", lambda e: e.memset(identf[:], 1.0), writes=[identfB])
        K.op("pool", lambda e: e.affine_select(out=identf[:], in_=identf[:], pattern=[[-1, 128]],
                                               compare_op=ALU.is_equal, fill=0.0, base=0,
                                               channel_multiplier=1), reads=[identfB], writes=[identfB])
        K.op("dve", lambda e: e.tensor_copy(out=ident[:], in_=identf[:]), reads=[identfB], writes=[identB])
        K.dma(bmask[:].rearrange("p a b -> p (a b)"), io["bmask"][:, :], bmaskB, writes=[bmaskB])
        K.dma(hsel[:], io["halosel"][:, :], hselB, writes=[hselB])
        if fused:
            rsel, rselB = sb("rsel", [128, 4], F32)
            K.dma(rsel[:], io["rsel"][:, :], rselB, writes=[rselB])

        def stats_rstd(x_t, x_B, width, s_t, s_B):
            K.op("dve", lambda e: e.scalar_tensor_tensor(out=junk[:, 0:width], in0=x_t, scalar=1.0, in1=x_t,
                                                         op0=ALU.mult, op1=ALU.mult, accum_out=s_t[:, 0:1]),
                 reads=[x_B], writes=[junkB, s_B])
            K.op("pool", lambda e: e.tensor_scalar(out=s_t[:, 1:2], in0=s_t[:, 0:1], scalar1=1.0 / width, scalar2=EPS,
                                                   op0=ALU.mult, op1=ALU.add), reads=[s_B], writes=[s_B])
            K.op("pool", lambda e: e.tensor_tensor(out=s_t[:, 2:3], in0=s_t[:, 1:2], in1=mhalf[:, 0:1], op=ALU.pow),
                 reads=[s_B, mhalfB], writes=[s_B])

        def prep_a(x_t, x_B):
            s_t, s_B = rot(st1)
            stats_rstd(x_t[:], x_B, 1024, s_t, s_B)
            b_t, b_B = rot(xb)
            K.op("act", lambda e: e.activation(out=b_t[:], in_=x_t[:], func=AF.Copy), reads=[x_B], writes=[b_B])
            return b_t, b_B, s_t, s_B

        def prep_b(b_t, b_B, eng="act"):
            p_t, p_B = rot(pT)
            for k in range(8):
                K.op("pe", lambda e, k=k: e.transpose(out=p_t[:, k * 128:(k + 1) * 128],
                                                      in_=b_t[:, k * 128:(k + 1) * 128], identity=ident[:]),
                     reads=[b_B, identB], writes=[p_B])
            T_t, T_B = rot(xT)
            if eng == "act":
                K.op("act", lambda e: e.activation(out=T_t[:].rearrange("p c t -> p (c t)"), in_=p_t[:], func=AF.Copy),
                     reads=[p_B], writes=[T_B])
            else:
                K.op("dve", lambda e: e.tensor_copy(out=T_t[:].rearrange("p c t -> p (c t)"), in_=p_t[:]),
                     reads=[p_B], writes=[T_B])
            return T_t, T_B

        def prep_cast(x_t, x_B):
            b_t, b_B = rot(xb)
            K.op("act", lambda e: e.activation(out=b_t[:], in_=x_t[:], func=AF.Copy), reads=[x_B], writes=[b_B])
            return b_t, b_B

        def prep_stats(x_t, x_B):
            s_t, s_B = rot(st1)
            stats_rstd(x_t[:], x_B, 1024, s_t, s_B)
            return s_t, s_B

        def run_pipe(n, order, lo=0):
            md = max(d for d, _ in order)
            for step in range(n + md):
                for d, f in order:
                    idx = step - d
                    if 0 <= idx < n:
                        f(lo + idx)

        def prep_tile(x_t, x_B):
            b_t, b_B, s_t, s_B = prep_a(x_t, x_B)
            T_t, T_B = prep_b(b_t, b_B)
            return T_t, T_B, s_t, s_B

        def proj(T_t, T_B, w_t, w_B, groups, z_t, z_B, s_t, s_B, evac="dve"):
            for (c0, n) in groups:
                p_t, p_B = rot(pzs, "pzproj")
                for k in range(8):
                    K.op("pe", lambda e, k=k: e.matmul(p_t[:, 0:n], lhsT=T_t[:, k, :], rhs=w_t[:, k, c0:c0 + n],
                                                       start=(k == 0), stop=(k == 7)),
                         reads=[T_B, w_B], writes=[p_B])
                if evac == "act":
                    K.op("act", lambda e: e.activation(out=z_t[:, c0:c0 + n], in_=p_t[:, 0:n], func=AF.Copy, scale=s_t[:, 2:3]),
                         reads=[p_B, s_B], writes=[z_B])
                else:
                    K.op("dve", lambda e: e.tensor_scalar(out=z_t[:, c0:c0 + n], in0=p_t[:, 0:n], scalar1=s_t[:, 2:3],
                                                          scalar2=None, op0=ALU.mult),
                         reads=[p_B, s_B], writes=[z_B])

        def headnorm(z_t, z_B, c0, H, gain_ap):
            headnorm_b(z_t, z_B, c0, H, gain_ap, headnorm_a(z_t, z_B, c0, H))

        def headnorm_a(z_t, z_B, c0, H):
            a_t, a_B = rot(t1)
            h_t, h_B = rot(hn)
            src = z_t[:, c0:c0 + 64 * H]
            src3 = src.rearrange("p (h d) -> p h d", h=H)
            for hh_ in range(H):
                K.op("dve", lambda e: e.scalar_tensor_tensor(
                    out=a_t[:, hh_ * 64:(hh_ + 1) * 64], in0=src[:, hh_ * 64:(hh_ + 1) * 64], scalar=1.0,
                    in1=src[:, hh_ * 64:(hh_ + 1) * 64], op0=ALU.mult, op1=ALU.mult, accum_out=h_t[:, hh_:hh_ + 1]),
                    reads=[z_B], writes=([a_B, h_B] if hh_ in (0, H - 1) else []))
            K.op("pool", lambda e: e.tensor_scalar(out=h_t[:, 0:H], in0=h_t[:, 0:H], scalar1=1.0 / 64, scalar2=EPS,
                                                   op0=ALU.mult, op1=ALU.add), reads=[h_B], writes=[h_B])
            K.op("pool", lambda e: e.tensor_tensor(out=h_t[:, 0:H], in0=h_t[:, 0:H], in1=mhalf[:, 0:H], op=ALU.pow),
                 reads=[h_B, mhalfB], writes=[h_B])
            return h_t, h_B

        def headnorm_b(z_t, z_B, c0, H, gain_ap, hb):
            h_t, h_B = hb
            src = z_t[:, c0:c0 + 64 * H]
            src3 = src.rearrange("p (h d) -> p h d", h=H)
            K.op("dve", lambda e: e.tensor_tensor(out=src3, in0=src3,
                                                  in1=h_t[:, 0:H].unsqueeze(2).to_broadcast([128, H, 64]), op=ALU.mult),
                 reads=[z_B, h_B], writes=[z_B])
            K.op("dve", lambda e: e.tensor_tensor(out=src3, in0=src3,
                                                  in1=gain_ap.unsqueeze(1).to_broadcast([128, H, 64]), op=ALU.mult),
                 reads=[z_B, vecsB], writes=[z_B])

        def rope(z_t, z_B, c0, H, hs, tab_t, tab_B, tc0, dsts, dst_B, mul_eng="pool"):
            nseg = 32 // hs
            a_t, a_B = rot(t2)
            b_t, b_B = rot(t3)
            src = z_t[:, c0:c0 + 64 * H]
            src3 = src.rearrange("p (h d) -> p h d", h=H)
            cc = tab_t[:, tc0:tc0 + 64]
            ss = tab_t[:, tc0 + 64:tc0 + 128]
            K.op("dve", lambda e: e.tensor_tensor(out=a_t[:, 0:64 * H].rearrange("p (h d) -> p h d", h=H), in0=src3,
                                                  in1=cc.unsqueeze(1).to_broadcast([128, H, 64]), op=ALU.mult),
                 reads=[z_B, tab_B], writes=[a_B])
            s5 = src.rearrange("p (h s two k) -> p h s two k", h=H, s=nseg, two=2)
            b5 = b_t[:, 0:64 * H].rearrange("p (h s two k) -> p h s two k", h=H, s=nseg, two=2)
            ss4 = ss.rearrange("p (s two k) -> p s two k", s=nseg, two=2)
            for (o_, i_) in ((0, 1), (1, 0)):
                K.op(mul_eng, lambda e, o_=o_, i_=i_: e.tensor_tensor(
                    out=b5[:, :, :, o_, :], in0=s5[:, :, :, i_, :],
                    in1=ss4[:, :, o_, :].unsqueeze(1).to_broadcast([128, H, nseg, hs]), op=ALU.mult),
                    reads=[z_B, tab_B], writes=[b_B])
            for (dst_ap, sel) in dsts:
                K.op("dve", lambda e, dst_ap=dst_ap, sel=sel: e.tensor_tensor(
                    out=dst_ap, in0=sel(a_t[:, 0:64 * H]), in1=sel(b_t[:, 0:64 * H]), op=ALU.add),
                    reads=[a_B, b_B], writes=[dst_B])

        def silu_gate(z_t, z_B, c0, n, dst_ap, dst_B):
            a_t, a_B = rot(t1)
            K.op("act", lambda e: e.activation(out=a_t[:, 0:n], in_=z_t[:, c0:c0 + n], func=AF.Exp, scale=-1.0),
                 reads=[z_B], writes=[a_B])
            K.op("act", lambda e: e.activation(out=a_t[:, 0:n], in_=a_t[:, 0:n], func=AF.Ln, scale=1.0, bias=1.0),
                 reads=[a_B], writes=[a_B])
            K.op("act", lambda e: e.activation(out=a_t[:, 0:n], in_=a_t[:, 0:n], func=AF.Exp, scale=-1.0),
                 reads=[a_B], writes=[a_B])
            K.op("dve", lambda e: e.tensor_tensor(out=dst_ap, in0=z_t[:, c0:c0 + n], in1=a_t[:, 0:n], op=ALU.mult),
                 reads=[a_B, z_B], writes=[dst_B])

        def transpose_multi(srcs, dsts):
            p_t, p_B = rot(pT)
            for i, (s_ap, s_B) in enumerate(srcs):
                K.op("pe", lambda e: e.transpose(out=p_t[:, i * 128:(i + 1) * 128], in_=s_ap, identity=ident[:]),
                     reads=[s_B, identB], writes=[p_B])
            for (dst_ap, dst_B, b0, nb_, eng) in dsts:
                src = p_t[:, b0 * 128:(b0 + nb_) * 128]
                if len(dst_ap.shape) == 3:
                    src = src.rearrange("p (a b) -> p a b", a=dst_ap.shape[1])
                if eng == "act":
                    K.op("act", lambda e: e.activation(out=dst_ap, in_=src, func=AF.Copy), reads=[p_B], writes=[dst_B])
                else:
                    K.op("dve", lambda e: e.tensor_copy(out=dst_ap, in_=src), reads=[p_B], writes=[dst_B])

        def transpose_to(src_aps, src_B, dst_ap, dst_B, eng="dve"):
            p_t, p_B = rot(pT)
            n = len(src_aps)
            for i, s_ap in enumerate(src_aps):
                K.op("pe", lambda e, i=i, s_ap=s_ap: e.transpose(out=p_t[:, i * 128:(i + 1) * 128], in_=s_ap,
                                                                 identity=ident[:]),
                     reads=[src_B, identB], writes=[p_B])
            src = p_t[:, 0:n * 128]
            if len(dst_ap.shape) == 3:
                src = src.rearrange("p (a b) -> p a b", a=dst_ap.shape[1])
            if eng == "act":
                K.op("act", lambda e: e.activation(out=dst_ap, in_=src, func=AF.Copy), reads=[p_B], writes=[dst_B])
            else:
                K.op("dve", lambda e: e.tensor_copy(out=dst_ap, in_=src), reads=[p_B], writes=[dst_B])

        def load_weights_chunk(l, c, pieces):
            groups, cur, used = [], [], 0
            for pc in pieces:
                if used + pc[1] > 1024:
                    groups.append(cur)
                    cur, used = [], 0
                cur.append(pc)
                used += pc[1]
            groups.append(cur)
            i = 0
            for grp_ in groups:
                w_t, w_B = rot(xs)
                off = 0
                offs = []
                for (c0, n, dst_ap, dst_B) in grp_:
                    K.dma(w_t[:, off:off + n], io["w_in"][l, c * 128:(c + 1) * 128, c0:c0 + n], w_B, writes=[w_B])
                    offs.append(off)
                    off += n
                for k_, (c0, n, dst_ap, dst_B) in enumerate(grp_):
                    o_ = offs[k_]
                    if (i + c) % 2 == 0:
                        K.op("act", lambda e: e.activation(out=dst_ap, in_=w_t[:, o_:o_ + n], func=AF.Copy, scale=gcol[:, c:c + 1]),
                             reads=[w_B, gcolB], writes=[dst_B])
                    else:
                        K.op("dve", lambda e: e.tensor_scalar(
                            out=dst_ap, in0=w_t[:, o_:o_ + n], scalar1=gcol[:, c:c + 1], scalar2=None, op0=ALU.mult),
                            reads=[w_B, gcolB], writes=[dst_B])
                    i += 1

        def load_x(src_ap, extra=()):
            x_t, x_B = rot(xs)
            K.dma(x_t[:], src_ap, x_B, reads=list(extra), writes=[x_B])
            return x_t, x_B

        def attn_pipe(n, qk, ex, pv, la, mids=None):
            for idx in range(n + la):
                if idx < n:
                    qk(idx)
                    ex(idx)
                if idx >= la:
                    pv(idx - la)
                if mids and idx in mids:
                    mids[idx]()

        for l in range(L):
            if l > 0:
                K.renew()
            x_src_full = io["xfull"] if l == 0 else io["xg_dst"]
            x_src_own = io["xown"] if l == 0 else io["x1own"]
            xfB = (lambda row: []) if l == 0 else (lambda row: [xgdB[row // 1024]])
            xoB = [] if l == 0 else [x1ownB]
            K.dma(vecs[:], io["vecs"][l, :, :], vecsB, writes=[vecsB])
            K.dma(gcol[:], io["gcol"][l, :, :], gcolB, writes=[gcolB])
            K.op("act", lambda e: e.activation(out=esink[:], in_=vecs[:, V_SINK:V_SINK + 4], func=AF.Exp),
                 reads=[vecsB], writes=[esinkB])
            K.op("dve", lambda e: e.tensor_tensor(out=junk[:, 0:32], in0=vecs[:, V_LQ1:V_LQ1 + 32],
                                                  in1=vecs[:, V_LK1:V_LK1 + 32], op=ALU.mult),
                 reads=[vecsB], writes=[junkB])
            K.op("dve", lambda e: e.tensor_reduce(out=lamt[:, 0:1], in_=junk[:, 0:32], axis=AX.X, op=ALU.add),
                 reads=[junkB], writes=[lamtB])
            K.op("dve", lambda e: e.tensor_tensor(out=junk[:, 32:64], in0=vecs[:, V_LQ2:V_LQ2 + 32],
                                                  in1=vecs[:, V_LK2:V_LK2 + 32], op=ALU.mult),
                 reads=[vecsB], writes=[junkB])
            K.op("dve", lambda e: e.tensor_reduce(out=lamt[:, 1:2], in_=junk[:, 32:64], axis=AX.X, op=ALU.add),
                 reads=[junkB], writes=[lamtB])
            K.op("act", lambda e: e.activation(out=lamt[:, 2:4], in_=lamt[:, 0:2], func=AF.Exp),
                 reads=[lamtB], writes=[lamtB])
            K.op("dve", lambda e: e.tensor_tensor(out=lamt[:, 4:5], in0=lamt[:, 2:3], in1=lamt[:, 3:4], op=ALU.subtract),
                 reads=[lamtB], writes=[lamtB])
            K.op("dve", lambda e: e.tensor_tensor(out=lamt[:, 4:5], in0=lamt[:, 4:5], in1=vecs[:, V_LC:V_LC + 1], op=ALU.add),
                 reads=[lamtB, vecsB], writes=[lamtB])
            K.op("dve", lambda e: e.tensor_scalar(out=lamt[:, 5:6], in0=lamt[:, 4:5], scalar1=-1.0, scalar2=None, op0=ALU.mult),
                 reads=[lamtB], writes=[lamtB])
            K.op("dve", lambda e: e.tensor_scalar(out=subg[:], in0=vecs[:, V_SUB:V_SUB + 64], scalar1=vecs[:, V_LC + 1:V_LC + 2],
                                                  scalar2=None, op0=ALU.mult), reads=[vecsB], writes=[subgB])

            with ExitStack() as ph:
              if '1' in PH:
                wBC, wBCB = sb("wBC", [128, 8, 1792], BF16, ph)
                kTb, kTbB = sb("kTb", [128, NT_LOC * 128], BF16, ph)
                vb, vbB = sb("vb", [128, NT_LOC, 2, 66], BF16, ph)
                kTc, kTcB = sb("kTc", [128, 2, NT_LOC * 128], BF16, ph)
                vc, vcB = sb("vc", [128, NT_LOC, 4, 66], BF16, ph)
                qTb, qTbB = sb("qTb", [128, NT_OWN, 2, 128], BF16, ph)
                qTc, qTcB = sb("qTc", [128, NT_OWN, 2, 128], BF16, ph)
                gbc, gbcB = sb("gbc", [128, NT_OWN, 512], BF16, ph)
                ctp = [sb("ctp%d" % i, [128, 512], F32, ph) for i in range(3)]
                zs = [sb("zs%d" % i, [128, 1792], F32, ph) for i in range(2)]
                finp = fin + fin2
                qtm = [sb("qtm%d" % i, [128, 512], BF16, ph) for i in range(3)]
                ktm = [sb("ktm%d" % i, [128, 384], BF16, ph) for i in range(3)]
                tmpc = [sb("tmpc%d" % i, [128, 512], F32, ph) for i in range(2)]

                K.op("pool", lambda e: e.memset(vb[:].rearrange("p a b c -> p (a b c)"), 1.0), writes=[vbB])
                K.op("pool", lambda e: e.memset(vc[:].rearrange("p a b c -> p (a b c)"), 1.0), writes=[vcB])
                for c in range(8):
                    load_weights_chunk(l, c, [(768, 1024, wBC[:, c, 0:1024], wBCB), (1792, 768, wBC[:, c, 1024:1792], wBCB)])

                order1 = list(range(2, 2 + NT_OWN)) + [0, 1, 18, 19]
                SP1 = {}

                def p1_s0(idx):
                    bt = order1[idx]
                    if 2 <= bt < 2 + NT_OWN:
                        x_t, x_B = load_x(x_src_own[(bt - 2) * 128:(bt - 1) * 128, :], xoB)
                    else:
                        prev = bt < 2
                        k0 = rr.get(id(xs), 0)
                        rr[id(xs)] = k0 + 3
                        x_t, x_B = xs[k0 % 3]
                        sc0 = 0 if prev else 3
                        for m in range(3):
                            h_t, h_B = xs[(k0 + 1 + (m % 2)) % 3]
                            row0 = (m + 1) * TOWN + ((bt - 2) * 128 if prev else (bt - 18) * 128)
                            h_v = h_t[:] if l == 0 else h_t[:].bitcast(BF16)[:, 0:1024]
                            K.dma(h_v, x_src_full[row0:row0 + 128, :], h_B, reads=xfB(row0), writes=[h_B])
                            if m == 0:
                                K.op("dve", lambda e: e.tensor_scalar(out=x_t[:], in0=h_v, scalar1=hsel[:, sc0:sc0 + 1],
                                                                      scalar2=None, op0=ALU.mult),
                                     reads=[h_B, hselB], writes=[x_B])
                            else:
                                K.op("dve", lambda e: e.scalar_tensor_tensor(
                                    out=x_t[:], in0=h_v, scalar=hsel[:, sc0 + m:sc0 + m + 1], in1=x_t[:],
                                    op0=ALU.mult, op1=ALU.add), reads=[h_B, hselB, x_B], writes=[x_B])
                    tb_t, tb_B = rot(tabt)
                    K.dma(tb_t[:], io["tabB"][bt * 128:(bt + 1) * 128, :], tb_B, writes=[tb_B], qn="act")
                    SP1[bt] = [x_t, x_B, tb_t, tb_B]

                def p1_cast(idx):
                    bt = order1[idx]
                    SP1[bt] += list(prep_cast(*SP1[bt][0:2]))

                def p1_stats(idx):
                    bt = order1[idx]
                    SP1[bt] += list(prep_stats(*SP1[bt][0:2]))

                def p1_b2(idx):
                    bt = order1[idx]
                    x_t, x_B, tb_t, tb_B, b_t, b_B, s_t, s_B = SP1[bt]
                    T_t, T_B = prep_b(b_t, b_B)
                    SP1[bt] = [tb_t, tb_B, T_t, T_B, s_t, s_B]

                def p1_c(idx):
                    bt = order1[idx]
                    tb_t, tb_B, T_t, T_B, s_t, s_B = SP1[bt]
                    z_t, z_B = rot(zs)
                    proj(T_t, T_B, wBC, wBCB, [(0, 512), (512, 512), (1024, 512), (1536, 256)], z_t, z_B, s_t, s_B)
                    SP1[bt] = (z_t, z_B, tb_t, tb_B)

                def p1_s1(idx):
                    bt = order1[idx]
                    z_t, z_B, tb_t, tb_B = SP1[bt]
                    own = 2 <= bt < 2 + NT_OWN
                    i_own = bt - 2
                    q_t, q_B = rot(qtm)
                    k_t, k_B = rot(ktm)
                    rope(z_t, z_B, 256, 2, 32, tb_t, tb_B, 0,
                         [(k_t[:, 0:128], lambda a: a)], k_B)
                    K.op("pool", lambda e: e.tensor_copy(out=k_t[:, 128:384], in_=z_t[:, 1024:1280]),
                         reads=[z_B], writes=[k_B])
                    K.op("act", lambda e: e.activation(out=vb[:, bt, :, 0:64],
                                                       in_=z_t[:, 384:512].rearrange("p (h d) -> p h d", h=2), func=AF.Copy),
                         reads=[z_B], writes=[vbB])
                    K.op("act", lambda e: e.activation(out=vc[:, bt, :, 0:64],
                                                       in_=z_t[:, 1280:1536].rearrange("p (h d) -> p h d", h=4), func=AF.Copy),
                         reads=[z_B], writes=[vcB])
                    if own:
                        dst = q_t[:, 0:256].rearrange("p (g kv d) -> p kv g d", g=2, kv=2)
                        rope(z_t, z_B, 0, 4, 32, tb_t, tb_B, 128,
                             [(dst, lambda a: a.rearrange("p (kv g d) -> p kv g d", kv=2, g=2))], q_B)
                        K.op("pool", lambda e: e.tensor_copy(out=q_t[:, 256:512], in_=z_t[:, 768:1024]),
                             reads=[z_B], writes=[q_B])
                        silu_gate(z_t, z_B, 512, 256, gbc[:, i_own, 0:256], gbcB)
                        silu_gate(z_t, z_B, 1536, 256, gbc[:, i_own, 256:512], gbcB)
                    SP1[bt] = (q_t, q_B, k_t, k_B)

                def p1_s2(idx):
                    bt = order1[idx]
                    q_t, q_B, k_t, k_B = SP1.pop(bt)
                    own = 2 <= bt < 2 + NT_OWN
                    i_own = bt - 2
                    srcs = [(k_t[:, 0:128], k_B), (k_t[:, 128:256], k_B), (k_t[:, 256:384], k_B)]
                    dsts = [(kTb[:, bt * 128:(bt + 1) * 128], kTbB, 0, 1, "act"),
                            (kTc[:, 0, bt * 128:(bt + 1) * 128], kTcB, 1, 1, "act"),
                            (kTc[:, 1, bt * 128:(bt + 1) * 128], kTcB, 2, 1, "act")]
                    if own:
                        srcs += [(q_t[:, k_ * 128:(k_ + 1) * 128], q_B) for k_ in range(4)]
                        dsts += [(qTb[:, i_own, :, :].rearrange("p g q -> p (g q)"), qTbB, 3, 2, "act"),
                                 (qTc[:, i_own, :, :].rearrange("p g q -> p (g q)"), qTcB, 5, 2, "act")]
                    transpose_multi(srcs, dsts)

                def run_p1_step1(lo, hi):
                    run_pipe(hi - lo, [(1, p1_cast), (1, p1_stats), (2, p1_b2), (3, p1_c), (4, p1_s1), (5, p1_s2), (0, p1_s0)], lo)

                def p1_attn(i):
                    acc_t, acc_B = pz[4]
                    accv = acc_t[:, 0:260].rearrange("p (a b) -> p a b", b=65)
                    wins = [i + 1, i + 2, i + 3]
                    mids = [0 if i == 0 else 1, None, 3 if i == NT_OWN - 1 else 2]
                    slots = {}

                    def qkB(idx, i=i, wins=wins, slots=slots):
                        j = wins[idx]
                        banks = [rot(pzs, "pzst"), rot(pzs, "pzst")]
                        slots[idx] = banks
                        for kv in range(2):
                            s_t, s_B = banks[kv]
                            K.op("pe", lambda e: e.matmul(
                                s_t[:, 0:256], lhsT=kTb[kv * 64:(kv + 1) * 64, j * 128:(j + 1) * 128],
                                rhs=qTb[kv * 64:(kv + 1) * 64, i, :, :].rearrange("p g q -> p (g q)"),
                                start=True, stop=True), reads=[kTbB, qTbB], writes=[s_B])

                    def exB(idx, mids=mids, slots=slots):
                        banks = slots[idx]
                        p_t, p_B = rot(PT)
                        slots[idx] = (p_t, p_B)
                        for kv in range(2):
                            s_t, s_B = banks[kv]
                            K.op("act", lambda e: e.activation(out=p_t[:, kv * 256:(kv + 1) * 256], in_=s_t[:, 0:256], func=AF.Exp),
                                 reads=[s_B], writes=[p_B])
                        if mids[idx] is not None:
                            mi = mids[idx]
                            K.op("pool", lambda e: e.tensor_tensor(
                                out=p_t[:].rearrange("p (h q) -> p h q", h=4), in0=p_t[:].rearrange("p (h q) -> p h q", h=4),
                                in1=bmask[:, mi, :].unsqueeze(1).to_broadcast([128, 4, 128]), op=ALU.mult),
                                reads=[p_B, bmaskB], writes=[p_B])

                    def pvB(idx, wins=wins, slots=slots, accv=accv, acc_B=acc_B):
                        j = wins[idx]
                        p_t, p_B = slots[idx]
                        for kv in range(2):
                            for g in range(2):
                                h = 2 * kv + g
                                K.op("pe", lambda e, kv=kv, g=g, h=h: e.matmul(
                                    accv[:, h, :], lhsT=p_t[:, (kv * 2 + g) * 128:(kv * 2 + g + 1) * 128],
                                    rhs=vb[:, j, kv, 0:65], start=(idx == 0 and h == 0), stop=(idx == 2 and h == 3)),
                                    reads=[p_B, vbB], writes=[acc_B])

                    if 'B' in PH:
                        attn_pipe(3, qkB, exB, pvB, 1)
                    r_t, r_B = rot(rd)
                    K.op("dve", lambda e: e.tensor_tensor(out=r_t[:, 0:4], in0=accv[:, :, 64], in1=esink[:], op=ALU.add),
                         reads=[acc_B, esinkB], writes=[r_B])
                    K.op("dve", lambda e: e.reciprocal(out=r_t[:, 0:4], in_=r_t[:, 0:4]), reads=[r_B], writes=[r_B])
                    fb_t, fb_B = rot(finp)
                    K.op("dve", lambda e: e.tensor_tensor(out=fb_t[:], in0=accv[:, :, 0:64],
                                                          in1=r_t[:, 0:4].unsqueeze(2).to_broadcast([128, 4, 64]), op=ALU.mult),
                         reads=[acc_B, r_B], writes=[fb_B])
                    if i == 0:
                        cwin = list(range(0, 6))
                    elif i == NT_OWN - 1:
                        cwin = list(range(14, 20))
                    else:
                        cwin = list(range(i, i + 5))
                    spi = {0: 0, 1: 1, NT_OWN - 2: 2, NT_OWN - 1: 3}.get(i, None)
                    acc2_t, acc2_B = pz[5]
                    accv2 = acc2_t[:, 0:260].rearrange("p (a b) -> p a b", b=65)
                    slots2 = {}
                    nw = len(cwin)

                    def qkC(idx, i=i, cwin=cwin, slots2=slots2):
                        j = cwin[idx]
                        banks = [rot(pzs, "pzst"), rot(pzs, "pzst")]
                        slots2[idx] = banks
                        for h in range(4):
                            p_, hh = h // 2, h % 2
                            s_t, s_B = banks[hh]
                            K.op("pe", lambda e: e.matmul(
                                s_t[:, p_ * 128:(p_ + 1) * 128], lhsT=kTc[hh * 64:(hh + 1) * 64, p_, j * 128:(j + 1) * 128],
                                rhs=qTc[hh * 64:(hh + 1) * 64, i, p_, :], start=True, stop=True),
                                reads=[kTcB, qTcB], writes=[s_B])

                    def exC(idx, spi=spi, slots2=slots2):
                        banks = slots2[idx]
                        tb_t, tb_B = rot(ctp)
                        if spi is None:
                            K.dma(tb_t[:], io["cb_int"][l, :, idx * 512:(idx + 1) * 512], tb_B, writes=[tb_B], qn="pool")
                        else:
                            K.dma(tb_t[:], io["cb_sp"][l, spi, :, idx * 512:(idx + 1) * 512], tb_B, writes=[tb_B], qn="pool")
                        c_t, c_B = rot(tmpc)
                        for hh in range(2):
                            s_t, s_B = banks[hh]
                            tv = tb_t[:].rearrange("p (pp hh q) -> p hh pp q", pp=2, hh=2)[:, hh]
                            cv = c_t[:].rearrange("p (pp hh q) -> p hh pp q", pp=2, hh=2)[:, hh]
                            K.op("dve", lambda e: e.scalar_tensor_tensor(
                                out=cv, in0=s_t[:, 0:256].rearrange("p (pp q) -> p pp q", pp=2), scalar=0.125, in1=tv,
                                op0=ALU.mult, op1=ALU.add), reads=[s_B, tb_B], writes=[c_B])
                        p_t, p_B = rot(PT)
                        slots2[idx] = (p_t, p_B)
                        K.op("act", lambda e: e.activation(out=p_t[:], in_=c_t[:], func=AF.Exp), reads=[c_B], writes=[p_B])

                    def pvC(idx, cwin=cwin, slots2=slots2, accv2=accv2, acc2_B=acc2_B, nw=nw):
                        j = cwin[idx]
                        p_t, p_B = slots2[idx]
                        for h in range(4):
                            K.op("pe", lambda e, h=h: e.matmul(
                                accv2[:, h, :], lhsT=p_t[:, h * 128:(h + 1) * 128], rhs=vc[:, j, h, 0:65],
                                start=(idx == 0 and h == 0), stop=(idx == nw - 1 and h == 3)), reads=[p_B, vcB], writes=[acc2_B])

                    if 'C' in PH:
                        attn_pipe(nw, qkC, exC, pvC, 1)
                    r_t, r_B = rot(rd)
                    K.op("dve", lambda e: e.reciprocal(out=r_t[:, 0:4], in_=accv2[:, :, 64]), reads=[acc2_B], writes=[r_B])
                    fc_t, fc_B = rot(finp)
                    K.op("dve", lambda e: e.tensor_tensor(out=fc_t[:], in0=accv2[:, :, 0:64],
                                                          in1=r_t[:, 0:4].unsqueeze(2).to_broadcast([128, 4, 64]), op=ALU.mult),
                         reads=[acc2_B, r_B], writes=[fc_B])
                    SY[i] = (fb_t, fb_B, fc_t, fc_B)

                def p1_fin(i):
                    fb_t, fb_B, fc_t, fc_B = SY.pop(i)
                    m_t, m_B = rot(mixtm)
                    K.op("pool", lambda e: e.tensor_tensor(out=m_t[:, 0, :], in0=fb_t[:].rearrange("p h d -> p (h d)"),
                                                           in1=gbc[:, i, 0:256], op=ALU.mult),
                         reads=[fb_B, gbcB], writes=[m_B])
                    K.op("pool", lambda e: e.tensor_tensor(out=m_t[:, 1, :], in0=fc_t[:].rearrange("p h d -> p (h d)"),
                                                           in1=gbc[:, i, 256:512], op=ALU.mult),
                         reads=[fc_B, gbcB], writes=[m_B])
                    transpose_to([m_t[:, 0, 0:128], m_t[:, 0, 128:256], m_t[:, 1, 0:128], m_t[:, 1, 128:256]], m_B,
                                 mixT[:, 2:6, i * 128:(i + 1) * 128], mixTB)

                def run_p1_attn(tiles):
                    for k_, i in enumerate(tiles):
                        p1_attn(i)
                        if k_ >= 1:
                            p1_fin(tiles[k_ - 1])
                    p1_fin(tiles[-1])

                SY = {}
                run_p1_step1(0, NT_OWN)
                run_p1_attn(list(range(2, NT_OWN - 2)))
                run_p1_step1(NT_OWN, NT_LOC)
                run_p1_attn([0, 1, NT_OWN - 2, NT_OWN - 1])
                K.barrier()

            with ExitStack() as ph:
              if '2' in PH:
                wKV, wKVB = sb("wKV", [128, 8, 512], BF16, ph)
                wQG, wQGB = sb("wQG", [128, 8, 1024], BF16, ph)
                kTa, kTaB = sb("kTa", [128, SEQ], BF16, ph)
                va, vaB = sb("va", [128, NT_ALL, 2, 66], BF16, ph)
                kTd, kTdB = sb("kTd", [128, SEQ], BF16, ph)
                vd, vdB = sb("vd", [128, NT_ALL, 2, 66], BF16, ph)
                qTa = [sb("qTa%d" % i, [128, 2, 512], BF16, ph) for i in range(2)]
                qTd = [sb("qTd%d" % i, [128, 2, 2, 512], BF16, ph) for i in range(2)]
                gads = [sb("gad%d" % i, [128, 4, 512], GDT, ph) for i in range(2)]
                qa_tm = [sb("qatm%d" % i, [128, 256], BF16, ph) for i in range(2)]
                qd_tm = [sb("qdtm%d" % i, [128, 2, 256], BF16, ph) for i in range(2)]
                k2_tm = [sb("k2tm%d" % i, [128, 256], BF16, ph) for i in range(2)]
                zs = [sb("zs%d" % i, [128, 1024], F32, ph) for i in range(2)]

                K.op("pool", lambda e: e.memset(va[:].rearrange("p a b c -> p (a b c)"), 1.0), writes=[vaB])
                K.op("pool", lambda e: e.memset(vd[:].rearrange("p a b c -> p (a b c)"), 1.0), writes=[vdB])
                for i in range(2):
                    K.op("pool", lambda e, i=i: e.memset(qd_tm[i][0][:].rearrange("p a b -> p (a b)"), 0.0),
                         writes=[qd_tm[i][1]])
                for c in range(8):
                    load_weights_chunk(l, c, [
                        (256, 256, wKV[:, c, 0:256], wKVB), (2816, 256, wKV[:, c, 256:512], wKVB),
                        (0, 256, wQG[:, c, 0:256], wQGB), (512, 256, wQG[:, c, 256:512], wQGB),
                        (2560, 256, wQG[:, c, 512:768], wQGB), (3072, 256, wQG[:, c, 768:1024], wQGB)])


                def staged(n, stages):
                    ns = len(stages)
                    for step in range(n + ns - 1):
                        for si in reversed(range(ns)):
                            idx = step - si
                            if 0 <= idx < n:
                                stages[si](idx)

                S1 = {}

                kvx = xb + [(xs[i_][0][:].bitcast(BF16)[:, 0:1024], xs[i_][1]) for i_ in range(3)]

                def kv_a(t):
                    if l == 0:
                        x_t, x_B = load_x(x_src_full[t * 128:(t + 1) * 128, :], xfB(t * 128))
                    else:
                        x_t, x_B = rot(kvx)
                        K.dma(x_t[:], x_src_full[t * 128:(t + 1) * 128, :], x_B, reads=xfB(t * 128), writes=[x_B])
                    tb_t, tb_B = rot(tabt)
                    K.dma(tb_t[:], io["tabKV"][t * 128:(t + 1) * 128, :], tb_B, writes=[tb_B], qn="act")
                    S1[t] = [x_t, x_B, tb_t, tb_B]

                def kv_cast(t):
                    x_t, x_B = S1[t][0:2]
                    S1[t] += list(prep_cast(x_t, x_B)) if l == 0 else [x_t, x_B]

                def kv_stats(t):
                    x_t, x_B = S1[t][0:2]
                    S1[t] += list(prep_stats(x_t, x_B))

                def kv_b2(t):
                    x_t, x_B, tb_t, tb_B, b_t, b_B, s_t, s_B = S1[t]
                    T_t, T_B = prep_b(b_t, b_B)
                    S1[t] = [x_t, x_B, tb_t, tb_B, T_t, T_B, s_t, s_B]

                def kv_c(t):
                    x_t, x_B, tb_t, tb_B, T_t, T_B, s_t, s_B = S1[t]
                    z_t, z_B = rot(zs)
                    proj(T_t, T_B, wKV, wKVB, [(0, 512)], z_t, z_B, s_t, s_B, evac="act")
                    S1[t] = (z_t, z_B, tb_t, tb_B)

                def kv_d(t):
                    z_t, z_B, tb_t, tb_B = S1[t]
                    k_t, k_B = rot(k2_tm)
                    hb = headnorm_a(z_t, z_B, 0, 2)
                    rope(z_t, z_B, 256, 2, 16, tb_t, tb_B, 128, [(k_t[:, 128:256], lambda a: a)], k_B, mul_eng="dve")
                    headnorm_b(z_t, z_B, 0, 2, vecs[:, V_KN:V_KN + 64], hb)
                    rope(z_t, z_B, 0, 2, 16, tb_t, tb_B, 0, [(k_t[:, 0:128], lambda a: a)], k_B, mul_eng="dve")
                    K.op("act", lambda e: e.activation(out=va[:, t, :, 0:64],
                                                       in_=z_t[:, 128:256].rearrange("p (h d) -> p h d", h=2), func=AF.Copy),
                         reads=[z_B], writes=[vaB])
                    K.op("act", lambda e: e.activation(out=vd[:, t, :, 0:64],
                                                       in_=z_t[:, 384:512].rearrange("p (h d) -> p h d", h=2), func=AF.Copy),
                         reads=[z_B], writes=[vdB])
                    S1[t] = (k_t, k_B)

                def kv_e(t):
                    k_t, k_B = S1.pop(t)
                    transpose_multi([(k_t[:, 0:128], k_B), (k_t[:, 128:256], k_B)],
                                    [(kTa[:, t * 128:(t + 1) * 128], kTaB, 0, 1, "act"),
                                     (kTd[:, t * 128:(t + 1) * 128], kTdB, 1, 1, "act")])

                run_pipe(NT_ALL, [(0, kv_a), (1, kv_cast), (1, kv_stats), (2, kv_b2), (3, kv_c), (4, kv_d), (5, kv_e)])

                def make_qproj(grp_):
                    qa_t_, qa_B_ = qTa[grp_ % 2]
                    qd_t_, qd_B_ = qTd[grp_ % 2]
                    gad_, gadB_ = gads[grp_ % 2]
                    S2 = {}

                    def st0(qt):
                        ti = grp_ * 4 + qt
                        x_t, x_B = load_x(x_src_own[ti * 128:(ti + 1) * 128, :], xoB)
                        tb_t, tb_B = rot(tabt)
                        K.dma(tb_t[:], io["tabQ"][ti * 128:(ti + 1) * 128, :], tb_B, writes=[tb_B], qn="act")
                        b_t, b_B, s_t, s_B = prep_a(x_t, x_B)
                        S2[qt] = (tb_t, tb_B, b_t, b_B, s_t, s_B)

                    def st1a(qt):
                        tb_t, tb_B, b_t, b_B, s_t, s_B = S2[qt]
                        T_t, T_B = prep_b(b_t, b_B, eng="dve")
                        S2[qt] = (tb_t, tb_B, T_t, T_B, s_t, s_B)

                    def st1b(qt):
                        tb_t, tb_B, T_t, T_B, s_t, s_B = S2[qt]
                        z_t, z_B = rot(zs)
                        proj(T_t, T_B, wQG, wQGB, [(0, 512), (512, 512)], z_t, z_B, s_t, s_B)
                        S2[qt] = (z_t, z_B, tb_t, tb_B)

                    def st2(qt):
                        z_t, z_B, tb_t, tb_B = S2[qt]
                        silu_gate(z_t, z_B, 256, 256, gad_[:, qt, 0:256], gadB_)
                        silu_gate(z_t, z_B, 768, 256, gad_[:, qt, 256:512], gadB_)
                        headnorm(z_t, z_B, 0, 4, vecs[:, V_QN:V_QN + 64])
                        a_t, a_B = rot(qa_tm)
                        rope(z_t, z_B, 0, 4, 16, tb_t, tb_B, 0,
                             [(a_t[:, 0:256].rearrange("p (g kv d) -> p kv g d", g=2, kv=2),
                               lambda a: a.rearrange("p (kv g d) -> p kv g d", kv=2, g=2))], a_B)
                        d_t, d_B = rot(qd_tm)
                        dsts = []
                        for c in range(2):
                            dsts.append((
                                d_t[:, c, :].rearrange("p (g kv c k) -> p kv g c k", g=2, kv=2, c=2)[:, :, :, c, :],
                                lambda a, c=c: a.rearrange("p (kv g c k) -> p kv g c k", kv=2, g=2, c=2)[:, :, :, c, :]))
                        rope(z_t, z_B, 512, 4, 16, tb_t, tb_B, 128, dsts, d_B)
                        S2[qt] = (a_t, a_B, d_t, d_B)

                    def st3(qt):
                        a_t, a_B, d_t, d_B = S2.pop(qt)
                        transpose_multi(
                            [(a_t[:, 0:128], a_B), (a_t[:, 128:256], a_B), (d_t[:, 0, 0:128], d_B), (d_t[:, 0, 128:256], d_B),
                             (d_t[:, 1, 0:128], d_B), (d_t[:, 1, 128:256], d_B)],
                            [(qa_t_[:, :, qt * 128:(qt + 1) * 128], qa_B_, 0, 2, "dve"),
                             (qd_t_[:, 0, :, qt * 128:(qt + 1) * 128], qd_B_, 2, 2, "dve"),
                             (qd_t_[:, 1, :, qt * 128:(qt + 1) * 128], qd_B_, 4, 2, "dve")])

                    def boundary(b_):
                        items = []
                        for si, fn in ((1, st1a), (3, st3), (1, st1b), (2, st2), (0, st0)):
                            qt = b_ - si
                            if 0 <= qt < 4:
                                items.append(lambda fn=fn, qt=qt: fn(qt))
                        return items
                    return boundary

                nxt = make_qproj(0)
                for b_ in range(7):
                    for it_ in nxt(b_):
                        it_()
                deferred = []
                for grp in range(4):
                    qa_t, qa_B = qTa[grp % 2]
                    qd_t, qd_B = qTd[grp % 2]
                    gad, gadB = gads[grp % 2]
                    nxt = make_qproj(grp + 1) if grp < 3 else (lambda b_: [])
                    bcount = [0]
                    pending = []

                    def loop_sched():
                        pending.extend(deferred)
                        del deferred[:]
                        pending.extend(nxt(bcount[0]))
                        bcount[0] += 1

                    def tick():
                        if pending:
                            pending.pop(0)()

                    def drain():
                        while pending:
                            pending.pop(0)()
                    mids = {i_: tick for i_ in range(4, 62, 2)}

                    ma_t, ma_B = rot(mixtm)
                    for g in range(2):
                        accs = [pz[4], pz[5]]
                        accvs = [a_[0][:, 0:260].rearrange("p (a b) -> p a b", b=65) for a_ in accs]
                        slots = {}

                        def qkA(j):
                            pi = rot([0, 1], "pzpair")
                            banks = [pzs[2 * pi], pzs[2 * pi + 1]]
                            slots[j] = (pi, banks)
                            for kv in range(2):
                                s_t, s_B = banks[kv]
                                K.op("pe", lambda e: e.matmul(s_t[:], lhsT=kTa[kv * 64:(kv + 1) * 64, j * 128:(j + 1) * 128],
                                                              rhs=qa_t[kv * 64:(kv + 1) * 64, g, :], start=True, stop=True),
                                     reads=[kTaB, qa_B], writes=[s_B])

                        def exA(j):
                            pi, banks = slots[j]
                            qi = rot([0, 1, 2], "ptpair")
                            pts = [PT[2 * qi], PT[2 * qi + 1]]
                            K.op("act", lambda e: e.activation(out=PT2[qi][:], in_=pzd[pi][:], func=AF.Exp),
                                 reads=[banks[0][1], banks[1][1]], writes=[pts[0][1], pts[1][1]])
                            slots[j] = pts

                        def pvA(j):
                            pts = slots.pop(j)
                            for kv in range(2):
                                p_t, p_B = pts[kv]
                                for qt in range(4):
                                    K.op("pe", lambda e: e.matmul(accvs[kv][:, qt, :], lhsT=p_t[:, qt * 128:(qt + 1) * 128],
                                                                  rhs=va[:, j, kv, 0:65], start=(j == 0 and qt == 0),
                                                                  stop=(j == NT_ALL - 1 and qt == 3)),
                                         reads=[p_B, vaB], writes=[accs[kv][1]])

                        loop_sched()
                        attn_pipe(NT_ALL, qkA, exA, pvA, 2, mids)
                        drain()
                        for kv in range(2):
                            h = 2 * kv + g
                            accv, acc_B = accvs[kv], accs[kv][1]
                            r_t, r_B = rot(rd)
                            K.op("dve", lambda e: e.reciprocal(out=r_t[:, 0:4], in_=accv[:, :, 64]), reads=[acc_B], writes=[r_B])
                            f_t, f_B = rot(fin)
                            K.op("dve", lambda e: e.tensor_tensor(out=f_t[:], in0=accv[:, :, 0:64],
                                                                  in1=r_t[:, 0:4].unsqueeze(2).to_broadcast([128, 4, 64]),
                                                                  op=ALU.mult), reads=[acc_B, r_B], writes=[f_B])
                            K.op("pool", lambda e: e.tensor_tensor(out=ma_t[:, :, h * 64:(h + 1) * 64], in0=f_t[:],
                                                                   in1=gad[:, :, h * 64:(h + 1) * 64], op=ALU.mult),
                                 reads=[f_B, gadB], writes=[ma_B])
                    for qt in range(4):
                        deferred.append(lambda ma_t=ma_t, ma_B=ma_B, grp=grp, qt=qt: transpose_to(
                            [ma_t[:, qt, 0:128], ma_t[:, qt, 128:256]], ma_B,
                            mixT[:, 0:2, (grp * 4 + qt) * 128:(grp * 4 + qt + 1) * 128], mixTB))

                    md_t, md_B = rot(mixtm)
                    for half in range(2):
                        for g in range(2):
                            accs = [pz[4], pz[5]]
                            accvs = [a_[0][:, 0:260].rearrange("p (c a b) -> p c a b", c=2, b=65) for a_ in accs]
                            slots = {}

                            def qkD(j):
                                pi = rot([0, 1], "pzpair")
                                banks = [pzs[2 * pi], pzs[2 * pi + 1]]
                                slots[j] = (pi, banks)
                                for c in range(2):
                                    for kv in range(2):
                                        s_t, s_B = banks[kv]
                                        K.op("pe", lambda e: e.matmul(
                                            s_t[:, c * 256:(c + 1) * 256], lhsT=kTd[kv * 64:(kv + 1) * 64, j * 128:(j + 1) * 128],
                                            rhs=qd_t[kv * 64:(kv + 1) * 64, c, g, half * 256:(half + 1) * 256],
                                            start=True, stop=True), reads=[kTdB, qd_B], writes=[s_B])

                            def exD(j):
                                pi, banks = slots[j]
                                qi = rot([0, 1, 2], "ptpair")
                                pts = [PT[2 * qi], PT[2 * qi + 1]]
                                K.op("act", lambda e: e.activation(out=PT2[qi][:], in_=pzd[pi][:], func=AF.Exp),
                                     reads=[banks[0][1], banks[1][1]], writes=[pts[0][1], pts[1][1]])
                                slots[j] = pts

                            def pvD(j):
                                pts = slots.pop(j)
                                for kv in range(2):
                                    p_t, p_B = pts[kv]
                                    for c in range(2):
                                        for qt in range(2):
                                            K.op("pe", lambda e: e.matmul(
                                                accvs[kv][:, c, qt, :], lhsT=p_t[:, c * 256 + qt * 128:c * 256 + (qt + 1) * 128],
                                                rhs=vd[:, j, kv, 0:65], start=(j == 0 and c == 0 and qt == 0),
                                                stop=(j == NT_ALL - 1 and c == 1 and qt == 1)),
                                                reads=[p_B, vdB], writes=[accs[kv][1]])

                            loop_sched()
                            if half == 1 and g == 1:
                                loop_sched()
                            attn_pipe(NT_ALL, qkD, exD, pvD, 2, mids)
                            drain()
                            eps_ = []
                            for kv in range(2):
                                h = 2 * kv + g
                                av, acc_B = accvs[kv], accs[kv][1]
                                r_t, r_B = rot(rd)
                                K.op("dve", lambda e: e.reciprocal(out=r_t[:, 0:4].rearrange("p (c a) -> p c a", c=2), in_=av[:, :, :, 64]),
                                     reads=[acc_B], writes=[r_B])
                                K.op("dve", lambda e: e.tensor_scalar(out=r_t[:, 2:4], in0=r_t[:, 2:4], scalar1=lamt[:, 5:6],
                                                                      scalar2=None, op0=ALU.mult), reads=[r_B, lamtB], writes=[r_B])
                                f_t, f_B = rot(fin)
                                g_t, g_B = rot(fin2)
                                K.op("dve", lambda e: e.tensor_tensor(out=f_t[:], in0=av[:, :, :, 0:64].rearrange("p c a d -> p (c a) d"),
                                                                      in1=r_t[:, 0:4].unsqueeze(2).to_broadcast([128, 4, 64]),
                                                                      op=ALU.mult), reads=[acc_B, r_B], writes=[f_B])
                                eps_.append((h, f_t, f_B, g_t, g_B))
                            for (h, f_t, f_B, g_t, g_B) in eps_:
                                K.op("pool", lambda e: e.tensor_tensor(out=f_t[:, 0:2, :], in0=f_t[:, 0:2, :], in1=f_t[:, 2:4, :], op=ALU.add),
                                     reads=[f_B], writes=[f_B])
                                K.op("pool", lambda e: e.tensor_tensor(out=g_t[:, 0:2, :], in0=f_t[:, 0:2, :], in1=f_t[:, 0:2, :], op=ALU.mult),
                                     reads=[f_B], writes=[g_B])
                                h_t, h_B = rot(hn)
                                K.op("dve", lambda e: e.tensor_reduce(out=h_t[:, 0:2], in_=g_t[:, 0:2, :], axis=AX.X, op=ALU.add),
                                     reads=[g_B], writes=[h_B])
                                K.op("pool", lambda e: e.tensor_scalar(out=h_t[:, 0:2], in0=h_t[:, 0:2], scalar1=1.0 / 64, scalar2=EPS,
                                                                       op0=ALU.mult, op1=ALU.add), reads=[h_B], writes=[h_B])
                                K.op("pool", lambda e: e.tensor_tensor(out=h_t[:, 0:2], in0=h_t[:, 0:2], in1=mhalf[:, 0:2], op=ALU.pow),
                                     reads=[h_B, mhalfB], writes=[h_B])
                                K.op("dve", lambda e: e.tensor_tensor(out=f_t[:, 0:2, :], in0=f_t[:, 0:2, :],
                                                                      in1=h_t[:, 0:2].unsqueeze(2).to_broadcast([128, 2, 64]), op=ALU.mult),
                                     reads=[f_B, h_B], writes=[f_B])
                                K.op("dve", lambda e: e.tensor_tensor(out=f_t[:, 0:2, :], in0=f_t[:, 0:2, :],
                                                                      in1=subg[:].unsqueeze(1).to_broadcast([128, 2, 64]), op=ALU.mult),
                                     reads=[f_B, subgB], writes=[f_B])
                                K.op("pool", lambda e: e.tensor_tensor(
                                    out=md_t[:, half * 2:half * 2 + 2, h * 64:(h + 1) * 64], in0=f_t[:, 0:2, :],
                                    in1=gad[:, half * 2:half * 2 + 2, 256 + h * 64:256 + (h + 1) * 64], op=ALU.mult),
                                    reads=[f_B, gadB], writes=[md_B])
                    for qt in range(4):
                        deferred.append(lambda md_t=md_t, md_B=md_B, grp=grp, qt=qt: transpose_to(
                            [md_t[:, qt, 0:128], md_t[:, qt, 128:256]], md_B,
                            mixT[:, 6:8, (grp * 4 + qt) * 128:(grp * 4 + qt + 1) * 128], mixTB))
                while deferred:
                    deferred.pop(0)()
                K.barrier()

            with ExitStack() as ph:
              if '3' in PH:
                wo, woB = sb("wo", [128, 8, 1024], BF16, ph)
                fg, fgB = sb("fg", [128, 1024], F32, ph)
                xn = [sb("xn%d" % i, [128, 1024], F32, ph) for i in range(2)]
                yo = [sb("yo%d" % i, [128, 1024], F32, ph) for i in range(2)]
                yb = [sb("yb%d" % i, [128, 1024], BF16, ph) for i in range(4)]
                K.dma(fg[:], io["fg"][:, :], fgB, writes=[fgB])
                for c in range(8):
                    w_t, w_B = rot(xs)
                    K.dma(w_t[:], io["w_out"][l, c * 128:(c + 1) * 128, :], w_B, writes=[w_B])
                    if c % 2:
                        K.op("act", lambda e, c=c: e.activation(out=wo[:, c, :], in_=w_t[:], func=AF.Copy), reads=[w_B], writes=[woB])
                    else:
                        K.op("dve", lambda e, c=c: e.tensor_copy(out=wo[:, c, :], in_=w_t[:]), reads=[w_B], writes=[woB])
                for ti in range(NT_OWN):
                    x_t, x_B = load_x(x_src_own[ti * 128:(ti + 1) * 128, :], xoB)
                    n_t, n_B = rot(xn)
                    for n in range(2):
                        p_t, p_B = rot(pzs, "pzproj")
                        for c in range(8):
                            K.op("pe", lambda e, c=c, n=n: e.matmul(p_t[:], lhsT=mixT[:, c, ti * 128:(ti + 1) * 128],
                                                                    rhs=wo[:, c, n * 512:(n + 1) * 512],
                                                                    start=(c == 0), stop=(c == 7)),
                                 reads=[mixTB, woB], writes=[p_B])
                        K.op("dve", lambda e, n=n: e.tensor_tensor(out=n_t[:, n * 512:(n + 1) * 512], in0=p_t[:],
                                                                   in1=x_t[:, n * 512:(n + 1) * 512], op=ALU.add),
                             reads=[p_B, x_B], writes=[n_B])
                    if not fused:
                        K.dma(io["xnext"][ti * 128:(ti + 1) * 128, :], n_t[:], n_B, reads=[n_B])
                    elif l < L - 1:
                        K.dma(io["x1own"][ti * 128:(ti + 1) * 128, :], n_t[:], n_B, reads=[n_B], writes=[x1ownB])
                        for m in range(4):
                            y_t, y_B = rot(yb)
                            if m % 2:
                                K.op("act", lambda e: e.activation(out=y_t[:], in_=n_t[:], func=AF.Copy, scale=rsel[:, m:m + 1]),
                                     reads=[n_B, rselB], writes=[y_B])
                            else:
                                K.op("dve", lambda e: e.tensor_scalar(
                                    out=y_t[:], in0=n_t[:], scalar1=rsel[:, m:m + 1], scalar2=None, op0=ALU.mult),
                                    reads=[n_B, rselB], writes=[y_B])
                            K.dma(io["xg_src"][m * TOWN + ti * 128:m * TOWN + (ti + 1) * 128, :], y_t[:], y_B,
                                  reads=[y_B], writes=[xgsB[(m * TOWN + ti * 128) // 1024]], qn=("act" if m % 2 == 0 else "pool"))
                        if ti % 8 == 7:
                            for m in range(4):
                                ch = m * 2 + ti // 8
                                K.collective("AllReduce", ALU.add, [[0, 1, 2, 3], [4, 5, 6, 7]],
                                             io["xg_src"][ch * 1024:(ch + 1) * 1024, :], io["xg_dst"][ch * 1024:(ch + 1) * 1024, :],
                                             xgsB[ch], xgdB[ch])
                    if l == L - 1:
                        s_t, s_B = rot(st1)
                        stats_rstd(n_t[:], n_B, 1024, s_t, s_B)
                        y_t, y_B = rot(yo)
                        K.op("dve", lambda e: e.scalar_tensor_tensor(out=y_t[:], in0=n_t[:], scalar=s_t[:, 2:3], in1=fg[:],
                                                                     op0=ALU.mult, op1=ALU.mult),
                             reads=[n_B, s_B, fgB], writes=[y_B])
                        K.dma(io["y"][ti * 128:(ti + 1) * 128, :], y_t[:], y_B, reads=[y_B])
                K.barrier()
        K.barrier(final=True)
        stats = {k: v.cnt for k, v in K.q.items()}
        stats["nsem"] = K.nsem
    build_program.stats = stats
    return nc


def _rope_cs(pos, d):
    inv = np.power(np.float32(10000.0), -(np.arange(0, d, 2, dtype=np.float32) / np.float32(d))).astype(np.float32)
    ang = (pos.astype(np.float32)[:, None] * inv[None, :]).astype(np.float32)
    return np.cos(ang.astype(np.float64)).astype(np.float32), np.sin(ang.astype(np.float64)).astype(np.float32)


def _tab_A(pos):
    row, col = pos // 64, pos % 64
    cr, sr = _rope_cs(row, 32)
    cc, sc = _rope_cs(col, 32)
    return np.concatenate([cr, cr, cc, cc, -sr, sr, -sc, sc], axis=1)


def _tab_D(pos):
    c, s = _rope_cs(pos, 32)
    return np.concatenate([c, c, c, c, -s, s, -s, s], axis=1)


def _tab_B(pos):
    c, s = _rope_cs(pos, 64)
    return np.concatenate([c, c, -s, s], axis=1)


def _c_bias_table(rpb_l, r, i, win):
    a = np.arange(128)
    ks = (TOWN * r - 256 + 128 * np.asarray(win)[:, None] + a[None, :])
    qt = TOWN * r + 128 * i + a
    kvalid = (ks >= 0) & (ks < SEQ)
    ksc = np.clip(ks, 0, SEQ - 1)
    kr, kc = ksc // 64, ksc % 64
    qr, qc = qt // 64, qt % 64
    rs = np.clip(qr - 4, 0, 120)
    cs = np.clip(qc - 8, 0, 48)
    ok = (kvalid[:, :, None] & (kr[:, :, None] >= rs[None, None, :]) & (kr[:, :, None] < rs[None, None, :] + 8)
          & (kc[:, :, None] >= cs[None, None, :]) & (kc[:, :, None] < cs[None, None, :] + 16))
    dr = np.clip(kr[:, :, None] - qr[None, None, :] + 7, 0, 14)
    dc = np.clip(kc[:, :, None] - qc[None, None, :], -15, 15) + 15
    g = rpb_l[:, dr, dc]
    out = np.where(ok[None], g, np.float32(MASKV)).astype(np.float32)
    return np.ascontiguousarray(out.transpose(2, 1, 0, 3))


def _core_inputs(c, xfull, xown, layers, P):
    r = c % 4
    m = {"xfull": xfull, "xown": xown}
    m["w_in"] = np.ascontiguousarray(P["w_in"][layers])
    m["w_out"] = np.ascontiguousarray(P["w_out"][layers])
    m["gcol"] = np.ascontiguousarray(P["norm_g"][layers].reshape(len(layers), 8, 128).transpose(0, 2, 1))
    vec = np.zeros((len(layers), V_N), np.float32)
    for k, l in enumerate(layers):
        lam_init = np.float32(0.8 - 0.6 * np.exp(-0.3 * l))
        vec[k, V_QN:V_QN + 64] = P["qn_a"][l]
        vec[k, V_KN:V_KN + 64] = P["kn_a"][l]
        vec[k, V_SUB:V_SUB + 64] = P["subln_d"][l]
        vec[k, V_SINK:V_SINK + 4] = P["sink_b"][l]
        vec[k, V_LQ1:V_LQ1 + 32] = P["lam_q1"][l]
        vec[k, V_LK1:V_LK1 + 32] = P["lam_k1"][l]
        vec[k, V_LQ2:V_LQ2 + 32] = P["lam_q2"][l]
        vec[k, V_LK2:V_LK2 + 32] = P["lam_k2"][l]
        vec[k, V_LC] = lam_init
        vec[k, V_LC + 1] = np.float32(1.0) - lam_init
    m["vecs"] = np.ascontiguousarray(np.broadcast_to(vec[:, None, :], (len(layers), 128, V_N)))
    m["fg"] = np.ascontiguousarray(np.broadcast_to(P["final_g"][None, :], (128, D_MODEL)))
    pos_all = np.arange(SEQ)
    m["tabKV"] = np.ascontiguousarray(np.concatenate([_tab_A(pos_all), _tab_D(pos_all)], axis=1))
    pos_own = TOWN * r + np.arange(TOWN)
    m["tabQ"] = np.ascontiguousarray(np.concatenate(
        [_tab_A(pos_own) * np.float32(0.125), _tab_D(pos_own) * np.float32(32 ** -0.5)], axis=1).astype(np.float32))
    pos_loc = np.clip(TOWN * r - 256 + np.arange(NT_LOC * 128), 0, SEQ - 1)
    tb = _tab_B(pos_loc)
    m["tabB"] = np.ascontiguousarray(np.concatenate([tb, tb * np.float32(0.125)], axis=1).astype(np.float32))
    cbi = np.zeros((len(layers), 128, 5 * 512), np.float32)
    cbsp = np.full((len(layers), 4, 128, 6 * 512), MASKV, np.float32)
    for k, l in enumerate(layers):
        rpb = P["rpb_c"][l]
        cbi[k] = _c_bias_table(rpb, r, 5, list(range(5, 10))).reshape(128, -1)
        for si, (i, win) in enumerate([(0, list(range(0, 6))), (1, list(range(1, 6))),
                                       (14, list(range(14, 19))), (15, list(range(14, 20)))]):
            t = _c_bias_table(rpb, r, i, win).reshape(128, -1)
            cbsp[k, si, :, :t.shape[1]] = t
    m["cb_int"] = cbi
    m["cb_sp"] = cbsp
    a = np.arange(128)
    mprev = (a[None, :] <= a[:, None]).astype(np.float32)
    mnext = (a[:, None] <= a[None, :]).astype(np.float32)
    z = np.zeros_like(mprev)
    m["bmask"] = np.ascontiguousarray(np.concatenate(
        [mprev if r > 0 else z, mprev, mnext, mnext if r < 3 else z], axis=1))
    sel = np.zeros((6,), np.float32)
    if r >= 1:
        sel[r - 1] = 1.0
    if r <= 2:
        sel[3 + r] = 1.0
    m["halosel"] = np.ascontiguousarray(np.broadcast_to(sel[None, :], (128, 6)))
    return m


_NC_CACHE = {}


def _get_nc(n_layers, fused):
    key = (n_layers, fused)
    if key not in _NC_CACHE:
        _NC_CACHE[key] = build_program(n_layers, fused)
    return _NC_CACHE[key]


def kernel(x, norm_g, w_in, w_out, qn_a, kn_a, sink_b, rpb_c, lam_q1, lam_k1, lam_q2, lam_k2, subln_d, final_g):
    P = dict(norm_g=norm_g, w_in=w_in, w_out=w_out, qn_a=qn_a, kn_a=kn_a, sink_b=sink_b, rpb_c=rpb_c,
             lam_q1=lam_q1, lam_k1=lam_k1, lam_q2=lam_q2, lam_k2=lam_k2, subln_d=subln_d, final_g=final_g)
    P = {k: np.asarray(v, dtype=np.float32) for k, v in P.items()}
    x = np.asarray(x, dtype=np.float32)
    nc = _get_nc(DEPTH, True)
    in_maps = []
    for c in range(NCORES):
        b, r = c // 4, c % 4
        m = _core_inputs(c, np.ascontiguousarray(x[b]), np.ascontiguousarray(x[b, r * TOWN:(r + 1) * TOWN]),
                         list(range(DEPTH)), P)
        rs = np.zeros((4,), np.float32)
        rs[r] = 1.0
        m["rsel"] = np.ascontiguousarray(np.broadcast_to(rs[None, :], (128, 4)))
        in_maps.append(m)
    res = run_bass_kernel_spmd(nc, in_maps, core_ids=list(range(NCORES)))
    y = np.empty_like(x)
    for c in range(NCORES):
        b, r = c // 4, c % 4
        y[b, r * TOWN:(r + 1) * TOWN] = res.results[c]["y"]
    return y
```

```python
import os
import numpy as np
from contextlib import ExitStack
import concourse.bass as bass
import concourse.mybir as mybir
from concourse.bass_utils import run_bass_kernel_spmd

F32 = mybir.dt.float32
GDT = mybir.dt.bfloat16
BF16 = mybir.dt.bfloat16
AF = mybir.ActivationFunctionType
ALU = mybir.AluOpType
AX = mybir.AxisListType

D_MODEL = 1024
SEQ = 8192
BATCH = 2
DEPTH = 2
D_IN = 3328
NCORES = 8
TOWN = 2048
NT_OWN = 16
NT_ALL = 64
NT_LOC = 20
EPS = 1e-6
MASKV = -30000.0

V_QN, V_KN, V_SUB, V_SINK, V_LQ1, V_LK1, V_LQ2, V_LK2, V_LC = 0, 64, 128, 192, 196, 228, 260, 292, 324
V_N = 328


class Buf:
    __slots__ = ("w", "r", "dsem", "dcnt", "name")
    ALL = []

    def __init__(self, name=""):
        self.w = None
        self.r = {}
        self.dsem = None
        self.dcnt = 0
        self.name = name
        Buf.ALL.append(self)


class Q:
    def __init__(self, name, eng, sem):
        self.name, self.eng, self.sem = name, eng, sem
        self.cnt = 0
        self.seen = {}


class Ctx:
    def __init__(self, nc, st):
        self.nc, self.st = nc, st
        self.q = {}
        for name, eng in (("pe", nc.tensor), ("act", nc.scalar), ("dve", nc.vector),
                          ("pool", nc.gpsimd), ("sp", nc.sync)):
            self.q[name] = Q(name, eng, st.enter_context(nc.semaphore("q_" + name)))
        self.dbufs = []
        self.nsem = 5

    def _wait(self, q, deps):
        for key, (s, v) in deps.items():
            if q.seen.get(key, 0) >= v:
                continue
            q.eng.wait_ge(s, v)
            q.seen[key] = v

    def _deps(self, q, reads, writes):
        deps = {}

        def add(tok, raw):
            if tok is None:
                return
            s, v = tok
            if s is q.sem and (q.name == "pe" or not raw):
                return
            k = id(s)
            if k not in deps or deps[k][1] < v:
                deps[k] = (s, v)
        for b in reads:
            add(b.w, True)
        for b in writes:
            add(b.w, True)
            for tok in b.r.values():
                add(tok, True)
        return deps

    def op(self, qn, fn, reads=(), writes=()):
        q = self.q[qn]
        self._wait(q, self._deps(q, reads, writes))
        ins = fn(q.eng)
        q.cnt += 1
        ins.then_inc(q.sem, 1)
        tok = (q.sem, q.cnt)
        for b in reads:
            b.r[id(q.sem)] = tok
        for b in writes:
            b.w = tok
            b.r = {}
        return ins

    def dma(self, out, in_, sb, reads=(), writes=(), qn="sp"):
        q = self.q[qn]
        if sb.dsem is None:
            sb.dsem = self.st.enter_context(self.nc.semaphore("d_%d" % self.nsem))
            self.nsem += 1
            self.dbufs.append(sb)
        self._wait(q, self._deps(q, reads, writes))
        ins = q.eng.dma_start(out=out, in_=in_)
        sb.dcnt += 16
        ins.then_inc(sb.dsem, 16)
        tok = (sb.dsem, sb.dcnt)
        for b in reads:
            b.r[id(sb.dsem)] = tok
        for b in writes:
            b.w = tok
            b.r = {}

    def collective(self, kind, op, groups, in_ap, out_ap, in_B, out_B):
        q = self.q["pool"]
        self._wait(q, self._deps(q, [in_B], [out_B]))
        sem = self.st.enter_context(self.nc.semaphore("cc_%d" % self.nsem))
        self.nsem += 1
        ins = q.eng.collective_compute(kind, op, replica_groups=groups, ins=[in_ap], outs=[out_ap])
        ins.then_inc(sem)
        tok = (sem, 1)
        in_B.r[id(sem)] = tok
        out_B.w = tok
        out_B.r = {}
        self.ccs = getattr(self, "ccs", []) + [tok]

    def renew(self):
        self.barrier()
        ccsems = set(id(cs) for (cs, cv) in getattr(self, "ccs", []))
        for b in Buf.ALL:
            if b.w is not None and id(b.w[0]) in ccsems:
                b.r = {}
                continue
            b.w = None
            b.r = {}
        for name, qq in self.q.items():
            qq.sem = self.st.enter_context(self.nc.semaphore("q%d_%s" % (self.nsem, name)))
            self.nsem += 1
            qq.cnt = 0
            qq.seen = {}
        for b in self.dbufs:
            pass

    def barrier(self, final=False):
        sp = self.q["sp"]
        deps = {}
        for b in self.dbufs:
            if b.dcnt:
                deps[id(b.dsem)] = (b.dsem, b.dcnt)
        if final:
            for (cs, cv) in getattr(self, "ccs", []):
                deps[id(cs)] = (cs, cv)
        for qq in self.q.values():
            if qq is not sp and qq.cnt:
                deps[id(qq.sem)] = (qq.sem, qq.cnt)
        self._wait(sp, deps)
        self.op("sp", lambda e: e.nop())
        for qq in self.q.values():
            if qq is sp:
                continue
            d = {id(sp.sem): (sp.sem, sp.cnt)}
            self._wait(qq, d)


def build_program(n_layers, fused):
    PH = os.environ.get('KPH', '123abBC')
    nc = bass.Bass("TRN2", target_bir_lowering=False)
    L = n_layers

    def din(name, shape, dt=F32):
        return nc.dram_tensor(name, shape, dt, kind="ExternalInput").ap()

    io = dict(
        xfull=din("xfull", [SEQ, D_MODEL]),
        xown=din("xown", [TOWN, D_MODEL]),
        w_in=din("w_in", [L, D_MODEL, D_IN]),
        w_out=din("w_out", [L, D_MODEL, D_MODEL]),
        gcol=din("gcol", [L, 128, 8]),
        vecs=din("vecs", [L, 128, V_N]),
        fg=din("fg", [128, D_MODEL]),
        tabKV=din("tabKV", [SEQ, 256]),
        tabQ=din("tabQ", [TOWN, 256]),
        tabB=din("tabB", [NT_LOC * 128, 256]),
        cb_int=din("cb_int", [L, 128, 5 * 512]),
        cb_sp=din("cb_sp", [L, 4, 128, 6 * 512]),
        bmask=din("bmask", [128, 512]),
        halosel=din("halosel", [128, 6]),
        y=nc.dram_tensor("y", [TOWN, D_MODEL], F32, kind="ExternalOutput").ap(),
    )
    if fused:
        io["rsel"] = din("rsel", [128, 4])
        io["x1own"] = nc.dram_tensor("x1own", [TOWN, D_MODEL], F32, kind="Internal").ap()
        io["xg_src"] = nc.dram_tensor("xg_src", [SEQ, D_MODEL], BF16, kind="Internal").ap()
        io["xg_dst"] = nc.dram_tensor("xg_dst", [SEQ, D_MODEL], BF16, kind="Internal").ap()
    else:
        io["xnext"] = nc.dram_tensor("xnext", [TOWN, D_MODEL], F32, kind="ExternalOutput").ap()
    x1ownB = Buf("x1own")
    xgsB = [Buf("xg_src%d" % i) for i in range(8)]
    xgdB = [Buf("xg_dst%d" % i) for i in range(8)]

    with ExitStack() as st:
        K = Ctx(nc, st)
        E = st.enter_context

        uniq = [0]

        def sb(name, shape, dt, stack=None):
            uniq[0] += 1
            t = (stack or st).enter_context(nc.sbuf_tensor("s%d_%s" % (uniq[0], name), shape, dt))
            return t, Buf(name)

        ident, identB = sb("ident", [128, 128], BF16)
        identf, identfB = sb("identf", [128, 128], F32)
        pT = [(E(nc.psum_tensor("pT%d" % i, [128, 1024], BF16)), Buf("pT%d" % i)) for i in range(2)]
        pzd = [E(nc.psum_tensor("pzd%d" % i, [128, 1024], F32)) for i in range(2)]
        pz = [(pzd[i // 2][:, (i % 2) * 512:(i % 2 + 1) * 512], Buf("pz%d" % i)) for i in range(4)]
        pz += [(E(nc.psum_tensor("pz%d" % i, [128, 512], F32)), Buf("pz%d" % i)) for i in (4, 5)]
        pzs = pz[:4]
        xs = [sb("xs%d" % i, [128, 1024], F32) for i in range(3)]
        xb = [sb("xb%d" % i, [128, 1024], BF16) for i in range(2)]
        xT = [sb("xT%d" % i, [128, 8, 128], BF16) for i in range(3)]
        junk, junkB = sb("junk", [128, 1024], BF16)
        st1 = [sb("st1_%d" % i, [128, 8], F32) for i in range(4)]
        hn = [sb("hn%d" % i, [128, 8], F32) for i in range(3)]
        t1 = [sb("t1_%d" % i, [128, 256], F32) for i in range(2)]
        t2 = [sb("t2_%d" % i, [128, 256], F32) for i in range(2)]
        t3 = [sb("t3_%d" % i, [128, 256], F32) for i in range(2)]
        tabt = [sb("tabt%d" % i, [128, 256], F32) for i in range(5)]
        PT2 = [sb("PT%d" % i, [128, 1024], BF16)[0] for i in range(3)]
        PT = [(PT2[i // 2][:, (i % 2) * 512:(i % 2 + 1) * 512], Buf("PT%d" % i)) for i in range(6)]
        mixT, mixTB = sb("mixT", [128, 8, TOWN], BF16)
        mixtm = [sb("mixtm%d" % i, [128, 4, 256], BF16) for i in range(2)]
        vecs, vecsB = sb("vecs", [128, V_N], F32)
        gcol, gcolB = sb("gcol", [128, 8], F32)
        lamt, lamtB = sb("lamt", [128, 8], F32)
        esink, esinkB = sb("esink", [128, 4], F32)
        subg, subgB = sb("subg", [128, 64], F32)
        hsel, hselB = sb("hsel", [128, 6], F32)
        bmask, bmaskB = sb("bmask", [128, 4, 128], F32)
        fin = [sb("fin%d" % i, [128, 4, 64], F32) for i in range(2)]
        fin2 = [sb("fin2_%d" % i, [128, 4, 64], F32) for i in range(2)]
        rd = [sb("rd%d" % i, [128, 8], F32) for i in range(2)]

        rr = {}

        def rot(lst, key=None):
            k = key or id(lst)
            i = rr.get(k, 0)
            rr[k] = i + 1
            return lst[i % len(lst)]

        mhalf, mhalfB = sb("mhalf", [128, 8], F32)
        K.op("pool", lambda e: e.memset(mhalf[:], -0.5), writes=[mhalfB])
        K.op("pool", lambda e: e.memset(identf[:], 1.0), writes=[identfB])
        K.op("pool", lambda e: e.affine_select(out=identf[:], in_=identf[:], pattern=[[-1, 128]],
                                               compare_op=ALU.is_equal, fill=0.0, base=0,
                                               channel_multiplier=1), reads=[identfB], writes=[identfB])
        K.op("dve", lambda e: e.tensor_copy(out=ident[:], in_=identf[:]), reads=[identfB], writes=[identB])
        K.dma(bmask[:].rearrange("p a b -> p (a b)"), io["bmask"][:, :], bmaskB, writes=[bmaskB])
        K.dma(hsel[:], io["halosel"][:, :], hselB, writes=[hselB])
        if fused:
            rsel, rselB = sb("rsel", [128, 4], F32)
            K.dma(rsel[:], io["rsel"][:, :], rselB, writes=[rselB])

        def stats_rstd(x_t, x_B, width, s_t, s_B):
            K.op("dve", lambda e: e.scalar_tensor_tensor(out=junk[:, 0:width], in0=x_t, scalar=1.0, in1=x_t,
                                                         op0=ALU.mult, op1=ALU.mult, accum_out=s_t[:, 0:1]),
                 reads=[x_B], writes=[junkB, s_B])
            K.op("pool", lambda e: e.tensor_scalar(out=s_t[:, 1:2], in0=s_t[:, 0:1], scalar1=1.0 / width, scalar2=EPS,
                                                   op0=ALU.mult, op1=ALU.add), reads=[s_B], writes=[s_B])
            K.op("pool", lambda e: e.tensor_tensor(out=s_t[:, 2:3], in0=s_t[:, 1:2], in1=mhalf[:, 0:1], op=ALU.pow),
                 reads=[s_B, mhalfB], writes=[s_B])

        def prep_a(x_t, x_B):
            s_t, s_B = rot(st1)
            stats_rstd(x_t[:], x_B, 1024, s_t, s_B)
            b_t, b_B = rot(xb)
            K.op("act", lambda e: e.activation(out=b_t[:], in_=x_t[:], func=AF.Copy), reads=[x_B], writes=[b_B])
            return b_t, b_B, s_t, s_B

        def prep_b(b_t, b_B, eng="act"):
            p_t, p_B = rot(pT)
            for k in range(8):
                K.op("pe", lambda e, k=k: e.transpose(out=p_t[:, k * 128:(k + 1) * 128],
                                                      in_=b_t[:, k * 128:(k + 1) * 128], identity=ident[:]),
                     reads=[b_B, identB], writes=[p_B])
            T_t, T_B = rot(xT)
            if eng == "act":
                K.op("act", lambda e: e.activation(out=T_t[:].rearrange("p c t -> p (c t)"), in_=p_t[:], func=AF.Copy),
                     reads=[p_B], writes=[T_B])
            else:
                K.op("dve", lambda e: e.tensor_copy(out=T_t[:].rearrange("p c t -> p (c t)"), in_=p_t[:]),
                     reads=[p_B], writes=[T_B])
            return T_t, T_B

        def prep_cast(x_t, x_B):
            b_t, b_B = rot(xb)
            K.op("act", lambda e: e.activation(out=b_t[:], in_=x_t[:], func=AF.Copy), reads=[x_B], writes=[b_B])
            return b_t, b_B

        def prep_stats(x_t, x_B):
            s_t, s_B = rot(st1)
            stats_rstd(x_t[:], x_B, 1024, s_t, s_B)
            return s_t, s_B

        def run_pipe(n, order, lo=0):
            md = max(d for d, _ in order)
            for step in range(n + md):
                for d, f in order:
                    idx = step - d
                    if 0 <= idx < n:
                        f(lo + idx)

        def prep_tile(x_t, x_B):
            b_t, b_B, s_t, s_B = prep_a(x_t, x_B)
            T_t, T_B = prep_b(b_t, b_B)
            return T_t, T_B, s_t, s_B

        def proj(T_t, T_B, w_t, w_B, groups, z_t, z_B, s_t, s_B, evac="dve"):
            for (c0, n) in groups:
                p_t, p_B = rot(pzs, "pzproj")
                for k in range(8):
                    K.op("pe", lambda e, k=k: e.matmul(p_t[:, 0:n], lhsT=T_t[:, k, :], rhs=w_t[:, k, c0:c0 + n],
                                                       start=(k == 0), stop=(k == 7)),
                         reads=[T_B, w_B], writes=[p_B])
                if evac == "act":
                    K.op("act", lambda e: e.activation(out=z_t[:, c0:c0 + n], in_=p_t[:, 0:n], func=AF.Copy, scale=s_t[:, 2:3]),
                         reads=[p_B, s_B], writes=[z_B])
                else:
                    K.op("dve", lambda e: e.tensor_scalar(out=z_t[:, c0:c0 + n], in0=p_t[:, 0:n], scalar1=s_t[:, 2:3],
                                                          scalar2=None, op0=ALU.mult),
                         reads=[p_B, s_B], writes=[z_B])

        def headnorm(z_t, z_B, c0, H, gain_ap):
            headnorm_b(z_t, z_B, c0, H, gain_ap, headnorm_a(z_t, z_B, c0, H))

        def headnorm_a(z_t, z_B, c0, H):
            a_t, a_B = rot(t1)
            h_t, h_B = rot(hn)
            src = z_t[:, c0:c0 + 64 * H]
            src3 = src.rearrange("p (h d) -> p h d", h=H)
            for hh_ in range(H):
                K.op("dve", lambda e: e.scalar_tensor_tensor(
                    out=a_t[:, hh_ * 64:(hh_ + 1) * 64], in0=src[:, hh_ * 64:(hh_ + 1) * 64], scalar=1.0,
                    in1=src[:, hh_ * 64:(hh_ + 1) * 64], op0=ALU.mult, op1=ALU.mult, accum_out=h_t[:, hh_:hh_ + 1]),
                    reads=[z_B], writes=([a_B, h_B] if hh_ in (0, H - 1) else []))
            K.op("pool", lambda e: e.tensor_scalar(out=h_t[:, 0:H], in0=h_t[:, 0:H], scalar1=1.0 / 64, scalar2=EPS,
                                                   op0=ALU.mult, op1=ALU.add), reads=[h_B], writes=[h_B])
            K.op("pool", lambda e: e.tensor_tensor(out=h_t[:, 0:H], in0=h_t[:, 0:H], in1=mhalf[:, 0:H], op=ALU.pow),
                 reads=[h_B, mhalfB], writes=[h_B])
            return h_t, h_B

        def headnorm_b(z_t, z_B, c0, H, gain_ap, hb):
            h_t, h_B = hb
            src = z_t[:, c0:c0 + 64 * H]
            src3 = src.rearrange("p (h d) -> p h d", h=H)
            K.op("dve", lambda e: e.tensor_tensor(out=src3, in0=src3,
                                                  in1=h_t[:, 0:H].unsqueeze(2).to_broadcast([128, H, 64]), op=ALU.mult),
                 reads=[z_B, h_B], writes=[z_B])
            K.op("dve", lambda e: e.tensor_tensor(out=src3, in0=src3,
                                                  in1=gain_ap.unsqueeze(1).to_broadcast([128, H, 64]), op=ALU.mult),
                 reads=[z_B, vecsB], writes=[z_B])

        def rope(z_t, z_B, c0, H, hs, tab_t, tab_B, tc0, dsts, dst_B, mul_eng="pool"):
            nseg = 32 // hs
            a_t, a_B = rot(t2)
            b_t, b_B = rot(t3)
            src = z_t[:, c0:c0 + 64 * H]
            src3 = src.rearrange("p (h d) -> p h d", h=H)
            cc = tab_t[:, tc0:tc0 + 64]
            ss = tab_t[:, tc0 + 64:tc0 + 128]
            K.op("dve", lambda e: e.tensor_tensor(out=a_t[:, 0:64 * H].rearrange("p (h d) -> p h d", h=H), in0=src3,
                                                  in1=cc.unsqueeze(1).to_broadcast([128, H, 64]), op=ALU.mult),
                 reads=[z_B, tab_B], writes=[a_B])
            s5 = src.rearrange("p (h s two k) -> p h s two k", h=H, s=nseg, two=2)
            b5 = b_t[:, 0:64 * H].rearrange("p (h s two k) -> p h s two k", h=H, s=nseg, two=2)
            ss4 = ss.rearrange("p (s two k) -> p s two k", s=nseg, two=2)
            for (o_, i_) in ((0, 1), (1, 0)):
                K.op(mul_eng, lambda e, o_=o_, i_=i_: e.tensor_tensor(
                    out=b5[:, :, :, o_, :], in0=s5[:, :, :, i_, :],
                    in1=ss4[:, :, o_, :].unsqueeze(1).to_broadcast([128, H, nseg, hs]), op=ALU.mult),
                    reads=[z_B, tab_B], writes=[b_B])
            for (dst_ap, sel) in dsts:
                K.op("dve", lambda e, dst_ap=dst_ap, sel=sel: e.tensor_tensor(
                    out=dst_ap, in0=sel(a_t[:, 0:64 * H]), in1=sel(b_t[:, 0:64 * H]), op=ALU.add),
                    reads=[a_B, b_B], writes=[dst_B])

        def silu_gate(z_t, z_B, c0, n, dst_ap, dst_B):
            a_t, a_B = rot(t1)
            K.op("act", lambda e: e.activation(out=a_t[:, 0:n], in_=z_t[:, c0:c0 + n], func=AF.Exp, scale=-1.0),
                 reads=[z_B], writes=[a_B])
            K.op("act", lambda e: e.activation(out=a_t[:, 0:n], in_=a_t[:, 0:n], func=AF.Ln, scale=1.0, bias=1.0),
                 reads=[a_B], writes=[a_B])
            K.op("act", lambda e: e.activation(out=a_t[:, 0:n], in_=a_t[:, 0:n], func=AF.Exp, scale=-1.0),
                 reads=[a_B], writes=[a_B])
            K.op("dve", lambda e: e.tensor_tensor(out=dst_ap, in0=z_t[:, c0:c0 + n], in1=a_t[:, 0:n], op=ALU.mult),
                 reads=[a_B, z_B], writes=[dst_B])

        def transpose_multi(srcs, dsts):
            p_t, p_B = rot(pT)
            for i, (s_ap, s_B) in enumerate(srcs):
                K.op("pe", lambda e: e.transpose(out=p_t[:, i * 128:(i + 1) * 128], in_=s_ap, identity=ident[:]),
                     reads=[s_B, identB], writes=[p_B])
            for (dst_ap, dst_B, b0, nb_, eng) in dsts:
                src = p_t[:, b0 * 128:(b0 + nb_) * 128]
                if len(dst_ap.shape) == 3:
                    src = src.rearrange("p (a b) -> p a b", a=dst_ap.shape[1])
                if eng == "act":
                    K.op("act", lambda e: e.activation(out=dst_ap, in_=src, func=AF.Copy), reads=[p_B], writes=[dst_B])
                else:
                    K.op("dve", lambda e: e.tensor_copy(out=dst_ap, in_=src), reads=[p_B], writes=[dst_B])

        def transpose_to(src_aps, src_B, dst_ap, dst_B, eng="dve"):
            p_t, p_B = rot(pT)
            n = len(src_aps)
            for i, s_ap in enumerate(src_aps):
                K.op("pe", lambda e, i=i, s_ap=s_ap: e.transpose(out=p_t[:, i * 128:(i + 1) * 128], in_=s_ap,
                                                                 identity=ident[:]),
                     reads=[src_B, identB], writes=[p_B])
            src = p_t[:, 0:n * 128]
            if len(dst_ap.shape) == 3:
                src = src.rearrange("p (a b) -> p a b", a=dst_ap.shape[1])
            if eng == "act":
                K.op("act", lambda e: e.activation(out=dst_ap, in_=src, func=AF.Copy), reads=[p_B], writes=[dst_B])
            else:
                K.op("dve", lambda e: e.tensor_copy(out=dst_ap, in_=src), reads=[p_B], writes=[dst_B])

        def load_weights_chunk(l, c, pieces):
            groups, cur, used = [], [], 0
            for pc in pieces:
                if used + pc[1] > 1024:
                    groups.append(cur)
                    cur, used = [], 0
                cur.append(pc)
                used += pc[1]
            groups.append(cur)
            i = 0
            for grp_ in groups:
                w_t, w_B = rot(xs)
                off = 0
                offs = []
                for (c0, n, dst_ap, dst_B) in grp_:
                    K.dma(w_t[:, off:off + n], io["w_in"][l, c * 128:(c + 1) * 128, c0:c0 + n], w_B, writes=[w_B])
                    offs.append(off)
                    off += n
                for k_, (c0, n, dst_ap, dst_B) in enumerate(grp_):
                    o_ = offs[k_]
                    if (i + c) % 2 == 0:
                        K.op("act", lambda e: e.activation(out=dst_ap, in_=w_t[:, o_:o_ + n], func=AF.Copy, scale=gcol[:, c:c + 1]),
                             reads=[w_B, gcolB], writes=[dst_B])
                    else:
                        K.op("dve", lambda e: e.tensor_scalar(
                            out=dst_ap, in0=w_t[:, o_:o_ + n], scalar1=gcol[:, c:c + 1], scalar2=None, op0=ALU.mult),
                            reads=[w_B, gcolB], writes=[dst_B])
                    i += 1

        def load_x(src_ap, extra=()):
            x_t, x_B = rot(xs)
            K.dma(x_t[:], src_ap, x_B, reads=list(extra), writes=[x_B])
            return x_t, x_B

        def attn_pipe(n, qk, ex, pv, la, mids=None):
            for idx in range(n + la):
                if idx < n:
                    qk(idx)
                    ex(idx)
                if idx >= la:
                    pv(idx - la)
                if mids and idx in mids:
                    mids[idx]()

        for l in range(L):
            if l > 0:
                K.renew()
            x_src_full = io["xfull"] if l == 0 else io["xg_dst"]
            x_src_own = io["xown"] if l == 0 else io["x1own"]
            xfB = (lambda row: []) if l == 0 else (lambda row: [xgdB[row // 1024]])
            xoB = [] if l == 0 else [x1ownB]
            K.dma(vecs[:], io["vecs"][l, :, :], vecsB, writes=[vecsB])
            K.dma(gcol[:], io["gcol"][l, :, :], gcolB, writes=[gcolB])
            K.op("act", lambda e: e.activation(out=esink[:], in_=vecs[:, V_SINK:V_SINK + 4], func=AF.Exp),
                 reads=[vecsB], writes=[esinkB])
            K.op("dve", lambda e: e.tensor_tensor(out=junk[:, 0:32], in0=vecs[:, V_LQ1:V_LQ1 + 32],
                                                  in1=vecs[:, V_LK1:V_LK1 + 32], op=ALU.mult),
                 reads=[vecsB], writes=[junkB])
            K.op("dve", lambda e: e.tensor_reduce(out=lamt[:, 0:1], in_=junk[:, 0:32], axis=AX.X, op=ALU.add),
                 reads=[junkB], writes=[lamtB])
            K.op("dve", lambda e: e.tensor_tensor(out=junk[:, 32:64], in0=vecs[:, V_LQ2:V_LQ2 + 32],
                                                  in1=vecs[:, V_LK2:V_LK2 + 32], op=ALU.mult),
                 reads=[vecsB], writes=[junkB])
            K.op("dve", lambda e: e.tensor_reduce(out=lamt[:, 1:2], in_=junk[:, 32:64], axis=AX.X, op=ALU.add),
                 reads=[junkB], writes=[lamtB])
            K.op("act", lambda e: e.activation(out=lamt[:, 2:4], in_=lamt[:, 0:2], func=AF.Exp),
                 reads=[lamtB], writes=[lamtB])
            K.op("dve", lambda e: e.tensor_tensor(out=lamt[:, 4:5], in0=lamt[:, 2:3], in1=lamt[:, 3:4], op=ALU.subtract),
                 reads=[lamtB], writes=[lamtB])
            K.op("dve", lambda e: e.tensor_tensor(out=lamt[:, 4:5], in0=lamt[:, 4:5], in1=vecs[:, V_LC:V_LC + 1], op=ALU.add),
                 reads=[lamtB, vecsB], writes=[lamtB])
            K.op("dve", lambda e: e.tensor_scalar(out=lamt[:, 5:6], in0=lamt[:, 4:5], scalar1=-1.0, scalar2=None, op0=ALU.mult),
                 reads=[lamtB], writes=[lamtB])
            K.op("dve", lambda e: e.tensor_scalar(out=subg[:], in0=vecs[:, V_SUB:V_SUB + 64], scalar1=vecs[:, V_LC + 1:V_LC + 2],
                                                  scalar2=None, op0=ALU.mult), reads=[vecsB], writes=[subgB])

            with ExitStack() as ph:
              if '1' in PH:
                wBC, wBCB = sb("wBC", [128, 8, 1792], BF16, ph)
                kTb, kTbB = sb("kTb", [128, NT_LOC * 128], BF16, ph)
                vb, vbB = sb("vb", [128, NT_LOC, 2, 66], BF16, ph)
                kTc, kTcB = sb("kTc", [128, 2, NT_LOC * 128], BF16, ph)
                vc, vcB = sb("vc", [128, NT_LOC, 4, 66], BF16, ph)
                qTb, qTbB = sb("qTb", [128, NT_OWN, 2, 128], BF16, ph)
                qTc, qTcB = sb("qTc", [128, NT_OWN, 2, 128], BF16, ph)
                gbc, gbcB = sb("gbc", [128, NT_OWN, 512], BF16, ph)
                ctp = [sb("ctp%d" % i, [128, 512], F32, ph) for i in range(3)]
                zs = [sb("zs%d" % i, [128, 1792], F32, ph) for i in range(2)]
                finp = fin + fin2
                qtm = [sb("qtm%d" % i, [128, 512], BF16, ph) for i in range(3)]
                ktm = [sb("ktm%d" % i, [128, 384], BF16, ph) for i in range(3)]
                tmpc = [sb("tmpc%d" % i, [128, 512], F32, ph) for i in range(2)]

                K.op("pool", lambda e: e.memset(vb[:].rearrange("p a b c -> p (a b c)"), 1.0), writes=[vbB])
                K.op("pool", lambda e: e.memset(vc[:].rearrange("p a b c -> p (a b c)"), 1.0), writes=[vcB])
                for c in range(8):
                    load_weights_chunk(l, c, [(768, 1024, wBC[:, c, 0:1024], wBCB), (1792, 768, wBC[:, c, 1024:1792], wBCB)])

                order1 = list(range(2, 2 + NT_OWN)) + [0, 1, 18, 19]
                SP1 = {}

                def p1_s0(idx):
                    bt = order1[idx]
                    if 2 <= bt < 2 + NT_OWN:
                        x_t, x_B = load_x(x_src_own[(bt - 2) * 128:(bt - 1) * 128, :], xoB)
                    else:
                        prev = bt < 2
                        k0 = rr.get(id(xs), 0)
                        rr[id(xs)] = k0 + 3
                        x_t, x_B = xs[k0 % 3]
                        sc0 = 0 if prev else 3
                        for m in range(3):
                            h_t, h_B = xs[(k0 + 1 + (m % 2)) % 3]
                            row0 = (m + 1) * TOWN + ((bt - 2) * 128 if prev else (bt - 18) * 128)
                            h_v = h_t[:] if l == 0 else h_t[:].bitcast(BF16)[:, 0:1024]
                            K.dma(h_v, x_src_full[row0:row0 + 128, :], h_B, reads=xfB(row0), writes=[h_B])
                            if m == 0:
                                K.op("dve", lambda e: e.tensor_scalar(out=x_t[:], in0=h_v, scalar1=hsel[:, sc0:sc0 + 1],
                                                                      scalar2=None, op0=ALU.mult),
                                     reads=[h_B, hselB], writes=[x_B])
                            else:
                                K.op("dve", lambda e: e.scalar_tensor_tensor(
                                    out=x_t[:], in0=h_v, scalar=hsel[:, sc0 + m:sc0 + m + 1], in1=x_t[:],
                                    op0=ALU.mult, op1=ALU.add), reads=[h_B, hselB, x_B], writes=[x_B])
                    tb_t, tb_B = rot(tabt)
                    K.dma(tb_t[:], io["tabB"][bt * 128:(bt + 1) * 128, :], tb_B, writes=[tb_B], qn="act")
                    SP1[bt] = [x_t, x_B, tb_t, tb_B]

                def p1_cast(idx):
                    bt = order1[idx]
                    SP1[bt] += list(prep_cast(*SP1[bt][0:2]))

                def p1_stats(idx):
                    bt = order1[idx]
                    SP1[bt] += list(prep_stats(*SP1[bt][0:2]))

                def p1_b2(idx):
                    bt = order1[idx]
                    x_t, x_B, tb_t, tb_B, b_t, b_B, s_t, s_B = SP1[bt]
                    T_t, T_B = prep_b(b_t, b_B)
                    SP1[bt] = [tb_t, tb_B, T_t, T_B, s_t, s_B]

                def p1_c(idx):
                    bt = order1[idx]
                    tb_t, tb_B, T_t, T_B, s_t, s_B = SP1[bt]
                    z_t, z_B = rot(zs)
                    proj(T_t, T_B, wBC, wBCB, [(0, 512), (512, 512), (1024, 512), (1536, 256)], z_t, z_B, s_t, s_B)
                    SP1[bt] = (z_t, z_B, tb_t, tb_B)

                def p1_s1(idx):
                    bt = order1[idx]
                    z_t, z_B, tb_t, tb_B = SP1[bt]
                    own = 2 <= bt < 2 + NT_OWN
                    i_own = bt - 2
                    q_t, q_B = rot(qtm)
                    k_t, k_B = rot(ktm)
                    rope(z_t, z_B, 256, 2, 32, tb_t, tb_B, 0,
                         [(k_t[:, 0:128], lambda a: a)], k_B)
                    K.op("pool", lambda e: e.tensor_copy(out=k_t[:, 128:384], in_=z_t[:, 1024:1280]),
                         reads=[z_B], writes=[k_B])
                    K.op("act", lambda e: e.activation(out=vb[:, bt, :, 0:64],
                                                       in_=z_t[:, 384:512].rearrange("p (h d) -> p h d", h=2), func=AF.Copy),
                         reads=[z_B], writes=[vbB])
                    K.op("act", lambda e: e.activation(out=vc[:, bt, :, 0:64],
                                                       in_=z_t[:, 1280:1536].rearrange("p (h d) -> p h d", h=4), func=AF.Copy),
                         reads=[z_B], writes=[vcB])
                    if own:
                        dst = q_t[:, 0:256].rearrange("p (g kv d) -> p kv g d", g=2, kv=2)
                        rope(z_t, z_B, 0, 4, 32, tb_t, tb_B, 128,
                             [(dst, lambda a: a.rearrange("p (kv g d) -> p kv g d", kv=2, g=2))], q_B)
                        K.op("pool", lambda e: e.tensor_copy(out=q_t[:, 256:512], in_=z_t[:, 768:1024]),
                             reads=[z_B], writes=[q_B])
                        silu_gate(z_t, z_B, 512, 256, gbc[:, i_own, 0:256], gbcB)
                        silu_gate(z_t, z_B, 1536, 256, gbc[:, i_own, 256:512], gbcB)
                    SP1[bt] = (q_t, q_B, k_t, k_B)

                def p1_s2(idx):
                    bt = order1[idx]
                    q_t, q_B, k_t, k_B = SP1.pop(bt)
                    own = 2 <= bt < 2 + NT_OWN
                    i_own = bt - 2
                    srcs = [(k_t[:, 0:128], k_B), (k_t[:, 128:256], k_B), (k_t[:, 256:384], k_B)]
                    dsts = [(kTb[:, bt * 128:(bt + 1) * 128], kTbB, 0, 1, "act"),
                            (kTc[:, 0, bt * 128:(bt + 1) * 128], kTcB, 1, 1, "act"),
                            (kTc[:, 1, bt * 128:(bt + 1) * 128], kTcB, 2, 1, "act")]
                    if own:
                        srcs += [(q_t[:, k_ * 128:(k_ + 1) * 128], q_B) for k_ in range(4)]
                        dsts += [(qTb[:, i_own, :, :].rearrange("p g q -> p (g q)"), qTbB, 3, 2, "act"),
                                 (qTc[:, i_own, :, :].rearrange("p g q -> p (g q)"), qTcB, 5, 2, "act")]
                    transpose_multi(srcs, dsts)

                def run_p1_step1(lo, hi):
                    run_pipe(hi - lo, [(1, p1_cast), (1, p1_stats), (2, p1_b2), (3, p1_c), (4, p1_s1), (5, p1_s2), (0, p1_s0)], lo)

                def p1_attn(i):
                    acc_t, acc_B = pz[4]
                    accv = acc_t[:, 0:260].rearrange("p (a b) -> p a b", b=65)
                    wins = [i + 1, i + 2, i + 3]
                    mids = [0 if i == 0 else 1, None, 3 if i == NT_OWN - 1 else 2]
                    slots = {}

                    def qkB(idx, i=i, wins=wins, slots=slots):
                        j = wins[idx]
                        banks = [rot(pzs, "pzst"), rot(pzs, "pzst")]
                        slots[idx] = banks
                        for kv in range(2):
                            s_t, s_B = banks[kv]
                            K.op("pe", lambda e: e.matmul(
                                s_t[:, 0:256], lhsT=kTb[kv * 64:(kv + 1) * 64, j * 128:(j + 1) * 128],
                                rhs=qTb[kv * 64:(kv + 1) * 64, i, :, :].rearrange("p g q -> p (g q)"),
                                start=True, stop=True), reads=[kTbB, qTbB], writes=[s_B])

                    def exB(idx, mids=mids, slots=slots):
                        banks = slots[idx]
                        p_t, p_B = rot(PT)
                        slots[idx] = (p_t, p_B)
                        for kv in range(2):
                            s_t, s_B = banks[kv]
                            K.op("act", lambda e: e.activation(out=p_t[:, kv * 256:(kv + 1) * 256], in_=s_t[:, 0:256], func=AF.Exp),
                                 reads=[s_B], writes=[p_B])
                        if mids[idx] is not None:
                            mi = mids[idx]
                            K.op("dve", lambda e: e.tensor_tensor(
                                out=p_t[:].rearrange("p (h q) -> p h q", h=4), in0=p_t[:].rearrange("p (h q) -> p h q", h=4),
                                in1=bmask[:, mi, :].unsqueeze(1).to_broadcast([128, 4, 128]), op=ALU.mult),
                                reads=[p_B, bmaskB], writes=[p_B])

                    def pvB(idx, wins=wins, slots=slots, accv=accv, acc_B=acc_B):
                        j = wins[idx]
                        p_t, p_B = slots[idx]
                        for kv in range(2):
                            for g in range(2):
                                h = 2 * kv + g
                                K.op("pe", lambda e, kv=kv, g=g, h=h: e.matmul(
                                    accv[:, h, :], lhsT=p_t[:, (kv * 2 + g) * 128:(kv * 2 + g + 1) * 128],
                                    rhs=vb[:, j, kv, 0:65], start=(idx == 0 and h == 0), stop=(idx == 2 and h == 3)),
                                    reads=[p_B, vbB], writes=[acc_B])

                    if 'B' in PH:
                        attn_pipe(3, qkB, exB, pvB, 1)
                    r_t, r_B = rot(rd)
                    K.op("dve", lambda e: e.tensor_tensor(out=r_t[:, 0:4], in0=accv[:, :, 64], in1=esink[:], op=ALU.add),
                         reads=[acc_B, esinkB], writes=[r_B])
                    K.op("dve", lambda e: e.reciprocal(out=r_t[:, 0:4], in_=r_t[:, 0:4]), reads=[r_B], writes=[r_B])
                    fb_t, fb_B = rot(finp)
                    K.op("dve", lambda e: e.tensor_tensor(out=fb_t[:], in0=accv[:, :, 0:64],
                                                          in1=r_t[:, 0:4].unsqueeze(2).to_broadcast([128, 4, 64]), op=ALU.mult),
                         reads=[acc_B, r_B], writes=[fb_B])
                    if i == 0:
                        cwin = list(range(0, 6))
                    elif i == NT_OWN - 1:
                        cwin = list(range(14, 20))
                    else:
                        cwin = list(range(i, i + 5))
                    spi = {0: 0, 1: 1, NT_OWN - 2: 2, NT_OWN - 1: 3}.get(i, None)
                    acc2_t, acc2_B = pz[5]
                    accv2 = acc2_t[:, 0:260].rearrange("p (a b) -> p a b", b=65)
                    slots2 = {}
                    nw = len(cwin)

                    def qkC(idx, i=i, cwin=cwin, slots2=slots2):
                        j = cwin[idx]
                        banks = [rot(pzs, "pzst"), rot(pzs, "pzst")]
                        slots2[idx] = banks
                        for h in range(4):
                            p_, hh = h // 2, h % 2
                            s_t, s_B = banks[hh]
                            K.op("pe", lambda e: e.matmul(
                                s_t[:, p_ * 128:(p_ + 1) * 128], lhsT=kTc[hh * 64:(hh + 1) * 64, p_, j * 128:(j + 1) * 128],
                                rhs=qTc[hh * 64:(hh + 1) * 64, i, p_, :], start=True, stop=True),
                                reads=[kTcB, qTcB], writes=[s_B])

                    def exC(idx, spi=spi, slots2=slots2):
                        banks = slots2[idx]
                        tb_t, tb_B = rot(ctp)
                        if spi is None:
                            K.dma(tb_t[:], io["cb_int"][l, :, idx * 512:(idx + 1) * 512], tb_B, writes=[tb_B])
                        else:
                            K.dma(tb_t[:], io["cb_sp"][l, spi, :, idx * 512:(idx + 1) * 512], tb_B, writes=[tb_B])
                        c_t, c_B = rot(tmpc)
                        for hh in range(2):
                            s_t, s_B = banks[hh]
                            tv = tb_t[:].rearrange("p (pp hh q) -> p hh pp q", pp=2, hh=2)[:, hh]
                            cv = c_t[:].rearrange("p (pp hh q) -> p hh pp q", pp=2, hh=2)[:, hh]
                            K.op("dve", lambda e: e.scalar_tensor_tensor(
                                out=cv, in0=s_t[:, 0:256].rearrange("p (pp q) -> p pp q", pp=2), scalar=0.125, in1=tv,
                                op0=ALU.mult, op1=ALU.add), reads=[s_B, tb_B], writes=[c_B])
                        p_t, p_B = rot(PT)
                        slots2[idx] = (p_t, p_B)
                        K.op("act", lambda e: e.activation(out=p_t[:], in_=c_t[:], func=AF.Exp), reads=[c_B], writes=[p_B])

                    def pvC(idx, cwin=cwin, slots2=slots2, accv2=accv2, acc2_B=acc2_B, nw=nw):
                        j = cwin[idx]
                        p_t, p_B = slots2[idx]
                        for h in range(4):
                            K.op("pe", lambda e, h=h: e.matmul(
                                accv2[:, h, :], lhsT=p_t[:, h * 128:(h + 1) * 128], rhs=vc[:, j, h, 0:65],
                                start=(idx == 0 and h == 0), stop=(idx == nw - 1 and h == 3)), reads=[p_B, vcB], writes=[acc2_B])

                    if 'C' in PH:
                        attn_pipe(nw, qkC, exC, pvC, 1)
                    r_t, r_B = rot(rd)
                    K.op("dve", lambda e: e.reciprocal(out=r_t[:, 0:4], in_=accv2[:, :, 64]), reads=[acc2_B], writes=[r_B])
                    fc_t, fc_B = rot(finp)
                    K.op("dve", lambda e: e.tensor_tensor(out=fc_t[:], in0=accv2[:, :, 0:64],
                                                          in1=r_t[:, 0:4].unsqueeze(2).to_broadcast([128, 4, 64]), op=ALU.mult),
                         reads=[acc2_B, r_B], writes=[fc_B])
                    SY[i] = (fb_t, fb_B, fc_t, fc_B)

                def p1_fin(i):
                    fb_t, fb_B, fc_t, fc_B = SY.pop(i)
                    m_t, m_B = rot(mixtm)
                    K.op("pool", lambda e: e.tensor_tensor(out=m_t[:, 0, :], in0=fb_t[:].rearrange("p h d -> p (h d)"),
                                                           in1=gbc[:, i, 0:256], op=ALU.mult),
                         reads=[fb_B, gbcB], writes=[m_B])
                    K.op("pool", lambda e: e.tensor_tensor(out=m_t[:, 1, :], in0=fc_t[:].rearrange("p h d -> p (h d)"),
                                                           in1=gbc[:, i, 256:512], op=ALU.mult),
                         reads=[fc_B, gbcB], writes=[m_B])
                    transpose_to([m_t[:, 0, 0:128], m_t[:, 0, 128:256], m_t[:, 1, 0:128], m_t[:, 1, 128:256]], m_B,
                                 mixT[:, 2:6, i * 128:(i + 1) * 128], mixTB)

                def run_p1_attn(tiles):
                    for k_, i in enumerate(tiles):
                        p1_attn(i)
                        if k_ >= 1:
                            p1_fin(tiles[k_ - 1])
                    p1_fin(tiles[-1])

                SY = {}
                run_p1_step1(0, NT_OWN)
                run_p1_attn(list(range(2, NT_OWN - 2)))
                run_p1_step1(NT_OWN, NT_LOC)
                run_p1_attn([0, 1, NT_OWN - 2, NT_OWN - 1])
                K.barrier()

            with ExitStack() as ph:
              if '2' in PH:
                wKV, wKVB = sb("wKV", [128, 8, 512], BF16, ph)
                wQG, wQGB = sb("wQG", [128, 8, 1024], BF16, ph)
                kTa, kTaB = sb("kTa", [128, SEQ], BF16, ph)
                va, vaB = sb("va", [128, NT_ALL, 2, 66], BF16, ph)
                kTd, kTdB = sb("kTd", [128, SEQ], BF16, ph)
                vd, vdB = sb("vd", [128, NT_ALL, 2, 66], BF16, ph)
                qTa = [sb("qTa%d" % i, [128, 2, 512], BF16, ph) for i in range(2)]
                qTd = [sb("qTd%d" % i, [128, 2, 2, 512], BF16, ph) for i in range(2)]
                gads = [sb("gad%d" % i, [128, 4, 512], GDT, ph) for i in range(2)]
                qa_tm = [sb("qatm%d" % i, [128, 256], BF16, ph) for i in range(2)]
                qd_tm = [sb("qdtm%d" % i, [128, 2, 256], BF16, ph) for i in range(2)]
                k2_tm = [sb("k2tm%d" % i, [128, 256], BF16, ph) for i in range(2)]
                zs = [sb("zs%d" % i, [128, 1024], F32, ph) for i in range(2)]

                K.op("pool", lambda e: e.memset(va[:].rearrange("p a b c -> p (a b c)"), 1.0), writes=[vaB])
                K.op("pool", lambda e: e.memset(vd[:].rearrange("p a b c -> p (a b c)"), 1.0), writes=[vdB])
                for i in range(2):
                    K.op("pool", lambda e, i=i: e.memset(qd_tm[i][0][:].rearrange("p a b -> p (a b)"), 0.0),
                         writes=[qd_tm[i][1]])
                for c in range(8):
                    load_weights_chunk(l, c, [
                        (256, 256, wKV[:, c, 0:256], wKVB), (2816, 256, wKV[:, c, 256:512], wKVB),
                        (0, 256, wQG[:, c, 0:256], wQGB), (512, 256, wQG[:, c, 256:512], wQGB),
                        (2560, 256, wQG[:, c, 512:768], wQGB), (3072, 256, wQG[:, c, 768:1024], wQGB)])


                def staged(n, stages):
                    ns = len(stages)
                    for step in range(n + ns - 1):
                        for si in reversed(range(ns)):
                            idx = step - si
                            if 0 <= idx < n:
                                stages[si](idx)

                S1 = {}

                kvx = xb + [(xs[i_][0][:].bitcast(BF16)[:, 0:1024], xs[i_][1]) for i_ in range(3)]

                def kv_a(t):
                    if l == 0:
                        x_t, x_B = load_x(x_src_full[t * 128:(t + 1) * 128, :], xfB(t * 128))
                    else:
                        x_t, x_B = rot(kvx)
                        K.dma(x_t[:], x_src_full[t * 128:(t + 1) * 128, :], x_B, reads=xfB(t * 128), writes=[x_B])
                    tb_t, tb_B = rot(tabt)
                    K.dma(tb_t[:], io["tabKV"][t * 128:(t + 1) * 128, :], tb_B, writes=[tb_B], qn="act")
                    S1[t] = [x_t, x_B, tb_t, tb_B]

                def kv_cast(t):
                    x_t, x_B = S1[t][0:2]
                    S1[t] += list(prep_cast(x_t, x_B)) if l == 0 else [x_t, x_B]

                def kv_stats(t):
                    x_t, x_B = S1[t][0:2]
                    S1[t] += list(prep_stats(x_t, x_B))

                def kv_b2(t):
                    x_t, x_B, tb_t, tb_B, b_t, b_B, s_t, s_B = S1[t]
                    T_t, T_B = prep_b(b_t, b_B)
                    S1[t] = [x_t, x_B, tb_t, tb_B, T_t, T_B, s_t, s_B]

                def kv_c(t):
                    x_t, x_B, tb_t, tb_B, T_t, T_B, s_t, s_B = S1[t]
                    z_t, z_B = rot(zs)
                    proj(T_t, T_B, wKV, wKVB, [(0, 512)], z_t, z_B, s_t, s_B, evac="act")
                    S1[t] = (z_t, z_B, tb_t, tb_B)

                def kv_d(t):
                    z_t, z_B, tb_t, tb_B = S1[t]
                    k_t, k_B = rot(k2_tm)
                    hb = headnorm_a(z_t, z_B, 0, 2)
                    rope(z_t, z_B, 256, 2, 16, tb_t, tb_B, 128, [(k_t[:, 128:256], lambda a: a)], k_B, mul_eng="dve")
                    headnorm_b(z_t, z_B, 0, 2, vecs[:, V_KN:V_KN + 64], hb)
                    rope(z_t, z_B, 0, 2, 16, tb_t, tb_B, 0, [(k_t[:, 0:128], lambda a: a)], k_B, mul_eng="dve")
                    K.op("pool", lambda e: e.tensor_copy(out=va[:, t, :, 0:64],
                                                         in_=z_t[:, 128:256].rearrange("p (h d) -> p h d", h=2)),
                         reads=[z_B], writes=[vaB])
                    K.op("pool", lambda e: e.tensor_copy(out=vd[:, t, :, 0:64],
                                                         in_=z_t[:, 384:512].rearrange("p (h d) -> p h d", h=2)),
                         reads=[z_B], writes=[vdB])
                    S1[t] = (k_t, k_B)

                def kv_e(t):
                    k_t, k_B = S1.pop(t)
                    transpose_multi([(k_t[:, 0:128], k_B), (k_t[:, 128:256], k_B)],
                                    [(kTa[:, t * 128:(t + 1) * 128], kTaB, 0, 1, "act"),
                                     (kTd[:, t * 128:(t + 1) * 128], kTdB, 1, 1, "act")])

                run_pipe(NT_ALL, [(0, kv_a), (1, kv_cast), (1, kv_stats), (2, kv_b2), (3, kv_c), (4, kv_d), (5, kv_e)])

                def make_qproj(grp_):
                    qa_t_, qa_B_ = qTa[grp_ % 2]
                    qd_t_, qd_B_ = qTd[grp_ % 2]
                    gad_, gadB_ = gads[grp_ % 2]
                    S2 = {}

                    def st0(qt):
                        ti = grp_ * 4 + qt
                        x_t, x_B = load_x(x_src_own[ti * 128:(ti + 1) * 128, :], xoB)
                        tb_t, tb_B = rot(tabt)
                        K.dma(tb_t[:], io["tabQ"][ti * 128:(ti + 1) * 128, :], tb_B, writes=[tb_B], qn="act")
                        b_t, b_B, s_t, s_B = prep_a(x_t, x_B)
                        S2[qt] = (tb_t, tb_B, b_t, b_B, s_t, s_B)

                    def st1a(qt):
                        tb_t, tb_B, b_t, b_B, s_t, s_B = S2[qt]
                        T_t, T_B = prep_b(b_t, b_B, eng="dve")
                        S2[qt] = (tb_t, tb_B, T_t, T_B, s_t, s_B)

                    def st1b(qt):
                        tb_t, tb_B, T_t, T_B, s_t, s_B = S2[qt]
                        z_t, z_B = rot(zs)
                        proj(T_t, T_B, wQG, wQGB, [(0, 512), (512, 512)], z_t, z_B, s_t, s_B)
                        S2[qt] = (z_t, z_B, tb_t, tb_B)

                    def st2(qt):
                        z_t, z_B, tb_t, tb_B = S2[qt]
                        silu_gate(z_t, z_B, 256, 256, gad_[:, qt, 0:256], gadB_)
                        silu_gate(z_t, z_B, 768, 256, gad_[:, qt, 256:512], gadB_)
                        headnorm(z_t, z_B, 0, 4, vecs[:, V_QN:V_QN + 64])
                        a_t, a_B = rot(qa_tm)
                        rope(z_t, z_B, 0, 4, 16, tb_t, tb_B, 0,
                             [(a_t[:, 0:256].rearrange("p (g kv d) -> p kv g d", g=2, kv=2),
                               lambda a: a.rearrange("p (kv g d) -> p kv g d", kv=2, g=2))], a_B)
                        d_t, d_B = rot(qd_tm)
                        dsts = []
                        for c in range(2):
                            dsts.append((
                                d_t[:, c, :].rearrange("p (g kv c k) -> p kv g c k", g=2, kv=2, c=2)[:, :, :, c, :],
                                lambda a, c=c: a.rearrange("p (kv g c k) -> p kv g c k", kv=2, g=2, c=2)[:, :, :, c, :]))
                        rope(z_t, z_B, 512, 4, 16, tb_t, tb_B, 128, dsts, d_B)
                        S2[qt] = (a_t, a_B, d_t, d_B)

                    def st3(qt):
                        a_t, a_B, d_t, d_B = S2.pop(qt)
                        transpose_multi(
                            [(a_t[:, 0:128], a_B), (a_t[:, 128:256], a_B), (d_t[:, 0, 0:128], d_B), (d_t[:, 0, 128:256], d_B),
                             (d_t[:, 1, 0:128], d_B), (d_t[:, 1, 128:256], d_B)],
                            [(qa_t_[:, :, qt * 128:(qt + 1) * 128], qa_B_, 0, 2, "dve"),
                             (qd_t_[:, 0, :, qt * 128:(qt + 1) * 128], qd_B_, 2, 2, "dve"),
                             (qd_t_[:, 1, :, qt * 128:(qt + 1) * 128], qd_B_, 4, 2, "dve")])

                    def boundary(b_):
                        items = []
                        for si, fn in ((1, st1a), (3, st3), (1, st1b), (2, st2), (0, st0)):
                            qt = b_ - si
                            if 0 <= qt < 4:
                                items.append(lambda fn=fn, qt=qt: fn(qt))
                        return items
                    return boundary

                nxt = make_qproj(0)
                for b_ in range(7):
                    for it_ in nxt(b_):
                        it_()
                deferred = []
                for grp in range(4):
                    qa_t, qa_B = qTa[grp % 2]
                    qd_t, qd_B = qTd[grp % 2]
                    gad, gadB = gads[grp % 2]
                    nxt = make_qproj(grp + 1) if grp < 3 else (lambda b_: [])
                    bcount = [0]
                    pending = []

                    def loop_sched():
                        pending.extend(deferred)
                        del deferred[:]
                        pending.extend(nxt(bcount[0]))
                        bcount[0] += 1

                    def tick():
                        if pending:
                            pending.pop(0)()

                    def drain():
                        while pending:
                            pending.pop(0)()
                    mids = {i_: tick for i_ in range(4, 62, 2)}

                    ma_t, ma_B = rot(mixtm)
                    for g in range(2):
                        accs = [pz[4], pz[5]]
                        accvs = [a_[0][:, 0:260].rearrange("p (a b) -> p a b", b=65) for a_ in accs]
                        slots = {}

                        def qkA(j):
                            pi = rot([0, 1], "pzpair")
                            banks = [pzs[2 * pi], pzs[2 * pi + 1]]
                            slots[j] = (pi, banks)
                            for kv in range(2):
                                s_t, s_B = banks[kv]
                                K.op("pe", lambda e: e.matmul(s_t[:], lhsT=kTa[kv * 64:(kv + 1) * 64, j * 128:(j + 1) * 128],
                                                              rhs=qa_t[kv * 64:(kv + 1) * 64, g, :], start=True, stop=True),
                                     reads=[kTaB, qa_B], writes=[s_B])

                        def exA(j):
                            pi, banks = slots[j]
                            qi = rot([0, 1, 2], "ptpair")
                            pts = [PT[2 * qi], PT[2 * qi + 1]]
                            K.op("act", lambda e: e.activation(out=PT2[qi][:], in_=pzd[pi][:], func=AF.Exp),
                                 reads=[banks[0][1], banks[1][1]], writes=[pts[0][1], pts[1][1]])
                            slots[j] = pts

                        def pvA(j):
                            pts = slots.pop(j)
                            for kv in range(2):
                                p_t, p_B = pts[kv]
                                for qt in range(4):
                                    K.op("pe", lambda e: e.matmul(accvs[kv][:, qt, :], lhsT=p_t[:, qt * 128:(qt + 1) * 128],
                                                                  rhs=va[:, j, kv, 0:65], start=(j == 0 and qt == 0),
                                                                  stop=(j == NT_ALL - 1 and qt == 3)),
                                         reads=[p_B, vaB], writes=[accs[kv][1]])

                        loop_sched()
                        attn_pipe(NT_ALL, qkA, exA, pvA, 2, mids)
                        drain()
                        for kv in range(2):
                            h = 2 * kv + g
                            accv, acc_B = accvs[kv], accs[kv][1]
                            r_t, r_B = rot(rd)
                            K.op("dve", lambda e: e.reciprocal(out=r_t[:, 0:4], in_=accv[:, :, 64]), reads=[acc_B], writes=[r_B])
                            f_t, f_B = rot(fin)
                            K.op("dve", lambda e: e.tensor_tensor(out=f_t[:], in0=accv[:, :, 0:64],
                                                                  in1=r_t[:, 0:4].unsqueeze(2).to_broadcast([128, 4, 64]),
                                                                  op=ALU.mult), reads=[acc_B, r_B], writes=[f_B])
                            K.op("pool", lambda e: e.tensor_tensor(out=ma_t[:, :, h * 64:(h + 1) * 64], in0=f_t[:],
                                                                   in1=gad[:, :, h * 64:(h + 1) * 64], op=ALU.mult),
                                 reads=[f_B, gadB], writes=[ma_B])
                    for qt in range(4):
                        deferred.append(lambda ma_t=ma_t, ma_B=ma_B, grp=grp, qt=qt: transpose_to(
                            [ma_t[:, qt, 0:128], ma_t[:, qt, 128:256]], ma_B,
                            mixT[:, 0:2, (grp * 4 + qt) * 128:(grp * 4 + qt + 1) * 128], mixTB))

                    md_t, md_B = rot(mixtm)
                    for half in range(2):
                        for g in range(2):
                            accs = [pz[4], pz[5]]
                            accvs = [a_[0][:, 0:260].rearrange("p (c a b) -> p c a b", c=2, b=65) for a_ in accs]
                            slots = {}

                            def qkD(j):
                                pi = rot([0, 1], "pzpair")
                                banks = [pzs[2 * pi], pzs[2 * pi + 1]]
                                slots[j] = (pi, banks)
                                for c in range(2):
                                    for kv in range(2):
                                        s_t, s_B = banks[kv]
                                        K.op("pe", lambda e: e.matmul(
                                            s_t[:, c * 256:(c + 1) * 256], lhsT=kTd[kv * 64:(kv + 1) * 64, j * 128:(j + 1) * 128],
                                            rhs=qd_t[kv * 64:(kv + 1) * 64, c, g, half * 256:(half + 1) * 256],
                                            start=True, stop=True), reads=[kTdB, qd_B], writes=[s_B])

                            def exD(j):
                                pi, banks = slots[j]
                                qi = rot([0, 1, 2], "ptpair")
                                pts = [PT[2 * qi], PT[2 * qi + 1]]
                                K.op("act", lambda e: e.activation(out=PT2[qi][:], in_=pzd[pi][:], func=AF.Exp),
                                     reads=[banks[0][1], banks[1][1]], writes=[pts[0][1], pts[1][1]])
                                slots[j] = pts

                            def pvD(j):
                                pts = slots.pop(j)
                                for kv in range(2):
                                    p_t, p_B = pts[kv]
                                    for c in range(2):
                                        for qt in range(2):
                                            K.op("pe", lambda e: e.matmul(
                                                accvs[kv][:, c, qt, :], lhsT=p_t[:, c * 256 + qt * 128:c * 256 + (qt + 1) * 128],
                                                rhs=vd[:, j, kv, 0:65], start=(j == 0 and c == 0 and qt == 0),
                                                stop=(j == NT_ALL - 1 and c == 1 and qt == 1)),
                                                reads=[p_B, vdB], writes=[accs[kv][1]])

                            loop_sched()
                            if half == 1 and g == 1:
                                loop_sched()
                            attn_pipe(NT_ALL, qkD, exD, pvD, 2, mids)
                            drain()
                            eps_ = []
                            for kv in range(2):
                                h = 2 * kv + g
                                av, acc_B = accvs[kv], accs[kv][1]
                                r_t, r_B = rot(rd)
                                K.op("dve", lambda e: e.reciprocal(out=r_t[:, 0:4].rearrange("p (c a) -> p c a", c=2), in_=av[:, :, :, 64]),
                                     reads=[acc_B], writes=[r_B])
                                K.op("dve", lambda e: e.tensor_scalar(out=r_t[:, 2:4], in0=r_t[:, 2:4], scalar1=lamt[:, 5:6],
                                                                      scalar2=None, op0=ALU.mult), reads=[r_B, lamtB], writes=[r_B])
                                f_t, f_B = rot(fin)
                                g_t, g_B = rot(fin2)
                                K.op("dve", lambda e: e.tensor_tensor(out=f_t[:], in0=av[:, :, :, 0:64].rearrange("p c a d -> p (c a) d"),
                                                                      in1=r_t[:, 0:4].unsqueeze(2).to_broadcast([128, 4, 64]),
                                                                      op=ALU.mult), reads=[acc_B, r_B], writes=[f_B])
                                eps_.append((h, f_t, f_B, g_t, g_B))
                            for (h, f_t, f_B, g_t, g_B) in eps_:
                                K.op("pool", lambda e: e.tensor_tensor(out=f_t[:, 0:2, :], in0=f_t[:, 0:2, :], in1=f_t[:, 2:4, :], op=ALU.add),
                                     reads=[f_B], writes=[f_B])
                                K.op("pool", lambda e: e.tensor_tensor(out=g_t[:, 0:2, :], in0=f_t[:, 0:2, :], in1=f_t[:, 0:2, :], op=ALU.mult),
                                     reads=[f_B], writes=[g_B])
                                h_t, h_B = rot(hn)
                                K.op("dve", lambda e: e.tensor_reduce(out=h_t[:, 0:2], in_=g_t[:, 0:2, :], axis=AX.X, op=ALU.add),
                                     reads=[g_B], writes=[h_B])
                                K.op("pool", lambda e: e.tensor_scalar(out=h_t[:, 0:2], in0=h_t[:, 0:2], scalar1=1.0 / 64, scalar2=EPS,
                                                                       op0=ALU.mult, op1=ALU.add), reads=[h_B], writes=[h_B])
                                K.op("pool", lambda e: e.tensor_tensor(out=h_t[:, 0:2], in0=h_t[:, 0:2], in1=mhalf[:, 0:2], op=ALU.pow),
                                     reads=[h_B, mhalfB], writes=[h_B])
                                K.op("dve", lambda e: e.tensor_tensor(out=f_t[:, 0:2, :], in0=f_t[:, 0:2, :],
                                                                      in1=h_t[:, 0:2].unsqueeze(2).to_broadcast([128, 2, 64]), op=ALU.mult),
                                     reads=[f_B, h_B], writes=[f_B])
                                K.op("dve", lambda e: e.tensor_tensor(out=f_t[:, 0:2, :], in0=f_t[:, 0:2, :],
                                                                      in1=subg[:].unsqueeze(1).to_broadcast([128, 2, 64]), op=ALU.mult),
                                     reads=[f_B, subgB], writes=[f_B])
                                K.op("pool", lambda e: e.tensor_tensor(
                                    out=md_t[:, half * 2:half * 2 + 2, h * 64:(h + 1) * 64], in0=f_t[:, 0:2, :],
                                    in1=gad[:, half * 2:half * 2 + 2, 256 + h * 64:256 + (h + 1) * 64], op=ALU.mult),
                                    reads=[f_B, gadB], writes=[md_B])
                    for qt in range(4):
                        deferred.append(lambda md_t=md_t, md_B=md_B, grp=grp, qt=qt: transpose_to(
                            [md_t[:, qt, 0:128], md_t[:, qt, 128:256]], md_B,
                            mixT[:, 6:8, (grp * 4 + qt) * 128:(grp * 4 + qt + 1) * 128], mixTB))
                while deferred:
                    deferred.pop(0)()
                K.barrier()

            with ExitStack() as ph:
              if '3' in PH:
                wo, woB = sb("wo", [128, 8, 1024], BF16, ph)
                fg, fgB = sb("fg", [128, 1024], F32, ph)
                xn = [sb("xn%d" % i, [128, 1024], F32, ph) for i in range(2)]
                yo = [sb("yo%d" % i, [128, 1024], F32, ph) for i in range(2)]
                yb = [sb("yb%d" % i, [128, 1024], BF16, ph) for i in range(4)]
                K.dma(fg[:], io["fg"][:, :], fgB, writes=[fgB])
                for c in range(8):
                    w_t, w_B = rot(xs)
                    K.dma(w_t[:], io["w_out"][l, c * 128:(c + 1) * 128, :], w_B, writes=[w_B])
                    if c % 2:
                        K.op("act", lambda e, c=c: e.activation(out=wo[:, c, :], in_=w_t[:], func=AF.Copy), reads=[w_B], writes=[woB])
                    else:
                        K.op("dve", lambda e, c=c: e.tensor_copy(out=wo[:, c, :], in_=w_t[:]), reads=[w_B], writes=[woB])
                for ti in range(NT_OWN):
                    x_t, x_B = load_x(x_src_own[ti * 128:(ti + 1) * 128, :], xoB)
                    n_t, n_B = rot(xn)
                    for n in range(2):
                        p_t, p_B = rot(pzs, "pzproj")
                        for c in range(8):
                            K.op("pe", lambda e, c=c, n=n: e.matmul(p_t[:], lhsT=mixT[:, c, ti * 128:(ti + 1) * 128],
                                                                    rhs=wo[:, c, n * 512:(n + 1) * 512],
                                                                    start=(c == 0), stop=(c == 7)),
                                 reads=[mixTB, woB], writes=[p_B])
                        K.op("dve", lambda e, n=n: e.tensor_tensor(out=n_t[:, n * 512:(n + 1) * 512], in0=p_t[:],
                                                                   in1=x_t[:, n * 512:(n + 1) * 512], op=ALU.add),
                             reads=[p_B, x_B], writes=[n_B])
                    if not fused:
                        K.dma(io["xnext"][ti * 128:(ti + 1) * 128, :], n_t[:], n_B, reads=[n_B])
                    elif l < L - 1:
                        K.dma(io["x1own"][ti * 128:(ti + 1) * 128, :], n_t[:], n_B, reads=[n_B], writes=[x1ownB])
                        for m in range(4):
                            y_t, y_B = rot(yb)
                            if m % 2:
                                K.op("act", lambda e: e.activation(out=y_t[:], in_=n_t[:], func=AF.Copy, scale=rsel[:, m:m + 1]),
                                     reads=[n_B, rselB], writes=[y_B])
                            else:
                                K.op("dve", lambda e: e.tensor_scalar(
                                    out=y_t[:], in0=n_t[:], scalar1=rsel[:, m:m + 1], scalar2=None, op0=ALU.mult),
                                    reads=[n_B, rselB], writes=[y_B])
                            K.dma(io["xg_src"][m * TOWN + ti * 128:m * TOWN + (ti + 1) * 128, :], y_t[:], y_B,
                                  reads=[y_B], writes=[xgsB[(m * TOWN + ti * 128) // 1024]], qn=("act" if m % 2 == 0 else "pool"))
                        if ti % 8 == 7:
                            for m in range(4):
                                ch = m * 2 + ti // 8
                                K.collective("AllReduce", ALU.add, [[0, 1, 2, 3], [4, 5, 6, 7]],
                                             io["xg_src"][ch * 1024:(ch + 1) * 1024, :], io["xg_dst"][ch * 1024:(ch + 1) * 1024, :],
                                             xgsB[ch], xgdB[ch])
                    if l == L - 1:
                        s_t, s_B = rot(st1)
                        stats_rstd(n_t[:], n_B, 1024, s_t, s_B)
                        y_t, y_B = rot(yo)
                        K.op("dve", lambda e: e.scalar_tensor_tensor(out=y_t[:], in0=n_t[:], scalar=s_t[:, 2:3], in1=fg[:],
                                                                     op0=ALU.mult, op1=ALU.mult),
                             reads=[n_B, s_B, fgB], writes=[y_B])
                        K.dma(io["y"][ti * 128:(ti + 1) * 128, :], y_t[:], y_B, reads=[y_B])
                K.barrier()
        K.barrier(final=True)
        stats = {k: v.cnt for k, v in K.q.items()}
        stats["nsem"] = K.nsem
    build_program.stats = stats
    return nc


def _rope_cs(pos, d):
    inv = np.power(np.float32(10000.0), -(np.arange(0, d, 2, dtype=np.float32) / np.float32(d))).astype(np.float32)
    ang = (pos.astype(np.float32)[:, None] * inv[None, :]).astype(np.float32)
    return np.cos(ang.astype(np.float64)).astype(np.float32), np.sin(ang.astype(np.float64)).astype(np.float32)


def _tab_A(pos):
    row, col = pos // 64, pos % 64
    cr, sr = _rope_cs(row, 32)
    cc, sc = _rope_cs(col, 32)
    return np.concatenate([cr, cr, cc, cc, -sr, sr, -sc, sc], axis=1)


def _tab_D(pos):
    c, s = _rope_cs(pos, 32)
    return np.concatenate([c, c, c, c, -s, s, -s, s], axis=1)


def _tab_B(pos):
    c, s = _rope_cs(pos, 64)
    return np.concatenate([c, c, -s, s], axis=1)


def _c_bias_table(rpb_l, r, i, win):
    a = np.arange(128)
    ks = (TOWN * r - 256 + 128 * np.asarray(win)[:, None] + a[None, :])
    qt = TOWN * r + 128 * i + a
    kvalid = (ks >= 0) & (ks < SEQ)
    ksc = np.clip(ks, 0, SEQ - 1)
    kr, kc = ksc // 64, ksc % 64
    qr, qc = qt // 64, qt % 64
    rs = np.clip(qr - 4, 0, 120)
    cs = np.clip(qc - 8, 0, 48)
    ok = (kvalid[:, :, None] & (kr[:, :, None] >= rs[None, None, :]) & (kr[:, :, None] < rs[None, None, :] + 8)
          & (kc[:, :, None] >= cs[None, None, :]) & (kc[:, :, None] < cs[None, None, :] + 16))
    dr = np.clip(kr[:, :, None] - qr[None, None, :] + 7, 0, 14)
    dc = np.clip(kc[:, :, None] - qc[None, None, :], -15, 15) + 15
    g = rpb_l[:, dr, dc]
    out = np.where(ok[None], g, np.float32(MASKV)).astype(np.float32)
    return np.ascontiguousarray(out.transpose(2, 1, 0, 3))


def _core_inputs(c, xfull, xown, layers, P):
    r = c % 4
    m = {"xfull": xfull, "xown": xown}
    m["w_in"] = np.ascontiguousarray(P["w_in"][layers])
    m["w_out"] = np.ascontiguousarray(P["w_out"][layers])
    m["gcol"] = np.ascontiguousarray(P["norm_g"][layers].reshape(len(layers), 8, 128).transpose(0, 2, 1))
    vec = np.zeros((len(layers), V_N), np.float32)
    for k, l in enumerate(layers):
        lam_init = np.float32(0.8 - 0.6 * np.exp(-0.3 * l))
        vec[k, V_QN:V_QN + 64] = P["qn_a"][l]
        vec[k, V_KN:V_KN + 64] = P["kn_a"][l]
        vec[k, V_SUB:V_SUB + 64] = P["subln_d"][l]
        vec[k, V_SINK:V_SINK + 4] = P["sink_b"][l]
        vec[k, V_LQ1:V_LQ1 + 32] = P["lam_q1"][l]
        vec[k, V_LK1:V_LK1 + 32] = P["lam_k1"][l]
        vec[k, V_LQ2:V_LQ2 + 32] = P["lam_q2"][l]
        vec[k, V_LK2:V_LK2 + 32] = P["lam_k2"][l]
        vec[k, V_LC] = lam_init
        vec[k, V_LC + 1] = np.float32(1.0) - lam_init
    m["vecs"] = np.ascontiguousarray(np.broadcast_to(vec[:, None, :], (len(layers), 128, V_N)))
    m["fg"] = np.ascontiguousarray(np.broadcast_to(P["final_g"][None, :], (128, D_MODEL)))
    pos_all = np.arange(SEQ)
    m["tabKV"] = np.ascontiguousarray(np.concatenate([_tab_A(pos_all), _tab_D(pos_all)], axis=1))
    pos_own = TOWN * r + np.arange(TOWN)
    m["tabQ"] = np.ascontiguousarray(np.concatenate(
        [_tab_A(pos_own) * np.float32(0.125), _tab_D(pos_own) * np.float32(32 ** -0.5)], axis=1).astype(np.float32))
    pos_loc = np.clip(TOWN * r - 256 + np.arange(NT_LOC * 128), 0, SEQ - 1)
    tb = _tab_B(pos_loc)
    m["tabB"] = np.ascontiguousarray(np.concatenate([tb, tb * np.float32(0.125)], axis=1).astype(np.float32))
    cbi = np.zeros((len(layers), 128, 5 * 512), np.float32)
    cbsp = np.full((len(layers), 4, 128, 6 * 512), MASKV, np.float32)
    for k, l in enumerate(layers):
        rpb = P["rpb_c"][l]
        cbi[k] = _c_bias_table(rpb, r, 5, list(range(5, 10))).reshape(128, -1)
        for si, (i, win) in enumerate([(0, list(range(0, 6))), (1, list(range(1, 6))),
                                       (14, list(range(14, 19))), (15, list(range(14, 20)))]):
            t = _c_bias_table(rpb, r, i, win).reshape(128, -1)
            cbsp[k, si, :, :t.shape[1]] = t
    m["cb_int"] = cbi
    m["cb_sp"] = cbsp
    a = np.arange(128)
    mprev = (a[None, :] <= a[:, None]).astype(np.float32)
    mnext = (a[:, None] <= a[None, :]).astype(np.float32)
    z = np.zeros_like(mprev)
    m["bmask"] = np.ascontiguousarray(np.concatenate(
        [mprev if r > 0 else z, mprev, mnext, mnext if r < 3 else z], axis=1))
    sel = np.zeros((6,), np.float32)
    if r >= 1:
        sel[r - 1] = 1.0
    if r <= 2:
        sel[3 + r] = 1.0
    m["halosel"] = np.ascontiguousarray(np.broadcast_to(sel[None, :], (128, 6)))
    return m


_NC_CACHE = {}


def _get_nc(n_layers, fused):
    key = (n_layers, fused)
    if key not in _NC_CACHE:
        _NC_CACHE[key] = build_program(n_layers, fused)
    return _NC_CACHE[key]


def kernel(x, norm_g, w_in, w_out, qn_a, kn_a, sink_b, rpb_c, lam_q1, lam_k1, lam_q2, lam_k2, subln_d, final_g):
    P = dict(norm_g=norm_g, w_in=w_in, w_out=w_out, qn_a=qn_a, kn_a=kn_a, sink_b=sink_b, rpb_c=rpb_c,
             lam_q1=lam_q1, lam_k1=lam_k1, lam_q2=lam_q2, lam_k2=lam_k2, subln_d=subln_d, final_g=final_g)
    P = {k: np.asarray(v, dtype=np.float32) for k, v in P.items()}
    x = np.asarray(x, dtype=np.float32)
    nc = _get_nc(DEPTH, True)
    in_maps = []
    for c in range(NCORES):
        b, r = c // 4, c % 4
        m = _core_inputs(c, np.ascontiguousarray(x[b]), np.ascontiguousarray(x[b, r * TOWN:(r + 1) * TOWN]),
                         list(range(DEPTH)), P)
        rs = np.zeros((4,), np.float32)
        rs[r] = 1.0
        m["rsel"] = np.ascontiguousarray(np.broadcast_to(rs[None, :], (128, 4)))
        in_maps.append(m)
    res = run_bass_kernel_spmd(nc, in_maps, core_ids=list(range(NCORES)))
    y = np.empty_like(x)
    for c in range(NCORES):
        b, r = c // 4, c % 4
        y[b, r * TOWN:(r + 1) * TOWN] = res.results[c]["y"]
    return y
```
